# Optimizing a Trainium2 kernel written in Bass

```python
import math
import jax, jax.numpy as jnp
from jax import lax
import numpy as np

D_MODEL = 1024
BATCH = 16
SEQ = 2048
DEPTH = 1

MLA_HEADS = 4
Q_LORA_RANK = 256
KV_LORA_RANK = 256
QK_NOPE_DIM = 128
QK_ROPE_DIM = 64
QK_HEAD_DIM = QK_NOPE_DIM + QK_ROPE_DIM
V_HEAD_DIM = 128
MLA_WIDTH = MLA_HEADS * V_HEAD_DIM
ROPE_THETA = 10000.0
Q_BLOCK = 128
GDN_HEADS = 4
GDN_HEAD_DIM = 128
GDN_WIDTH = GDN_HEADS * GDN_HEAD_DIM
CONV_WIDTH = 4
CHUNK = 64
MIX_WIDTH = MLA_WIDTH + GDN_WIDTH
D_FF = 4 * D_MODEL
EPS = 1e-6
IN_SPLITS = (Q_LORA_RANK, KV_LORA_RANK, QK_ROPE_DIM,
             GDN_WIDTH, GDN_WIDTH, GDN_WIDTH, GDN_WIDTH, GDN_HEADS, GDN_HEADS)
D_IN = sum(IN_SPLITS)

kernel_name = "hymba_mla_gdn_sqrelu_layer"


def rms_norm(x, w):
    xf = x.astype(jnp.float32)
    y = xf * lax.rsqrt(jnp.mean(xf * xf, axis=-1, keepdims=True) + EPS)
    return (y * w.astype(jnp.float32)).astype(x.dtype)


def l2_norm(x):
    return x * lax.rsqrt(jnp.sum(x * x, axis=-1, keepdims=True) + EPS)


def split_cols(t, sizes):
    offs = np.cumsum(sizes)[:-1].tolist()
    return jnp.split(t, offs, axis=-1)


def rope_angles(positions):
    half = QK_ROPE_DIM // 2
    inv_freq = ROPE_THETA ** (-jnp.arange(half, dtype=jnp.float32) / half)
    ang = positions.astype(jnp.float32)[..., None] * inv_freq
    return jnp.cos(ang)[:, :, None, :], jnp.sin(ang)[:, :, None, :]


def apply_rope(t, cos, sin):
    tf = t.astype(jnp.float32)
    t1, t2 = jnp.split(tf, 2, axis=-1)
    return jnp.concatenate([t1 * cos - t2 * sin, t2 * cos + t1 * sin], axis=-1).astype(t.dtype)


def causal_attention(q, k, v):
    B, S, H, _ = q.shape
    n_blocks = S // Q_BLOCK
    scale = QK_HEAD_DIM ** -0.5
    qb = jnp.moveaxis(q.reshape(B, n_blocks, Q_BLOCK, H, QK_HEAD_DIM), 1, 0)
    key_pos = jnp.arange(S)

    def one_block(args):
        q_blk, blk = args
        s = jnp.einsum('bqhd,bkhd->bhqk', q_blk, k,
                       preferred_element_type=jnp.float32) * scale
        q_pos = blk * Q_BLOCK + jnp.arange(Q_BLOCK)
        s = jnp.where(key_pos[None, :] <= q_pos[:, None], s, -jnp.inf)
        p = jax.nn.softmax(s, axis=-1).astype(v.dtype)
        return jnp.einsum('bhqk,bkhd->bqhd', p, v)

    o = lax.map(one_block, (qb, jnp.arange(n_blocks)))
    return jnp.moveaxis(o, 0, 1).reshape(B, S, H, V_HEAD_DIM)


def mla_group(q_lat, kv_lat, k_pe, cos, sin, q_lat_norm_w, w_uq, kv_lat_norm_w, w_ukv,
              q_norm_w, k_norm_w, mla_out_norm_w):
    B, S, _ = q_lat.shape
    q = (rms_norm(q_lat, q_lat_norm_w) @ w_uq).reshape(B, S, MLA_HEADS, QK_HEAD_DIM)
    kv = (rms_norm(kv_lat, kv_lat_norm_w) @ w_ukv).reshape(B, S, MLA_HEADS, QK_NOPE_DIM + V_HEAD_DIM)
    k_nope, v = jnp.split(kv, [QK_NOPE_DIM], axis=-1)
    q_nope = rms_norm(q[..., :QK_NOPE_DIM], q_norm_w[:QK_NOPE_DIM])
    q_pe = apply_rope(rms_norm(q[..., QK_NOPE_DIM:], q_norm_w[QK_NOPE_DIM:]), cos, sin)
    k_nope = rms_norm(k_nope, k_norm_w[:QK_NOPE_DIM])
    k_pe = apply_rope(rms_norm(k_pe[:, :, None, :], k_norm_w[QK_NOPE_DIM:]), cos, sin)
    q = jnp.concatenate([q_nope, q_pe], axis=-1)
    k = jnp.concatenate([k_nope, jnp.broadcast_to(k_pe, (B, S, MLA_HEADS, QK_ROPE_DIM))], axis=-1)
    o = causal_attention(q, k, v)
    o = rms_norm(o, mla_out_norm_w)
    return o.reshape(B, S, MLA_WIDTH)


def causal_conv(x, w):
    S = x.shape[1]
    xp = jnp.pad(x, ((0, 0), (CONV_WIDTH - 1, 0), (0, 0)))
    return sum(w[i] * xp[:, i:i + S] for i in range(CONV_WIDTH))


def chunk_gated_delta(q, k, v, g, beta):
    B, H, S, D = q.shape
    N = S // CHUNK
    q, k, v = [t.reshape(B, H, N, CHUNK, D) for t in (q, k, v)]
    g = g.reshape(B, H, N, CHUNK)
    beta = beta.reshape(B, H, N, CHUNK)
    G = jnp.cumsum(g, axis=-1)
    idx = jnp.arange(CHUNK)
    causal = idx[:, None] >= idx[None, :]
    strict = idx[:, None] > idx[None, :]
    decay = jnp.exp(jnp.where(causal, G[..., :, None] - G[..., None, :], -jnp.inf))
    kk = jnp.einsum('bhncd,bhnjd->bhncj', k, k)
    L = jnp.where(strict, beta[..., :, None] * kk * decay, 0.0)
    A = L + jnp.eye(CHUNK, dtype=L.dtype)
    rhs = jnp.concatenate([v * beta[..., None], k * (beta * jnp.exp(G))[..., None]], axis=-1)
    sol = lax.linalg.triangular_solve(A, rhs, left_side=True, lower=True, unit_diagonal=True)
    u, w = jnp.split(sol, 2, axis=-1)
    attn_intra = jnp.einsum('bhncd,bhnjd->bhncj', q, k) * decay
    q_dec = q * jnp.exp(G)[..., None]
    k_dec = k * jnp.exp(G[..., -1:] - G)[..., None]
    chunk_decay = jnp.exp(G[..., -1])

    def step(state, xs):
        u_c, w_c, a_c, qd_c, kd_c, cd_c = xs
        v_new = u_c - jnp.einsum('bhcd,bhde->bhce', w_c, state)
        o_c = jnp.einsum('bhcd,bhde->bhce', qd_c, state) + jnp.einsum('bhcj,bhje->bhce', a_c, v_new)
        state = state * cd_c[..., None, None] + jnp.einsum('bhcd,bhce->bhde', kd_c, v_new)
        return state, o_c

    xs = tuple(jnp.moveaxis(t, 2, 0) for t in (u, w, attn_intra, q_dec, k_dec, chunk_decay))
    state0 = jnp.zeros((B, H, D, D), jnp.float32)
    _, o = lax.scan(step, state0, xs)
    return jnp.moveaxis(o, 0, 2).reshape(B, H, S, D)


def gdn_group(q, k, v, z, a, b, conv_w, a_log, dt_bias, gdn_norm_w):
    B, S, _ = q.shape
    qkv = jax.nn.silu(causal_conv(jnp.concatenate([q, k, v], axis=-1), conv_w))
    q, k, v = [t.reshape(B, S, GDN_HEADS, GDN_HEAD_DIM).transpose(0, 2, 1, 3).astype(jnp.float32)
               for t in jnp.split(qkv, 3, axis=-1)]
    q = l2_norm(q) * (GDN_HEAD_DIM ** -0.5)
    k = l2_norm(k)
    beta = jax.nn.sigmoid(b.astype(jnp.float32)).transpose(0, 2, 1)
    g = (-jnp.exp(a_log.astype(jnp.float32))
         * jax.nn.softplus(a.astype(jnp.float32) + dt_bias.astype(jnp.float32))).transpose(0, 2, 1)
    o = chunk_gated_delta(q, k, v, g, beta).transpose(0, 2, 1, 3).astype(z.dtype)
    zh = z.reshape(B, S, GDN_HEADS, GDN_HEAD_DIM)
    o = rms_norm(o, gdn_norm_w) * jax.nn.silu(zh)
    return o.reshape(B, S, GDN_WIDTH)


def setup_inputs(seed: int = 0) -> dict:
    key = jax.random.key(seed)
    ks = jax.random.split(key, 20)
    L = DEPTH

    def normal(k, shape, fan_in):
        return jax.random.normal(k, shape, jnp.float32) * (fan_in ** -0.5)

    def gain(k, shape):
        return 1.0 + 0.02 * jax.random.normal(k, shape, jnp.float32)

    return {
        "x": jax.random.normal(ks[0], (BATCH, SEQ, D_MODEL), jnp.float32),
        "positions": jnp.broadcast_to(jnp.arange(SEQ, dtype=jnp.int32), (BATCH, SEQ)),
        "attn_norm_w": gain(ks[1], (L, D_MODEL)),
        "w_in": normal(ks[2], (L, D_MODEL, D_IN), D_MODEL),
        "q_lat_norm_w": gain(ks[3], (L, Q_LORA_RANK)),
        "w_uq": normal(ks[4], (L, Q_LORA_RANK, MLA_HEADS * QK_HEAD_DIM), Q_LORA_RANK),
        "kv_lat_norm_w": gain(ks[5], (L, KV_LORA_RANK)),
        "w_ukv": normal(ks[6], (L, KV_LORA_RANK, MLA_HEADS * (QK_NOPE_DIM + V_HEAD_DIM)), KV_LORA_RANK),
        "q_norm_w": gain(ks[7], (L, QK_HEAD_DIM)),
        "k_norm_w": gain(ks[8], (L, QK_HEAD_DIM)),
        "mla_out_norm_w": gain(ks[9], (L, MLA_HEADS, V_HEAD_DIM)),
        "conv_w": normal(ks[10], (L, CONV_WIDTH, 3 * GDN_WIDTH), CONV_WIDTH),
        "a_log": jnp.log(jax.random.uniform(ks[11], (L, GDN_HEADS), jnp.float32, 1.0, 16.0)),
        "dt_bias": 0.1 * jax.random.normal(ks[12], (L, GDN_HEADS), jnp.float32),
        "gdn_norm_w": gain(ks[13], (L, GDN_HEAD_DIM)),
        "w_out": normal(ks[14], (L, MIX_WIDTH, D_MODEL), MIX_WIDTH),
        "mlp_norm_w": gain(ks[15], (L, D_MODEL)),
        "w_up": normal(ks[16], (L, D_MODEL, D_FF), D_MODEL),
        "w_down": normal(ks[17], (L, D_FF, D_MODEL), D_FF),
    }


def reference(x, positions, attn_norm_w, w_in, q_lat_norm_w, w_uq, kv_lat_norm_w, w_ukv,
              q_norm_w, k_norm_w, mla_out_norm_w, conv_w, a_log, dt_bias, gdn_norm_w,
              w_out, mlp_norm_w, w_up, w_down):
    cos, sin = rope_angles(positions)
    h = x
    for l in range(DEPTH):
        xn = rms_norm(h, attn_norm_w[l])
        proj = xn @ w_in[l]
        q_lat, kv_lat, k_pe, gq, gk, gv, gz, ga, gb = split_cols(proj, IN_SPLITS)
        mla_o = mla_group(q_lat, kv_lat, k_pe, cos, sin, q_lat_norm_w[l], w_uq[l],
                          kv_lat_norm_w[l], w_ukv[l], q_norm_w[l], k_norm_w[l], mla_out_norm_w[l])
        gdn_o = gdn_group(gq, gk, gv, gz, ga, gb, conv_w[l], a_log[l], dt_bias[l], gdn_norm_w[l])
        h = h + jnp.concatenate([mla_o, gdn_o], axis=-1) @ w_out[l]
        hn = rms_norm(h, mlp_norm_w[l])
        h = h + jnp.square(jax.nn.relu(hn @ w_up[l])) @ w_down[l]
    return h
```

```python
import numpy as np
from contextlib import ExitStack
import concourse.bass as bass
import concourse.mybir as mybir
from concourse.bass_utils import run_bass_kernel_spmd

F32 = mybir.dt.float32
BF16 = mybir.dt.bfloat16
I32 = mybir.dt.int32
AF = mybir.ActivationFunctionType
ALU = mybir.AluOpType
AX = mybir.AxisListType

NCORES = 8
GDN_STOP = 99
NEU_LEVELS = 6
GDN_MAXTILE = 99
M1_BG = False
NEU_PINGPONG = True
SEQ = 2048
DM = 1024
NT = 16
DFF = 4096
EPS = 1e-6
PI = float(np.pi)
C_QL, C_KVL, C_KPE, C_A, C_B, C_Z, C_G = 0, 256, 512, 576, 580, 584, 1096
NW1 = 576
NW2 = 2632 - 576
B_QN, B_KN, B_MO, B_GN, B_AL, B_DT, NBC = 0, 192, 384, 896, 1024, 1028, 1032
P_AN, P_QLN, P_KVLN, P_MN, P_CW, NPP = 0, 8, 10, 12, 20, 68
K_ID, K_U, K_MNEG, K_STR, K_INC, K_ONE, K_IF, NK = 0, 128, 256, 384, 512, 640, 768, 800


class Buf:
    __slots__ = ("name", "last_w", "readers", "psum")

    def __init__(self, name):
        self.name = name
        self.last_w = None
        self.readers = []
        self.psum = False


class Op:
    __slots__ = ("eng", "fn", "deps", "signal", "dma", "tok")


class Chan:
    __slots__ = ("sem", "count", "last")


class Fw:
    ENG = ("pe", "act", "dve", "pool", "sp")

    def __init__(self, nc, stack, nchan=24):
        self.nc = nc
        self.ops = {e: [] for e in self.ENG}
        self.esem = {e: stack.enter_context(nc.semaphore("s_" + e)) for e in self.ENG}
        self.ecnt = {e: 0 for e in self.ENG}
        self.known = {e: {} for e in self.ENG}
        self.chans = []
        for i in range(nchan):
            c = Chan()
            c.sem = stack.enter_context(nc.semaphore(f"dch{i}"))
            c.count = 0
            c.last = None
            self.chans.append(c)
        self.nextchan = 0
        self.autoflush = True
        self.bufs = []
        self.pass_dmas = []
        self.ninst = 0
        self.nwait = 0

    def buf(self, name="b"):
        b = Buf(name)
        self.bufs.append(b)
        return b

    def bufs_n(self, name, n):
        return [self.buf(f"{name}{i}") for i in range(n)]

    def op(self, eng, fn, reads=(), writes=(), dma=False, extra=()):
        if self.autoflush and len(self.ops[eng]) >= 900:
            self.flush()
        o = Op()
        o.eng = eng; o.fn = fn; o.signal = False; o.dma = dma; o.tok = None
        deps = list(extra)
        for b in reads:
            if b.last_w is not None:
                deps.append(b.last_w)
            if b.psum:
                deps.extend(r for r in b.readers if r.eng != eng)
        for b in writes:
            if b.last_w is not None:
                deps.append(b.last_w)
            deps.extend(b.readers)
        dd = []
        seen = set()
        for d in deps:
            if id(d) in seen or d is None:
                continue
            seen.add(id(d))
            if eng == "pe" and d.eng == "pe" and not d.dma and not dma:
                continue
            dd.append(d)
        o.deps = dd
        for b in reads:
            b.readers.append(o)
        for b in writes:
            b.last_w = o
            b.readers = []
        self.ops[eng].append(o)
        return o

    def maybe_flush(self, limit=900):
        if max(len(v) for v in self.ops.values()) >= limit:
            self.flush()

    def flush(self):
        for b in self.bufs:
            if b.last_w is not None and not b.last_w.dma:
                b.last_w.signal = True
            for r in b.readers:
                if not r.dma:
                    r.signal = True
        self._emit()

    def dma(self, out, in_, reads=(), writes=(), eng="sp"):
        ch = self.chans[self.nextchan]
        self.nextchan = (self.nextchan + 1) % len(self.chans)
        ch.count += 16
        o = self.op(eng, lambda e: e.dma_start(out=out, in_=in_), reads=reads, writes=writes, dma=True,
                    extra=[ch.last])
        o.tok = (ch.sem, ch.count)
        ch.last = o
        self.pass_dmas.append(o)
        return o

    def end_pass(self):
        self.autoflush = False
        lasts = {}
        for e in self.ENG:
            for o in reversed(self.ops[e]):
                if not o.dma:
                    lasts[e] = o
                    break
        dma_last = [c.last for c in self.chans if c.last is not None]
        for f in self.ENG:
            extra = [lasts[e] for e in lasts if e != f] + dma_last
            self.op(f, lambda e: e.nop(), extra=extra)
        self._emit()
        self.autoflush = True
        for b in self.bufs:
            b.last_w = None
            b.readers = []
        for c in self.chans:
            c.last = None
        self.pass_dmas = []

    def _emit(self):
        for e in self.ENG:
            for o in self.ops[e]:
                for d in o.deps:
                    if not d.dma:
                        d.signal = True
        for e in self.ENG:
            for o in self.ops[e]:
                if (not o.dma) and o.signal and o.tok is None:
                    self.ecnt[e] += 1
                    o.tok = (self.esem[e], self.ecnt[e])
        fw = self
        with self.nc.Block() as block:
            def run(ename):
                def body(eng):
                    known = fw.known[ename]
                    for o in fw.ops[ename]:
                        need = {}
                        for d in o.deps:
                            assert d.tok is not None, f"dep without token on {ename}"
                            sem, val = d.tok
                            k = id(sem)
                            if known.get(k, 0) >= val:
                                continue
                            if k not in need or need[k][1] < val:
                                need[k] = (sem, val)
                        for k, (sem, val) in need.items():
                            eng.wait_ge(sem, val)
                            known[k] = val
                            fw.nwait += 1
                        ins = o.fn(eng)
                        fw.ninst += 1
                        if o.dma:
                            ins.then_inc(o.tok[0], 16)
                        elif o.signal:
                            ins.then_inc(o.tok[0], 1)
                return body
            block.tensor(run("pe"))
            block.scalar(run("act"))
            block.vector(run("dve"))
            block.gpsimd(run("pool"))
            block.sync(run("sp"))
        for e in self.ENG:
            self.ops[e] = []


class PS:
    def __init__(self, nc, fw, stack):
        self.big = stack.enter_context(nc.psum_tensor("psbig", [128, 8, 512], F32))
        self.t = [self.big[:, i, :] for i in range(8)]
        self.b = [fw.buf(f"psb{i}") for i in range(8)]
        for b in self.b:
            b.psum = True
        self.rb = 0

    def rot(self):
        i = 4 + self.rb
        self.rb = (self.rb + 1) % 4
        return self.t[i], self.b[i]

    def acc(self, i):
        return self.t[i], self.b[i]


def build_program(stages=("m1", "m2", "wout", "mlp"), nseq=2, dbg=False):
    nc = bass.Bass("TRN2", target_bir_lowering=False)

    def din(name, shape, dt=F32):
        return nc.dram_tensor(name, list(shape), dt, kind="ExternalInput").ap()
    x = din("x", [2, SEQ, DM])
    pos = din("pos", [2, 128, NT], I32)
    w_in = din("w_in", [DM, 2632])
    w_uq = din("w_uq", [256, 768])
    w_ukv = din("w_ukv", [256, 1024])
    w_out = din("w_out", [DM, DM])
    w_up = din("w_up", [DM, DFF])
    w_down = din("w_down", [DFF, DM])
    pp_d = din("pp", [128, NPP])
    bc_d = din("bc", [128, NBC])
    cst_d = din("cst", [128, NK])
    out = nc.dram_tensor("out", [2, SEQ, DM], F32, kind="ExternalOutput").ap()
    dbg_mix = None
    if dbg:
        dbg_mix = nc.dram_tensor("dbg_mix", [2, 128, NT * DM], BF16, kind="ExternalOutput").ap()

    with ExitStack() as gs:
        fw = Fw(nc, gs)
        ps = PS(nc, fw, gs)

        cnt = [0]

        def sb(st, name, shape, dt):
            cnt[0] += 1
            return st.enter_context(nc.sbuf_tensor(f"sb{cnt[0]}_{name}", list(shape), dt))

        cst = sb(gs, "cst", [128, NK], F32); b_cst = fw.buf("cst")
        ppt = sb(gs, "ppt", [128, NPP], F32); b_pp = fw.buf("pp")
        bct = sb(gs, "bct", [128, NBC], F32); b_bc = fw.buf("bc")
        idb = sb(gs, "idb", [128, 128], BF16); b_idb = fw.buf("idb")
        incb = sb(gs, "incb", [128, 128], BF16); b_incb = fw.buf("incb")
        fw.dma(cst[:], cst_d, writes=[b_cst])
        fw.dma(ppt[:], pp_d, writes=[b_pp])
        fw.dma(bct[:], bc_d, writes=[b_bc])
        fw.op("dve", lambda e: e.tensor_copy(out=idb[:], in_=cst[:, K_ID:K_ID + 128]), [b_cst], [b_idb])
        fw.op("dve", lambda e: e.tensor_copy(out=incb[:], in_=cst[:, K_INC:K_INC + 128]), [b_cst], [b_incb])
        fw.op("dve", lambda e: e.tensor_scalar(out=bct[:, B_QN:B_QN + 192], in0=bct[:, B_QN:B_QN + 192],
                                               scalar1=float(192 ** -0.5), scalar2=None, op0=ALU.mult), [b_bc], [b_bc])
        idf = cst[:, K_ID:K_ID + 128]
        fw.end_pass()

        def rstd_ops(eng_unused, ssq_ap, out_ap, scale, reads, writes, tmp_ap, b_tmp):
            fw.op("act", lambda e: e.activation(out=tmp_ap, in_=ssq_ap, func=AF.Ln, scale=scale, bias=EPS), reads, [b_tmp])
            fw.op("act", lambda e: e.activation(out=out_ap, in_=tmp_ap, func=AF.Exp, scale=-0.5), [b_tmp], writes)

        for s in range(nseq):
            with ExitStack() as ss:
                MIXR = sb(ss, "mixr", [128, NT * DM], BF16)
                MIX = MIXR[:].rearrange("p (t f) -> p t f", t=NT)
                HNT = MIXR[:].rearrange("p (k t) -> p k t", k=8)
                b_mix = fw.bufs_n("mix", NT)
                cs = sb(ss, "cs", [128, NT, 64], F32); b_cs = fw.buf("cs")

                if "m2" not in stages:
                    fw.op("pool", lambda e: e.memset(MIX[:, :, 512:1024], 0.0), [], b_mix)
                if "m1" in stages:
                    with ExitStack() as p1:
                        build_m1(nc, fw, ps, p1, sb, s, x, pos, w_in, w_uq, w_ukv, cst, ppt, bct, idb, incb,
                                 b_cst, b_pp, b_bc, b_idb, b_incb, MIX, b_mix, cs, b_cs, rstd_ops)
                        fw.end_pass()
                else:
                    fw.op("pool", lambda e: e.memset(MIXR[:], 0.0), [], b_mix)
                    fw.end_pass()
                if "m2" in stages:
                    with ExitStack() as p2:
                        build_m2(nc, fw, ps, p2, sb, s, x, w_in, cst, ppt, bct, idb, b_cst, b_pp, b_bc, b_idb, MIX, b_mix, rstd_ops)
                        fw.end_pass()
                if dbg:
                    fw.dma(dbg_mix[s], MIXR[:], reads=b_mix)
                    fw.end_pass()
                with ExitStack() as hs:
                    H = sb(hs, "H", [128, NT, DM], F32)
                    b_h = fw.bufs_n("h", NT)
                    if "wout" in stages:
                        with ExitStack() as p3:
                            build_wout(nc, fw, ps, p3, sb, s, x, w_out, idb, b_idb, MIX, b_mix, H, b_h)
                            fw.end_pass()
                    if "mlp" in stages:
                        with ExitStack() as p4:
                            build_mlp(nc, fw, ps, p4, sb, s, w_up, w_down, ppt, b_pp, idb, b_idb, HNT, H, b_h, out, rstd_ops)
                            fw.end_pass()
                    else:
                        for t in range(NT):
                            fw.dma(out[s, t * 128:(t + 1) * 128, :], H[:, t, :], reads=[b_h[t]])
                        fw.end_pass()
        print("instructions", fw.ninst, "waits", fw.nwait)
    return nc


def load_cast_gen(fw, st, sb, name, w_ap, nk, ncols, dst, b_dst, gain=None, b_gain=None, stg=None, engs=("act", "dve")):
    sw = min(int(stg[0][0].shape[1]), ncols)
    n = 0
    for k in range(nk):
        for c0 in range(0, ncols, sw):
            c1 = min(ncols, c0 + sw)
            stt, b_st = stg[n % len(stg)]
            eng = engs[n % len(engs)]
            n += 1
            fw.dma(stt[:, 0:c1 - c0], w_ap[k * 128:(k + 1) * 128, c0:c1], writes=[b_st])
            rd = [b_st] + ([b_gain] if gain is not None else [])
            if eng == "act":
                if gain is not None:
                    fw.op("act", lambda e, k=k, stt=stt, c0=c0, c1=c1: e.activation(out=dst[:, k, c0:c1], in_=stt[:, 0:c1 - c0], func=AF.Copy, scale=gain[:, k:k + 1]),
                          rd, [b_dst[k]])
                else:
                    fw.op("act", lambda e, k=k, stt=stt, c0=c0, c1=c1: e.activation(out=dst[:, k, c0:c1], in_=stt[:, 0:c1 - c0], func=AF.Copy), rd, [b_dst[k]])
            elif eng == "dve":
                if gain is not None:
                    fw.op("dve", lambda e, k=k, stt=stt, c0=c0, c1=c1: e.tensor_scalar(out=dst[:, k, c0:c1], in0=stt[:, 0:c1 - c0], scalar1=gain[:, k:k + 1], scalar2=None, op0=ALU.mult),
                          rd, [b_dst[k]])
                else:
                    fw.op("dve", lambda e, k=k, stt=stt, c0=c0, c1=c1: e.tensor_copy(out=dst[:, k, c0:c1], in_=stt[:, 0:c1 - c0]), rd, [b_dst[k]])
            else:
                if gain is not None:
                    fw.op("pool", lambda e, k=k, stt=stt, c0=c0, c1=c1: e.tensor_scalar(out=dst[:, k, c0:c1], in0=stt[:, 0:c1 - c0], scalar1=gain[:, k:k + 1], scalar2=1.0,
                                                                                     op0=ALU.mult, op1=ALU.mult), rd, [b_dst[k]])
                else:
                    fw.op("pool", lambda e, k=k, stt=stt, c0=c0, c1=c1: e.tensor_copy(out=dst[:, k, c0:c1], in_=stt[:, 0:c1 - c0]), rd, [b_dst[k]])
            yield


def load_cast(*a, **kw):
    for _ in load_cast_gen(*a, **kw):
        pass


def drain(g):
    if g is not None:
        for _ in g:
            pass


def run_rolling(gens, width=2, bg=None):
    gens = list(gens)
    active = []
    nxt = 0
    while nxt < len(gens) or active:
        while len(active) < width and nxt < len(gens):
            active.append(gens[nxt]); nxt += 1
        for g in list(active):
            try:
                next(g)
            except StopIteration:
                active.remove(g)
                break
        if bg is not None:
            try:
                next(bg)
            except StopIteration:
                bg = None
    return bg


def build_m1(nc, fw, ps, st, sb, s, x, pos, w_in, w_uq, w_ukv, cst, ppt, bct, idb, incb,
             b_cst, b_pp, b_bc, b_idb, b_incb, MIX, b_mix, cs, b_cs, rstd_ops):
    w1 = sb(st, "w1", [128, 8, NW1], BF16); b_w1 = fw.bufs_n("w1", 8)
    wuq = sb(st, "wuq", [128, 2, 768], BF16); b_wuq = fw.bufs_n("wuq", 2)
    wukv = sb(st, "wukv", [128, 2, 1024], BF16); b_wukv = fw.bufs_n("wukv", 2)
    stg = [(sb(st, f"stg{i}", [128, 1024], F32), fw.buf(f"stg{i}")) for i in range(2)]
    def m1_loader():
        yield from load_cast_gen(fw, st, sb, "w1", w_in[:, 0:NW1], 8, NW1, w1, b_w1, ppt[:, P_AN:P_AN + 8], b_pp, stg)
        yield from load_cast_gen(fw, st, sb, "wuq", w_uq, 2, 768, wuq, b_wuq, ppt[:, P_QLN:P_QLN + 2], b_pp, stg)
        yield from load_cast_gen(fw, st, sb, "wukv", w_ukv, 2, 1024, wukv, b_wukv, ppt[:, P_KVLN:P_KVLN + 2], b_pp, stg)
    wl = [m1_loader()]

    def need_weights():
        drain(wl[0])
        wl[0] = None

    posi = sb(st, "posi", [128, NT], I32); b_posi = fw.buf("posi")
    ang = sb(st, "ang", [128, NT, 32], F32); b_ang = fw.buf("ang")
    kq = sb(st, "kq", [128, NT, 32], F32); b_kq = fw.buf("kq")
    kqi = sb(st, "kqi", [128, NT, 32], I32); b_kqi = fw.buf("kqi")
    posf = sb(st, "posf", [128, NT], F32); b_posf = fw.buf("posf")
    fw.dma(posi[:], pos[s], writes=[b_posi])
    fw.op("dve", lambda e: e.tensor_copy(out=posf[:], in_=posi[:]), [b_posi], [b_posf])
    invf = cst[:, K_IF:K_IF + 32]
    fw.op("dve", lambda e: e.tensor_tensor(out=ang[:], in0=posf[:].unsqueeze(2).to_broadcast([128, NT, 32]),
                                           in1=invf.unsqueeze(1).to_broadcast([128, NT, 32]), op=ALU.mult),
          [b_posf, b_cst], [b_ang])
    fw.op("dve", lambda e: e.tensor_scalar(out=kq[:], in0=ang[:], scalar1=float(1.0 / (2 * PI)), scalar2=None, op0=ALU.mult), [b_ang], [b_kq])
    fw.op("dve", lambda e: e.tensor_copy(out=kqi[:], in_=kq[:]), [b_kq], [b_kqi])
    fw.op("dve", lambda e: e.tensor_copy(out=kq[:], in_=kqi[:]), [b_kqi], [b_kq])
    C1 = 6.28125
    C2 = float(2 * np.pi - 6.28125)
    fw.op("dve", lambda e: e.scalar_tensor_tensor(out=ang[:], in0=kq[:], scalar=-C1, in1=ang[:], op0=ALU.mult, op1=ALU.add), [b_kq, b_ang], [b_ang])
    fw.op("dve", lambda e: e.scalar_tensor_tensor(out=ang[:], in0=kq[:], scalar=-C2, in1=ang[:], op0=ALU.mult, op1=ALU.add), [b_kq, b_ang], [b_ang])
    fw.op("dve", lambda e: e.tensor_scalar(out=kq[:], in0=ang[:], scalar1=PI, scalar2=None, op0=ALU.is_gt), [b_ang], [b_kq])
    fw.op("dve", lambda e: e.scalar_tensor_tensor(out=ang[:], in0=kq[:], scalar=-2 * PI, in1=ang[:], op0=ALU.mult, op1=ALU.add), [b_kq, b_ang], [b_ang])
    fw.op("dve", lambda e: e.tensor_scalar(out=kq[:], in0=ang[:], scalar1=-PI, scalar2=None, op0=ALU.is_lt), [b_ang], [b_kq])
    fw.op("dve", lambda e: e.scalar_tensor_tensor(out=ang[:], in0=kq[:], scalar=2 * PI, in1=ang[:], op0=ALU.mult, op1=ALU.add), [b_kq, b_ang], [b_ang])
    fw.op("dve", lambda e: e.tensor_scalar(out=ang[:], in0=ang[:], scalar1=PI, scalar2=-PI, op0=ALU.min, op1=ALU.max), [b_ang], [b_ang])
    fw.op("act", lambda e: e.activation(out=cs[:, :, 32:64], in_=ang[:], func=AF.Sin), [b_ang], [b_cs])
    fw.op("act", lambda e: e.activation(out=kq[:], in_=ang[:], func=AF.Abs), [b_ang], [b_kq])
    fw.op("dve", lambda e: e.tensor_scalar(out=kq[:], in0=kq[:], scalar1=-1.0, scalar2=PI / 2, op0=ALU.mult, op1=ALU.add), [b_kq], [b_kq])
    fw.op("act", lambda e: e.activation(out=cs[:, :, 0:32], in_=kq[:], func=AF.Sin), [b_kq], [b_cs])

    KT = sb(st, "KT", [128, 4, SEQ], BF16); b_kt = fw.bufs_n("kt", NT)
    KR = sb(st, "KR", [128, SEQ], BF16); b_kr = fw.bufs_n("kr", NT)
    fw.op("pool", lambda e: e.memset(KR[:], 0.0), [], b_kr)
    V = sb(st, "V", [128, NT, 4, 132], BF16); b_v = fw.bufs_n("v", NT)
    fw.op("pool", lambda e: e.memset(V[:], 1.0), [], b_v)
    xt = [sb(st, f"xt{i}", [128, DM], F32) for i in range(2)]; b_xt = fw.bufs_n("xt", 2)
    junk = sb(st, "junk", [128, DM], F32); b_junk = fw.buf("junk")
    xn = [sb(st, f"xn{i}", [128, DM], BF16) for i in range(2)]; b_xn = fw.bufs_n("xn", 2)
    xnT = [sb(st, f"xnT{i}", [128, 8, 128], BF16) for i in range(2)]; b_xnT = fw.bufs_n("xnT", 2)
    st8 = [sb(st, f"st8{i}", [128, 16], F32) for i in range(2)]; b_st8 = fw.bufs_n("st8", 2)
    tm8 = [sb(st, f"tm8{i}", [128, 16], F32) for i in range(2)]; b_tm8 = fw.bufs_n("tm8", 2)
    latn = [sb(st, f"latn{i}", [128, 512], BF16) for i in range(2)]; b_latn = fw.bufs_n("latn", 2)
    latT = [sb(st, f"latT{i}", [128, 4, 128], BF16) for i in range(2)]; b_latT = fw.bufs_n("latT", 2)
    kpe = [sb(st, f"kpe{i}", [128, 64], F32) for i in range(2)]; b_kpe = fw.bufs_n("kpe", 2)
    rtmp = [sb(st, f"rtmp{i}", [128, 4, 4, 32], F32) for i in range(2)]; b_rtmp = fw.bufs_n("rtmp", 2)
    krb = [sb(st, f"krb{i}", [128, 64], BF16) for i in range(2)]; b_krb = fw.bufs_n("krb", 2)
    qf = [sb(st, f"qf{i}", [128, 768], F32) for i in range(2)]; b_qf = fw.bufs_n("qf", 2)
    sq = [sb(st, f"sq{i}", [128, 1024], F32) for i in range(2)]; b_sq = fw.bufs_n("sq", 2)
    qb = [sb(st, f"qb{i}", [128, 4, 192], BF16) for i in range(2)]; b_qb = fw.bufs_n("qb", 2)
    kvf = [sb(st, f"kvf{i}", [128, 1024], F32) for i in range(2)]; b_kvf = fw.bufs_n("kvf", 2)
    kb = [sb(st, f"kb{i}", [128, 4, 128], BF16) for i in range(2)]; b_kb = fw.bufs_n("kb", 2)
    QT2 = [sb(st, f"QT{i}", [128, 4, 512], BF16) for i in range(2)]; b_qt2 = [fw.bufs_n(f"qt{i}", 4) for i in range(2)]
    QR2 = [sb(st, f"QR{i}", [128, 4, 512], BF16) for i in range(2)]; b_qr2 = [fw.bufs_n(f"qr{i}", 4) for i in range(2)]
    for i_ in range(2):
        fw.op("pool", lambda e, i_=i_: e.memset(QR2[i_][:], 0.0), [], b_qr2[i_])
    PT = [sb(st, f"PT{i}", [128, 512], BF16) for i in range(3)]; b_pt = fw.bufs_n("pt", 3)
    of = [sb(st, f"of{i}", [128, 128], F32) for i in range(2)]; b_of = fw.bufs_n("of", 2)
    ost = [sb(st, f"ost{i}", [128, 4], F32) for i in range(2)]; b_ost = fw.bufs_n("ost", 2)
    ptc = 0
    ofc = 0

    def proj_tile(t, tl, sti):
        nonlocal ptc, ofc
        t = sti * 4 + tl
        r = t % 2
        fw.dma(xt[r][:], x[s, t * 128:(t + 1) * 128, :], writes=[b_xt[r]])
        fw.op("act", lambda e, r=r: e.activation(out=junk[:], in_=xt[r][:], func=AF.Square, accum_out=st8[r][:, 0:1]),
              [b_xt[r]], [b_st8[r]])
        rstd_ops(None, st8[r][:, 0:1], st8[r][:, 1:2], 1.0 / DM, [b_st8[r]], [b_st8[r]], tm8[r][:, 0:1], b_tm8[r])
        fw.op("dve", lambda e, r=r: e.tensor_scalar(out=xn[r][:], in0=xt[r][:], scalar1=st8[r][:, 1:2], scalar2=None, op0=ALU.mult),
              [b_xt[r], b_st8[r]], [b_xn[r]])
        pt_, bpt_ = ps.rot()
        ptb = pt_[:].bitcast(BF16).rearrange("p (k c) -> p k c", k=8)
        for k in range(8):
            fw.op("pe", lambda e, k=k, r=r, ptb=ptb: e.transpose(out=ptb[:, k, :], in_=xn[r][:, k * 128:(k + 1) * 128], identity=idb[:]),
                  [b_xn[r], b_idb], [bpt_])
        fw.op("dve", lambda e, r=r, ptb=ptb: e.tensor_copy(out=xnT[r][:], in_=ptb), [bpt_], [b_xnT[r]])
        yield
        need_weights()
        pl, bpl = ps.rot()
        for k in range(8):
            fw.op("pe", lambda e, k=k, r=r, pl=pl: e.matmul(pl[:, 0:512], lhsT=xnT[r][:, k, :], rhs=w1[:, k, 0:512], start=(k == 0), stop=(k == 7)),
                  [b_xnT[r], b_w1[k]], [bpl])
        pk, bpk = ps.rot()
        for k in range(8):
            fw.op("pe", lambda e, k=k, r=r, pk=pk: e.matmul(pk[:, 0:64], lhsT=xnT[r][:, k, :], rhs=w1[:, k, 512:576], start=(k == 0), stop=(k == 7)),
                  [b_xnT[r], b_w1[k]], [bpk])
        fw.op("act", lambda e, r=r, pl=pl: e.activation(out=junk[:, 0:256], in_=pl[:, 0:256], func=AF.Square, accum_out=st8[r][:, 2:3]),
              [bpl], [b_st8[r]])
        fw.op("act", lambda e, r=r, pl=pl: e.activation(out=junk[:, 256:512], in_=pl[:, 256:512], func=AF.Square, accum_out=st8[r][:, 3:4]),
              [bpl], [b_st8[r]])
        fw.op("act", lambda e, r=r, pk=pk: e.activation(out=junk[:, 512:576], in_=pk[:, 0:64], func=AF.Square, accum_out=st8[r][:, 4:5]),
              [bpk], [b_st8[r]])
        rstd_ops(None, st8[r][:, 2:4], st8[r][:, 5:7], 1.0 / 256, [b_st8[r]], [b_st8[r]], tm8[r][:, 2:4], b_tm8[r])
        rstd_ops(None, st8[r][:, 4:5], st8[r][:, 7:8], 1.0 / 64, [b_st8[r]], [b_st8[r]], tm8[r][:, 4:5], b_tm8[r])
        fw.op("dve", lambda e, r=r, pl=pl: e.tensor_scalar(out=latn[r][:, 0:256], in0=pl[:, 0:256], scalar1=st8[r][:, 5:6], scalar2=None, op0=ALU.mult),
              [bpl, b_st8[r]], [b_latn[r]])
        fw.op("dve", lambda e, r=r, pl=pl: e.tensor_scalar(out=latn[r][:, 256:512], in0=pl[:, 256:512], scalar1=st8[r][:, 6:7], scalar2=None, op0=ALU.mult),
              [bpl, b_st8[r]], [b_latn[r]])
        fw.op("dve", lambda e, r=r, pk=pk: e.scalar_tensor_tensor(out=kpe[r][:], in0=pk[:, 0:64], scalar=st8[r][:, 7:8], in1=bct[:, B_KN + 128:B_KN + 192],
                                                                   op0=ALU.mult, op1=ALU.mult),
              [bpk, b_st8[r], b_bc], [b_kpe[r]])
        cosb = cs[:, t, 0:32]
        sinb = cs[:, t, 32:64]
        R = rtmp[r]
        fw.op("pool", lambda e, r=r, R=R, cosb=cosb: e.tensor_tensor(out=R[:, 0, 0, :], in0=kpe[r][:, 0:32], in1=cosb, op=ALU.mult), [b_kpe[r], b_cs], [b_rtmp[r]])
        fw.op("pool", lambda e, r=r, R=R, sinb=sinb: e.tensor_tensor(out=R[:, 1, 0, :], in0=kpe[r][:, 32:64], in1=sinb, op=ALU.mult), [b_kpe[r], b_cs], [b_rtmp[r]])
        fw.op("pool", lambda e, r=r, R=R, cosb=cosb: e.tensor_tensor(out=R[:, 2, 0, :], in0=kpe[r][:, 32:64], in1=cosb, op=ALU.mult), [b_kpe[r], b_cs], [b_rtmp[r]])
        fw.op("pool", lambda e, r=r, R=R, sinb=sinb: e.tensor_tensor(out=R[:, 3, 0, :], in0=kpe[r][:, 0:32], in1=sinb, op=ALU.mult), [b_kpe[r], b_cs], [b_rtmp[r]])
        fw.op("pool", lambda e, r=r, R=R: e.tensor_tensor(out=krb[r][:, 0:32], in0=R[:, 0, 0, :], in1=R[:, 1, 0, :], op=ALU.subtract), [b_rtmp[r]], [b_krb[r]])
        fw.op("pool", lambda e, r=r, R=R: e.tensor_tensor(out=krb[r][:, 32:64], in0=R[:, 2, 0, :], in1=R[:, 3, 0, :], op=ALU.add), [b_rtmp[r]], [b_krb[r]])
        yield
        pt2, bpt2 = ps.rot()
        pt2b = pt2[:].bitcast(BF16).rearrange("p (k c) -> p k c", k=8)
        for c in range(4):
            fw.op("pe", lambda e, c=c, r=r, pt2b=pt2b: e.transpose(out=pt2b[:, c, :], in_=latn[r][:, c * 128:(c + 1) * 128], identity=idb[:]),
                  [b_latn[r], b_idb], [bpt2])
        fw.op("pe", lambda e, r=r, pt2b=pt2b: e.transpose(out=pt2b[0:64, 4, :], in_=krb[r][:, 0:64], identity=idb[:]),
              [b_krb[r], b_idb], [bpt2])
        fw.op("act", lambda e, r=r, pt2b=pt2b: e.copy(out=latT[r][:], in_=pt2b[:, 0:4, :]), [bpt2], [b_latT[r]])
        fw.op("act", lambda e, t=t, pt2b=pt2b: e.copy(out=KR[0:64, t * 128:(t + 1) * 128], in_=pt2b[0:64, 4, :]), [bpt2], [b_kr[t]])
        yield
        pq0, bpq0 = ps.rot()
        pq1, bpq1 = ps.rot()
        for c in range(2):
            fw.op("pe", lambda e, c=c, r=r, pq0=pq0: e.matmul(pq0[:, 0:512], lhsT=latT[r][:, c, :], rhs=wuq[:, c, 0:512], start=(c == 0), stop=(c == 1)),
                  [b_latT[r], b_wuq[c]], [bpq0])
        for c in range(2):
            fw.op("pe", lambda e, c=c, r=r, pq1=pq1: e.matmul(pq1[:, 0:256], lhsT=latT[r][:, c, :], rhs=wuq[:, c, 512:768], start=(c == 0), stop=(c == 1)),
                  [b_latT[r], b_wuq[c]], [bpq1])
        fw.op("act", lambda e, r=r, pq0=pq0: e.copy(out=qf[r][:, 0:512], in_=pq0[:, 0:512]), [bpq0], [b_qf[r]])
        fw.op("act", lambda e, r=r, pq1=pq1: e.copy(out=qf[r][:, 512:768], in_=pq1[:, 0:256]), [bpq1], [b_qf[r]])
        pv0, bpv0 = ps.rot()
        pv1, bpv1 = ps.rot()
        for hh, (pv, bpv) in enumerate(((pv0, bpv0), (pv1, bpv1))):
            for c in range(2):
                fw.op("pe", lambda e, c=c, r=r, pv=pv, hh=hh: e.matmul(pv[:, 0:512], lhsT=latT[r][:, 2 + c, :], rhs=wukv[:, c, hh * 512:(hh + 1) * 512],
                                                                      start=(c == 0), stop=(c == 1)),
                      [b_latT[r], b_wukv[c]], [bpv])
            fw.op("dve", lambda e, r=r, pv=pv, hh=hh: e.tensor_copy(out=kvf[r][:, hh * 512:(hh + 1) * 512], in_=pv[:, 0:512]), [bpv], [b_kvf[r]])
        yield
        q3 = qf[r][:].rearrange("p (h d) -> p h d", h=4)
        s3 = sq[r][:, 0:768].rearrange("p (h d) -> p h d", h=4)
        fw.op("pool", lambda e, r=r: e.tensor_tensor(out=sq[r][:, 0:768], in0=qf[r][:], in1=qf[r][:], op=ALU.mult), [b_qf[r]], [b_sq[r]])
        fw.op("dve", lambda e, r=r, s3=s3: e.tensor_reduce(out=st8[r][:, 8:12], in_=s3[:, :, 0:128], axis=AX.X, op=ALU.add), [b_sq[r]], [b_st8[r]])
        fw.op("dve", lambda e, r=r, s3=s3: e.tensor_reduce(out=st8[r][:, 12:16], in_=s3[:, :, 128:192], axis=AX.X, op=ALU.add), [b_sq[r]], [b_st8[r]])
        rstd_ops(None, st8[r][:, 8:12], tm8[r][:, 8:12], 1.0 / 128, [b_st8[r]], [b_tm8[r]], st8[r][:, 8:12], b_st8[r])
        rstd_ops(None, st8[r][:, 12:16], tm8[r][:, 12:16], 1.0 / 64, [b_st8[r]], [b_tm8[r]], st8[r][:, 12:16], b_st8[r])
        yield
        s4 = sq[r][:, 0:768].rearrange("p (h d) -> p h d", h=4)
        fw.op("dve", lambda e, r=r, q3=q3, s4=s4: e.tensor_tensor(out=s4[:, :, 0:128], in0=q3[:, :, 0:128],
                                                                  in1=tm8[r][:, 8:12].unsqueeze(2).to_broadcast([128, 4, 128]), op=ALU.mult),
              [b_qf[r], b_tm8[r]], [b_sq[r]])
        fw.op("pool", lambda e, r=r, s4=s4: e.tensor_tensor(out=qb[r][:, :, 0:128], in0=s4[:, :, 0:128],
                                                            in1=bct[:, B_QN:B_QN + 128].unsqueeze(1).to_broadcast([128, 4, 128]), op=ALU.mult),
              [b_sq[r], b_bc], [b_qb[r]])
        fw.op("dve", lambda e, r=r, q3=q3, s4=s4: e.tensor_tensor(out=s4[:, :, 128:192], in0=q3[:, :, 128:192],
                                                                  in1=tm8[r][:, 12:16].unsqueeze(2).to_broadcast([128, 4, 64]), op=ALU.mult),
              [b_qf[r], b_tm8[r]], [b_sq[r]])
        fw.op("pool", lambda e, r=r, s4=s4: e.tensor_tensor(out=s4[:, :, 128:192], in0=s4[:, :, 128:192],
                                                            in1=bct[:, B_QN + 128:B_QN + 192].unsqueeze(1).to_broadcast([128, 4, 64]), op=ALU.mult),
              [b_sq[r], b_bc], [b_sq[r]])
        cos4 = cosb.unsqueeze(1).to_broadcast([128, 4, 32])
        sin4 = sinb.unsqueeze(1).to_broadcast([128, 4, 32])
        fw.op("pool", lambda e, R=R, s4=s4, cos4=cos4: e.tensor_tensor(out=R[:, 0, :, :], in0=s4[:, :, 128:160], in1=cos4, op=ALU.mult), [b_sq[r], b_cs], [b_rtmp[r]])
        fw.op("pool", lambda e, R=R, s4=s4, sin4=sin4: e.tensor_tensor(out=R[:, 1, :, :], in0=s4[:, :, 160:192], in1=sin4, op=ALU.mult), [b_sq[r], b_cs], [b_rtmp[r]])
        fw.op("pool", lambda e, R=R, s4=s4, cos4=cos4: e.tensor_tensor(out=R[:, 2, :, :], in0=s4[:, :, 160:192], in1=cos4, op=ALU.mult), [b_sq[r], b_cs], [b_rtmp[r]])
        fw.op("pool", lambda e, R=R, s4=s4, sin4=sin4: e.tensor_tensor(out=R[:, 3, :, :], in0=s4[:, :, 128:160], in1=sin4, op=ALU.mult), [b_sq[r], b_cs], [b_rtmp[r]])
        fw.op("pool", lambda e, r=r, R=R: e.tensor_tensor(out=qb[r][:, :, 128:160], in0=R[:, 0, :, :], in1=R[:, 1, :, :], op=ALU.subtract), [b_rtmp[r]], [b_qb[r]])
        fw.op("pool", lambda e, r=r, R=R: e.tensor_tensor(out=qb[r][:, :, 160:192], in0=R[:, 2, :, :], in1=R[:, 3, :, :], op=ALU.add), [b_rtmp[r]], [b_qb[r]])
        yield
        pt3, bpt3 = ps.rot()
        pt3b = pt3[:].bitcast(BF16).rearrange("p (k c) -> p k c", k=8)
        for h in range(4):
            fw.op("pe", lambda e, h=h, r=r, pt3b=pt3b: e.transpose(out=pt3b[:, h, :], in_=qb[r][:, h, 0:128], identity=idb[:]), [b_qb[r], b_idb], [bpt3])
        for h in range(4):
            fw.op("pe", lambda e, h=h, r=r, pt3b=pt3b: e.transpose(out=pt3b[0:64, 4 + h, :], in_=qb[r][:, h, 128:192], identity=idb[:]), [b_qb[r], b_idb], [bpt3])
        fw.op("act", lambda e, tl=tl, pt3b=pt3b: e.copy(out=QT2[sti % 2][:, :, tl * 128:(tl + 1) * 128], in_=pt3b[:, 0:4, :]), [bpt3], [b_qt2[sti % 2][tl]])
        fw.op("act", lambda e, tl=tl, pt3b=pt3b: e.copy(out=QR2[sti % 2][0:64, :, tl * 128:(tl + 1) * 128], in_=pt3b[0:64, 4:8, :]), [bpt3], [b_qr2[sti % 2][tl]])
        yield
        k3 = kvf[r][:].rearrange("p (h d) -> p h d", h=4)
        sk = sq[r][:, 0:512].rearrange("p (h d) -> p h d", h=4)
        fw.op("pool", lambda e, k3=k3, sk=sk: e.tensor_tensor(out=sk, in0=k3[:, :, 0:128], in1=k3[:, :, 0:128], op=ALU.mult), [b_kvf[r]], [b_sq[r]])
        fw.op("dve", lambda e, r=r, sk=sk: e.tensor_reduce(out=st8[r][:, 8:12], in_=sk, axis=AX.X, op=ALU.add), [b_sq[r]], [b_st8[r]])
        rstd_ops(None, st8[r][:, 8:12], tm8[r][:, 8:12], 1.0 / 128, [b_st8[r]], [b_tm8[r]], st8[r][:, 8:12], b_st8[r])
        fw.op("dve", lambda e, r=r, k3=k3, sk=sk: e.tensor_tensor(out=sk, in0=k3[:, :, 0:128],
                                                                  in1=tm8[r][:, 8:12].unsqueeze(2).to_broadcast([128, 4, 128]), op=ALU.mult),
              [b_kvf[r], b_tm8[r]], [b_sq[r]])
        fw.op("pool", lambda e, r=r, sk=sk: e.tensor_tensor(out=kb[r][:], in0=sk,
                                                            in1=bct[:, B_KN:B_KN + 128].unsqueeze(1).to_broadcast([128, 4, 128]), op=ALU.mult),
              [b_sq[r], b_bc], [b_kb[r]])
        fw.op("dve", lambda e, t=t, k3=k3: e.tensor_copy(out=V[:, t, :, 0:128], in_=k3[:, :, 128:256]), [b_kvf[r]], [b_v[t]])
        pt4, bpt4 = ps.rot()
        pt4b = pt4[:].bitcast(BF16).rearrange("p (k c) -> p k c", k=8)
        for h in range(4):
            fw.op("pe", lambda e, h=h, r=r, pt4b=pt4b: e.transpose(out=pt4b[:, h, :], in_=kb[r][:, h, :], identity=idb[:]), [b_kb[r], b_idb], [bpt4])
        fw.op("act", lambda e, t=t, pt4b=pt4b: e.copy(out=KT[:, :, t * 128:(t + 1) * 128], in_=pt4b[:, 0:4, :]), [bpt4], [b_kt[t]])

        yield

    def attn(sti):
        nonlocal ptc, ofc
        qp = sti % 2
        nkt = sti * 4 + 4
        for h in range(4):
            oacc = [ps.acc(i) for i in range(4)]
            def issue_qk(j, h=h):
                c0 = max(j, sti * 4) - sti * 4
                ncol = (4 - c0) * 128
                sp_, bsp_ = ps.rot()
                rds = [b_kt[j], b_kr[j]] + [b_qt2[qp][i] for i in range(c0, 4)] + [b_qr2[qp][i] for i in range(c0, 4)]
                fw.op("pe", lambda e, h=h, j=j, c0=c0, ncol=ncol, sp_=sp_, qp=qp: e.matmul(sp_[:, 0:ncol], lhsT=KT[:, h, j * 128:(j + 1) * 128],
                                                                                   rhs=QT2[qp][:, h, c0 * 128:512], start=True, stop=False), rds, [bsp_])
                fw.op("pe", lambda e, h=h, j=j, c0=c0, ncol=ncol, sp_=sp_, qp=qp: e.matmul(sp_[:, 0:ncol], lhsT=KR[:, j * 128:(j + 1) * 128],
                                                                                   rhs=QR2[qp][:, h, c0 * 128:512], start=False, stop=True), rds, [bsp_])
                return sp_, bsp_, c0, ncol

            nxt = issue_qk(0)
            for j in range(nkt):
                sp_, bsp_, c0, ncol = nxt
                if j + 1 < nkt:
                    nxt = issue_qk(j + 1)
                pi = ptc % 3
                ptc += 1
                fw.op("act", lambda e, pi=pi, ncol=ncol, sp_=sp_: e.activation(out=PT[pi][:, 0:ncol], in_=sp_[:, 0:ncol], func=AF.Exp), [bsp_], [b_pt[pi]])
                if j >= sti * 4:
                    fw.op("pool", lambda e, pi=pi: e.tensor_tensor(out=PT[pi][:, 0:128], in0=PT[pi][:, 0:128], in1=incb[:], op=ALU.mult),
                          [b_pt[pi], b_incb], [b_pt[pi]])
                for qi in range(c0, 4):
                    po, bpo = oacc[qi]
                    fw.op("pe", lambda e, pi=pi, qi=qi, c0=c0, j=j, h=h, po=po, sti=sti: e.matmul(po[:, 0:129], lhsT=PT[pi][:, (qi - c0) * 128:(qi - c0 + 1) * 128],
                                                                                        rhs=V[:, j, h, 0:129], start=(j == 0), stop=(j == sti * 4 + qi)),
                          [b_pt[pi], b_v[j]], [bpo])
                if M1_BG:
                    yield
            for qi in range(4):
                t = sti * 4 + qi
                po, bpo = oacc[qi]
                oi = ofc % 2
                ofc += 1
                fw.op("dve", lambda e, oi=oi, po=po: e.reciprocal(out=ost[oi][:, 0:1], in_=po[:, 128:129]), [bpo], [b_ost[oi]])
                fw.op("dve", lambda e, oi=oi, po=po: e.tensor_scalar(out=of[oi][:], in0=po[:, 0:128], scalar1=ost[oi][:, 0:1], scalar2=None, op0=ALU.mult),
                      [bpo, b_ost[oi]], [b_of[oi]])
                fw.op("act", lambda e, oi=oi: e.activation(out=junk[:, 0:128], in_=of[oi][:], func=AF.Square, accum_out=ost[oi][:, 1:2]),
                      [b_of[oi]], [b_ost[oi]])
                rstd_ops(None, ost[oi][:, 1:2], ost[oi][:, 2:3], 1.0 / 128, [b_ost[oi]], [b_ost[oi]], ost[oi][:, 3:4], b_ost[oi])
                fw.op("dve", lambda e, oi=oi, t=t, h=h: e.scalar_tensor_tensor(out=MIX[:, t, h * 128:(h + 1) * 128], in0=of[oi][:], scalar=ost[oi][:, 2:3],
                                                                              in1=bct[:, B_MO + h * 128:B_MO + (h + 1) * 128], op0=ALU.mult, op1=ALU.mult),
                      [b_of[oi], b_ost[oi], b_bc], [b_mix[t]])


                yield

    def run_strands(strands, bg=None, bg_steps=1):
        strands = list(strands)
        while strands:
            for g in list(strands):
                try:
                    next(g)
                except StopIteration:
                    strands.remove(g)
            if bg is not None:
                for _ in range(bg_steps):
                    try:
                        next(bg)
                    except StopIteration:
                        bg = None
                        break
        return bg

    for sti in range(4):
        run_rolling([proj_tile(sti * 4 + tl, tl, sti) for tl in range(4)], 2, bg=wl[0])
        need_weights()
        run_strands([attn(sti)])


def build_m2(nc, fw, ps, st, sb, s, x, w_in, cst, ppt, bct, idb, b_cst, b_pp, b_bc, b_idb, MIX, b_mix, rstd_ops):
    idf = cst[:, K_ID:K_ID + 128]
    Uf = cst[:, K_U:K_U + 128]
    onesf = cst[:, K_ONE:K_ONE + 128]
    mneg = cst[:, K_MNEG:K_MNEG + 128]
    strf = cst[:, K_STR:K_STR + 128]
    w2 = sb(st, "w2", [128, 8, NW2], BF16); b_w2 = fw.bufs_n("w2", 8)
    stg = [(sb(st, f"stg2{i}", [128, NW2 // 2], F32), fw.buf(f"stg2{i}")) for i in range(2)]
    wl = [load_cast_gen(fw, st, sb, "w2", w_in[:, NW1:NW1 + NW2], 8, NW2, w2, b_w2, ppt[:, P_AN:P_AN + 8], b_pp, stg)]

    def need_weights():
        drain(wl[0])
        wl[0] = None
    O_AB, O_Z, O_GQ = 0, 8, 520
    xt = [sb(st, f"xt{i}", [128, DM], F32) for i in range(2)]; b_xt = fw.bufs_n("xt", 2)
    junk = sb(st, "junk", [128, DM], F32)
    xn = [sb(st, f"xn{i}", [128, DM], BF16) for i in range(2)]; b_xn = fw.bufs_n("xn", 2)
    xnT = sb(st, "xnT", [128, 8, 512], BF16); b_xnT = fw.bufs_n("xnT", 4)
    st8 = [sb(st, f"st8{i}", [128, 4], F32) for i in range(2)]; b_st8 = fw.bufs_n("st8", 2)
    ab = sb(st, "ab", [128, 4, 8], F32); b_ab = fw.bufs_n("ab", 4)
    gb = sb(st, "gb", [128, 4, 16], F32); b_gb = fw.bufs_n("gb", 4)
    zsg = sb(st, "zsg", [128, 4, 512], F32); b_zsg = fw.bufs_n("zsg", 4)
    Xc = [sb(st, f"Xc{i}", [128, 515], F32) for i in range(2)]; b_Xc = fw.bufs_n("Xc", 2)
    yacc = [sb(st, f"yacc{i}", [128, 512], F32) for i in range(2)]; b_yacc = fw.bufs_n("yacc", 2)
    halo = sb(st, "halo", [128, 12, 4], F32); b_halo = fw.bufs_n("halo", 12)
    Y = sb(st, "Y", [128, 12, 512], BF16); b_Y = fw.bufs_n("Y", 12)
    Sf = sb(st, "Sf", [128, 4, 128], F32); b_Sf = fw.bufs_n("Sf", 4)
    Sb = sb(st, "Sb", [128, 4, 128], BF16); b_Sb = fw.bufs_n("Sb", 4)
    fw.op("pool", lambda e: e.memset(halo[:], 0.0), [], b_halo)
    fw.op("pool", lambda e: e.memset(Sf[:], 0.0), [], b_Sf)
    fw.op("pool", lambda e: e.memset(Sb[:], 0.0), [], b_Sb)
    def two(f):
        return [f(0), f(1)]
    Gs2 = two(lambda q: sb(st, f"Gs{q}", [128, 24], F32)); b_Gs2 = fw.bufs_n("Gs", 2)
    sc2 = two(lambda q: sb(st, f"sc{q}", [128, 40], F32)); b_sc2 = fw.bufs_n("sc", 2)
    so2 = two(lambda q: sb(st, f"so{q}", [128, 12], F32)); b_so2 = fw.bufs_n("so", 2)
    g3f2 = two(lambda q: sb(st, f"g3f{q}", [128, 3, 4], F32)); b_g3f2 = fw.bufs_n("g3f", 2)
    g3b2 = two(lambda q: sb(st, f"g3b{q}", [128, 3, 4], BF16)); b_g3b2 = fw.bufs_n("g3b", 2)
    gr2 = two(lambda q: sb(st, f"gr{q}", [128, 8], F32)); b_gr2 = fw.bufs_n("gr", 2)
    NS = 2 if NEU_PINGPONG else 7

    def per_head(name, shape, dt):
        arrs = two(lambda q: sb(st, f"{name}{q}", [128, 4] + list(shape[1:]), dt))
        views = [[arrs[q][:, h] for h in range(4)] for q in range(2)]
        return views, two(lambda q: fw.bufs_n(f"{name}{q}", 4)), arrs

    def per_head_ns(name, shape, dt):
        arrs = two(lambda q: [sb(st, f"{name}{q}_{i}", [128, 4] + list(shape[1:]), dt) for i in range(NS)])
        views = [[[arrs[q][i][:, h] for i in range(NS)] for h in range(4)] for q in range(2)]
        return views, two(lambda q: [fw.bufs_n(f"{name}{q}{h}", NS) for h in range(4)]), arrs
    k6s, b_k6s, k6A = per_head("k6", [128, 6, 128], BF16)
    kqTs, b_kqTs, kqTA = per_head("kqT", [128, 3, 128], BF16)
    Ug3s, b_Ugs, Ug3A = per_head("Ug3", [128, 3, 128], BF16)
    tDs, b_tDs, tDA = per_head("tD", [128, 128], F32)
    dTs, b_dTs, dTA = per_head("dT", [128, 128], F32)
    dSs, b_dSs, dSA = per_head("dS", [128, 128], F32)
    Mms, b_Mms, MmA = per_head_ns("Mm", [128, 128], BF16)
    MmTs, b_MmTs, MmTA = per_head_ns("MmT", [128, 128], BF16)
    Pms, b_Pms, PmA = per_head_ns("Pm", [128, 128], BF16)
    aTs, b_aTs, aTA = per_head("attT", [128, 128], BF16)
    Ubs, b_Ubs, UbA = per_head("Ub", [128, 128], F32)
    WTs, b_WTs, WTA = per_head("WT", [128, 128], BF16)
    vns, b_vns, vnA = per_head("vn", [128, 128], BF16)
    ubf = sb(st, "ubf", [128, 128], BF16); b_ubf = fw.buf("ubf")
    onb = sb(st, "onb", [128, 128], BF16); b_onb = fw.buf("onb")
    fw.op("dve", lambda e: e.tensor_copy(out=ubf[:], in_=Uf), [b_cst], [b_ubf])
    fw.op("dve", lambda e: e.tensor_copy(out=onb[:], in_=onesf), [b_cst], [b_onb])
    stage = [0]

    def bank(h):
        i = (stage[0] % 2) * 4 + h
        return ps.t[i], ps.b[i]

    def next_stage():
        stage[0] += 1

    s8_done = set()

    def gdn_tile(sti, tl):
        sel = tl % 2
        Gs, sc, so, g3f, g3b, gr = Gs2[sel], sc2[sel], so2[sel], g3f2[sel], g3b2[sel], gr2[sel]
        b_Gs, b_sc, b_so, b_g3f, b_g3b, b_gr = b_Gs2[sel], b_sc2[sel], b_so2[sel], b_g3f2[sel], b_g3b2[sel], b_gr2[sel]
        k6, kqT, Ug3, tD, dT, dS, Mm, MmT, Pm, aT, Ub, WT, vn = [x_[sel] for x_ in (k6s, kqTs, Ug3s, tDs, dTs, dSs, Mms, MmTs, Pms, aTs, Ubs, WTs, vns)]
        b_k6, b_kqT, b_Ug, b_tD, b_dT, b_dS, b_Mm, b_MmT, b_Pm, b_aT, b_Ub, b_WT, b_vn = [x_[sel] for x_ in (b_k6s, b_kqTs, b_Ugs, b_tDs, b_dTs, b_dSs, b_Mms, b_MmTs, b_Pms, b_aTs, b_Ubs, b_WTs, b_vns)]

        def bank(h):
            return ps.t[sel * 4 + h], ps.b[sel * 4 + h]
        bpall = [ps.b[sel * 4 + h] for h in range(4)]
        G4 = ps.big[:, sel * 4:(sel + 1) * 4, :]
        G4b = G4.bitcast(BF16)
        k6a, kqTa, Ug3a, tDa, dTa, dSa, aTa, Uba, WTa = k6A[sel], kqTA[sel], Ug3A[sel], tDA[sel], dTA[sel], dSA[sel], aTA[sel], UbA[sel], WTA[sel]
        Mma, MmTa, Pma = MmA[sel], MmTA[sel], PmA[sel]

        def bc(ap4):
            return ap4.unsqueeze(2).to_broadcast([128, 4, 128])

        def allb(bl, i=None):
            return [bl[h] if i is None else bl[h][i] for h in range(4)]
        t = sti * 4 + tl
        cols = slice(tl * 128, (tl + 1) * 128)
        G_ = gb[:, tl, :]
        if t > GDN_MAXTILE:
            return
        fw.op("dve", lambda e, G_=G_: e.tensor_copy(out=g3b[:, 0, :], in_=G_[:, 0:4]), [b_gb[tl]], [b_g3b])
        fw.op("dve", lambda e: e.tensor_copy(out=g3f[:, 0, :], in_=g3b[:, 0, :]), [b_g3b], [b_g3f])
        fw.op("dve", lambda e, G_=G_: e.tensor_tensor(out=gr[:, 0:4], in0=G_[:, 0:4], in1=g3f[:, 0, :], op=ALU.subtract), [b_gb[tl], b_g3f], [b_gr])
        fw.op("dve", lambda e: e.tensor_copy(out=g3b[:, 1, :], in_=gr[:, 0:4]), [b_gr], [b_g3b])
        fw.op("dve", lambda e: e.tensor_copy(out=g3f[:, 1, :], in_=g3b[:, 1, :]), [b_g3b], [b_g3f])
        fw.op("dve", lambda e: e.tensor_tensor(out=gr[:, 4:8], in0=gr[:, 0:4], in1=g3f[:, 1, :], op=ALU.subtract), [b_gr, b_g3f], [b_gr])
        fw.op("dve", lambda e: e.tensor_copy(out=g3b[:, 2, :], in_=gr[:, 4:8]), [b_gr], [b_g3b])
        fw.op("dve", lambda e: e.tensor_copy(out=g3f[:, 2, :], in_=g3b[:, 2, :]), [b_g3b], [b_g3f])
        pg, bpg = ps.rot()
        for i in range(3):
            fw.op("pe", lambda e, pg=pg, i=i: e.matmul(pg[:, 0:4], lhsT=ubf[:], rhs=g3b[:, i, :], start=(i == 0), stop=(i == 2)), [b_ubf, b_g3b], [bpg])
        for i in range(3):
            fw.op("pe", lambda e, pg=pg, i=i: e.matmul(pg[:, 4:8], lhsT=onb[:], rhs=g3b[:, i, :], start=(i == 0), stop=(i == 2), skip_group_check=True), [b_onb, b_g3b], [bpg])
        fw.op("dve", lambda e, pg=pg: e.tensor_copy(out=Gs[:, 0:8], in_=pg[:, 0:8]), [bpg], [b_Gs])
        fw.op("act", lambda e: e.activation(out=Gs[:, 8:12], in_=Gs[:, 0:4], func=AF.Exp), [b_Gs], [b_Gs])
        fw.op("dve", lambda e: e.tensor_tensor(out=Gs[:, 20:24], in0=Gs[:, 4:8], in1=Gs[:, 0:4], op=ALU.subtract), [b_Gs], [b_Gs])
        fw.op("act", lambda e: e.activation(out=Gs[:, 12:16], in_=Gs[:, 20:24], func=AF.Exp), [b_Gs], [b_Gs])
        fw.op("act", lambda e: e.activation(out=Gs[:, 16:20], in_=Gs[:, 4:8], func=AF.Exp), [b_Gs], [b_Gs])
        yield
        B1 = [bank(h) for h in range(4)]
        for h in range(4):
            p1, bp1 = B1[h]
            p1b = p1[:].bitcast(BF16).rearrange("p (k c) -> p k c", k=8)
            for i, c in enumerate((h, 4 + h, 8 + h)):
                fw.op("pe", lambda e, p1b=p1b, i=i, c=c, cols=cols: e.transpose(out=p1b[:, i, :], in_=Y[:, c, cols], identity=idb[:]), [b_Y[c], b_idb], [bp1])
            fw.op("act", lambda e, p1b=p1b, h=h: e.activation(out=junk[:, 0:128], in_=p1b[:, 0, :], func=AF.Square, accum_out=sc[:, h:h + 1]), [bp1], [b_sc])
            fw.op("act", lambda e, p1b=p1b, h=h: e.activation(out=junk[:, 128:256], in_=p1b[:, 1, :], func=AF.Square, accum_out=sc[:, 4 + h:5 + h]), [bp1], [b_sc])
        fw.op("act", lambda e: e.activation(out=sc[:, 8:16], in_=sc[:, 0:8], func=AF.Ln, bias=EPS), [b_sc], [b_sc])
        fw.op("act", lambda e: e.activation(out=sc[:, 16:24], in_=sc[:, 8:16], func=AF.Exp, scale=-0.5), [b_sc], [b_sc])
        fw.op("dve", lambda e: e.tensor_tensor(out=sc[:, 24:28], in0=sc[:, 20:24], in1=Gs[:, 8:12], op=ALU.mult), [b_sc, b_Gs], [b_sc])
        fw.op("dve", lambda e: e.tensor_tensor(out=sc[:, 28:32], in0=sc[:, 20:24], in1=Gs[:, 12:16], op=ALU.mult), [b_sc, b_Gs], [b_sc])
        fw.op("dve", lambda e: e.tensor_scalar(out=sc[:, 32:36], in0=sc[:, 16:20], scalar1=float(128 ** -0.5), scalar2=None, op0=ALU.mult), [b_sc], [b_sc])
        fw.op("dve", lambda e: e.tensor_tensor(out=sc[:, 36:40], in0=sc[:, 32:36], in1=Gs[:, 8:12], op=ALU.mult), [b_sc, b_Gs], [b_sc])
        kps4, qps4, vps4 = G4b[:, :, 128:256], G4b[:, :, 0:128], G4b[:, :, 256:384]
        fw.op("act", lambda e: e.copy(out=k6a[:, :, 5, :], in_=vps4), bpall, b_k6)
        for i_, (src4, c0_) in enumerate(((kps4, 20), (kps4, 24), (kps4, 28), (qps4, 32), (qps4, 36))):
            fw.op("dve", lambda e, i_=i_, src4=src4, c0_=c0_: e.tensor_tensor(out=k6a[:, :, i_, :], in0=src4, in1=bc(sc[:, c0_:c0_ + 4]), op=ALU.mult), bpall + [b_sc], b_k6)
        for i in range(3):
            fw.op("pool", lambda e, i=i: e.tensor_tensor(out=Ug3a[:, :, i, :], in0=Uf.unsqueeze(1).to_broadcast([128, 4, 128]), in1=bc(g3f[:, i, :]), op=ALU.mult),
                  [b_cst, b_g3f], b_Ug)
        if GDN_STOP <= 1:
            return
        yield
        for h in range(4):
            p2, bp2 = bank(h)
            p2b = p2[:].bitcast(BF16).rearrange("p (k c) -> p k c", k=8)
            for i, src in enumerate((0, 3, 4)):
                fw.op("pe", lambda e, p2b=p2b, i=i, src=src, h=h: e.transpose(out=p2b[:, i, :], in_=k6[h][:, src, :], identity=idb[:]), [b_k6[h], b_idb], [bp2])
        fw.op("act", lambda e: e.copy(out=kqTa[:].rearrange("p h k c -> p h (k c)"), in_=G4b[:, :, 0:384]), bpall, b_kqT)
        if GDN_STOP <= 2:
            return
        yield
        for h in range(4):
            p3, bp3 = bank(h)
            fw.op("pe", lambda e, p3=p3, h=h: e.matmul(p3[:, 0:256], lhsT=kqT[h][:, 0, :], rhs=kqT[h][:, 0:2, :], start=True, stop=True), [b_kqT[h]], [bp3])
            for i in range(3):
                fw.op("pe", lambda e, p3=p3, h=h, i=i: e.matmul(p3[:, 256:384], lhsT=onb[:], rhs=Ug3[h][:, i, :], start=(i == 0), stop=(i == 2), skip_group_check=True), [b_onb, b_Ug[h]], [bp3])
            fw.op("dve", lambda e, p3=p3, h=h: e.scalar_tensor_tensor(out=tD[h][:], in0=p3[:, 256:384], scalar=Gs[:, h:h + 1], in1=mneg, op0=ALU.subtract, op1=ALU.add),
                  [bp3, b_Gs, b_cst], [b_tD[h]])
        fw.op("act", lambda e: e.activation(out=dTa[:], in_=tDa[:], func=AF.Exp), b_tD, b_dT)
        fw.op("pool", lambda e: e.tensor_tensor(out=dSa[:], in0=dTa[:], in1=strf.unsqueeze(1).to_broadcast([128, 4, 128]), op=ALU.mult), b_dT + [b_cst], b_dS)
        for h in range(4):
            p3, bp3 = bank(h)
            fw.op("dve", lambda e, p3=p3, h=h, G_=G_: e.scalar_tensor_tensor(out=Mm[h][0][:], in0=p3[:, 0:128], scalar=G_[:, 8 + h:9 + h], in1=dS[h][:], op0=ALU.mult, op1=ALU.mult),
                  [bp3, b_gb[tl], b_dS[h]], [b_Mm[h][0]])
        fw.op("dve", lambda e: e.tensor_tensor(out=aTa[:], in0=G4[:, :, 128:256], in1=dTa[:], op=ALU.mult), bpall + b_dT, b_aT)
        fw.op("pool", lambda e: e.tensor_tensor(out=Pma[0][:], in0=Mma[0][:], in1=idb[:].unsqueeze(1).to_broadcast([128, 4, 128]), op=ALU.add), allb(b_Mm, 0) + [b_idb], allb(b_Pm, 0))
        if GDN_STOP <= 3:
            return
        yield
        for h in range(4):
            p4, bp4 = bank(h)
            p4b = p4[:].bitcast(BF16)
            fw.op("pe", lambda e, p4b=p4b, h=h: e.transpose(out=p4b[:, 0:128], in_=Mm[h][0][:], identity=idb[:]), [b_Mm[h][0], b_idb], [bp4])
        fw.op("act", lambda e: e.copy(out=MmTa[0][:], in_=G4b[:, :, 0:128]), bpall, allb(b_MmT, 0))
        if GDN_STOP <= 4:
            return
        for lvl in range(NEU_LEVELS):
            a, b = (lvl % 2, (lvl + 1) % 2) if NEU_PINGPONG else (lvl, lvl + 1)
            yield
            for h in range(4):
                p5, bp5 = bank(h)
                if lvl < NEU_LEVELS - 1:
                    fw.op("pe", lambda e, p5=p5, h=h, a=a: e.matmul(p5[:, 0:128], lhsT=MmT[h][a][:], rhs=Mm[h][a][:], start=True, stop=True), [b_MmT[h][a], b_Mm[h][a]], [bp5])
                fw.op("pe", lambda e, p5=p5, h=h, a=a: e.matmul(p5[:, 128:256], lhsT=Mm[h][a][:], rhs=MmT[h][a][:], start=True, stop=True), [b_MmT[h][a], b_Mm[h][a]], [bp5])
            ev = "act"
            if lvl < NEU_LEVELS - 1:
                if ev == "act":
                    fw.op("act", lambda e, b=b: e.copy(out=Mma[b][:], in_=G4[:, :, 0:128]), bpall, allb(b_Mm, b))
                else:
                    fw.op("dve", lambda e, b=b: e.tensor_copy(out=Mma[b][:], in_=G4[:, :, 0:128]), bpall, allb(b_Mm, b))
            if ev == "act":
                fw.op("act", lambda e, b=b: e.copy(out=MmTa[b][:], in_=G4[:, :, 128:256]), bpall, allb(b_MmT, b))
            else:
                fw.op("dve", lambda e, b=b: e.tensor_copy(out=MmTa[b][:], in_=G4[:, :, 128:256]), bpall, allb(b_MmT, b))
            yield
            for h in range(4):
                p6, bp6 = bank(h)
                fw.op("pe", lambda e, p6=p6, h=h, a=a, b=b: e.matmul(p6[:, 0:128], lhsT=MmT[h][b][:], rhs=Pm[h][a][:], start=True, stop=True), [b_MmT[h][b], b_Pm[h][a]], [bp6])
            fw.op("dve", lambda e, a=a, b=b: e.tensor_tensor(out=Pma[b][:], in0=G4[:, :, 0:128], in1=Pma[a][:], op=ALU.add), bpall + allb(b_Pm, a), allb(b_Pm, b))
        PF = (NEU_LEVELS % 2) if NEU_PINGPONG else NEU_LEVELS
        if GDN_STOP <= 5:
            return
        yield
        for h in range(4):
            p7, bp7 = bank(h)
            fw.op("pe", lambda e, p7=p7, h=h: e.matmul(p7[:, 0:128], lhsT=Pm[h][PF][:], rhs=k6[h][:, 5, :], start=True, stop=True), [b_Pm[h][PF], b_k6[h]], [bp7])
            fw.op("pe", lambda e, p7=p7, h=h: e.matmul(p7[:, 128:256], lhsT=k6[h][:, 1, :], rhs=Pm[h][PF][:], start=True, stop=True), [b_Pm[h][PF], b_k6[h]], [bp7])
        fw.op("dve", lambda e, G_=G_: e.tensor_tensor(out=Uba[:], in0=G4[:, :, 0:128], in1=bc(G_[:, 4:8]), op=ALU.mult), bpall + [b_gb[tl]], b_Ub)
        fw.op("dve", lambda e: e.tensor_copy(out=WTa[:], in_=G4[:, :, 128:256]), bpall, b_WT)
        yield
        while t > 0 and (t - 1) not in s8_done:
            yield
        for h in range(4):
            p8, bp8 = bank(h)
            fw.op("pe", lambda e, p8=p8, h=h: e.matmul(p8[:, 0:128], lhsT=WT[h][:], rhs=Sb[:, h, :], start=True, stop=True), [b_WT[h], b_Sb[h]], [bp8])
            fw.op("dve", lambda e, p8=p8, h=h, G_=G_: e.scalar_tensor_tensor(out=vn[h][:], in0=p8[:, 0:128], scalar=G_[:, 8 + h:9 + h], in1=Ub[h][:], op0=ALU.mult, op1=ALU.add),
                  [bp8, b_gb[tl], b_Ub[h]], [b_vn[h]])
        yield
        B9 = [bank(h) for h in range(4)]
        for h in range(4):
            p9, bp9 = B9[h]
            fw.op("pe", lambda e, p9=p9, h=h: e.matmul(p9[:, 0:128], lhsT=kqT[h][:, 2, :], rhs=Sb[:, h, :], start=True, stop=False), [b_kqT[h], b_Sb[h]], [bp9])
            fw.op("pe", lambda e, p9=p9, h=h: e.matmul(p9[:, 0:128], lhsT=aT[h][:], rhs=vn[h][:], start=False, stop=True), [b_aT[h], b_vn[h]], [bp9])
            fw.op("pe", lambda e, p9=p9, h=h: e.matmul(p9[:, 128:256], lhsT=k6[h][:, 2, :], rhs=vn[h][:], start=True, stop=True), [b_k6[h], b_vn[h]], [bp9])
            fw.op("act", lambda e, p9=p9, h=h: e.activation(out=junk[:, 0:128], in_=p9[:, 0:128], func=AF.Square, accum_out=so[:, h:h + 1]), [bp9], [b_so])
            fw.op("dve", lambda e, p9=p9, h=h: e.scalar_tensor_tensor(out=Sf[:, h, :], in0=Sf[:, h, :], scalar=Gs[:, 16 + h:17 + h], in1=p9[:, 128:256], op0=ALU.mult, op1=ALU.add),
                  [bp9, b_Gs, b_Sf[h]], [b_Sf[h]])
        fw.op("pool", lambda e: e.tensor_copy(out=Sb[:], in_=Sf[:]), b_Sf, b_Sb)
        fw.op("act", lambda e: e.activation(out=so[:, 4:8], in_=so[:, 0:4], func=AF.Ln, scale=1.0 / 128, bias=EPS), [b_so], [b_so])
        fw.op("act", lambda e: e.activation(out=so[:, 8:12], in_=so[:, 4:8], func=AF.Exp, scale=-0.5), [b_so], [b_so])
        for h in range(4):
            p9, bp9 = B9[h]
            fw.op("dve", lambda e, p9=p9, h=h, t=t, tl=tl: e.scalar_tensor_tensor(out=MIX[:, t, 512 + h * 128:512 + (h + 1) * 128], in0=p9[:, 0:128], scalar=so[:, 8 + h:9 + h],
                                                                               in1=zsg[:, tl, h * 128:(h + 1) * 128], op0=ALU.mult, op1=ALU.mult),
                  [bp9, b_so, b_zsg[tl]], [b_mix[t]])
        s8_done.add(t)


    for sti in range(4):
        if sti * 4 > GDN_MAXTILE:
            break
        def m2_proj_tile(tl, sti=sti):
            t = sti * 4 + tl
            r = t % 2
            fw.dma(xt[r][:], x[s, t * 128:(t + 1) * 128, :], writes=[b_xt[r]])
            fw.op("act", lambda e, r=r: e.activation(out=xn[r][:], in_=xt[r][:], func=AF.Square, accum_out=st8[r][:, 0:1]), [b_xt[r]], [b_st8[r], b_xn[r]])
            yield
            rstd_ops(None, st8[r][:, 0:1], st8[r][:, 1:2], 1.0 / DM, [b_st8[r]], [b_st8[r]], st8[r][:, 2:3], b_st8[r])
            fw.op("dve", lambda e, r=r: e.tensor_scalar(out=xn[r][:], in0=xt[r][:], scalar1=st8[r][:, 1:2], scalar2=None, op0=ALU.mult),
                  [b_xt[r], b_st8[r]], [b_xn[r]])
            yield
            pt_, bpt_ = ps.rot()
            ptb = pt_[:].bitcast(BF16).rearrange("p (k c) -> p k c", k=8)
            for k in range(8):
                fw.op("pe", lambda e, k=k, r=r, ptb=ptb: e.transpose(out=ptb[:, k, :], in_=xn[r][:, k * 128:(k + 1) * 128], identity=idb[:]),
                      [b_xn[r], b_idb], [bpt_])
            fw.op("dve", lambda e, tl=tl, ptb=ptb: e.tensor_copy(out=xnT[:, :, tl * 128:(tl + 1) * 128], in_=ptb), [bpt_], [b_xnT[tl]])
            yield
            need_weights()
            pa, bpa = ps.rot()
            for k in range(8):
                fw.op("pe", lambda e, k=k, tl=tl, pa=pa: e.matmul(pa[:, 0:8], lhsT=xnT[:, k, tl * 128:(tl + 1) * 128], rhs=w2[:, k, O_AB:O_AB + 8],
                                                                 start=(k == 0), stop=(k == 7)), [b_xnT[tl], b_w2[k]], [bpa])
            fw.op("act", lambda e, tl=tl, pa=pa: e.copy(out=ab[:, tl, :], in_=pa[:, 0:8]), [bpa], [b_ab[tl]])
            yield
            pz, bpz = ps.rot()
            for k in range(8):
                fw.op("pe", lambda e, k=k, tl=tl, pz=pz: e.matmul(pz[:, 0:512], lhsT=xnT[:, k, tl * 128:(tl + 1) * 128], rhs=w2[:, k, O_Z:O_Z + 512],
                                                                 start=(k == 0), stop=(k == 7)), [b_xnT[tl], b_w2[k]], [bpz])
            fw.op("act", lambda e, tl=tl, pz=pz: e.activation(out=zsg[:, tl, :], in_=pz[:, 0:512], func=AF.Silu), [bpz], [b_zsg[tl]])
            z3 = zsg[:, tl, :].rearrange("p (h d) -> p h d", h=4)
            fw.op("pool", lambda e, z3=z3: e.tensor_tensor(out=z3, in0=z3, in1=bct[:, B_GN:B_GN + 128].unsqueeze(1).to_broadcast([128, 4, 128]), op=ALU.mult),
                  [b_zsg[tl], b_bc], [b_zsg[tl]])
            yield
            G_ = gb[:, tl, :]
            fw.op("dve", lambda e, tl=tl, G_=G_: e.tensor_tensor(out=G_[:, 12:16], in0=ab[:, tl, 0:4], in1=bct[:, B_DT:B_DT + 4], op=ALU.add), [b_ab[tl], b_bc], [b_gb[tl]])
            fw.op("act", lambda e, G_=G_: e.activation(out=G_[:, 12:16], in_=G_[:, 12:16], func=AF.Exp), [b_gb[tl]], [b_gb[tl]])
            fw.op("act", lambda e, G_=G_: e.activation(out=G_[:, 12:16], in_=G_[:, 12:16], func=AF.Ln, bias=1.0), [b_gb[tl]], [b_gb[tl]])
            fw.op("act", lambda e, G_=G_: e.activation(out=G_[:, 8:12], in_=bct[:, B_AL:B_AL + 4], func=AF.Exp), [b_bc], [b_gb[tl]])
            fw.op("dve", lambda e, G_=G_: e.scalar_tensor_tensor(out=G_[:, 0:4], in0=G_[:, 12:16], scalar=-1.0, in1=G_[:, 8:12], op0=ALU.mult, op1=ALU.mult),
                  [b_gb[tl]], [b_gb[tl]])
            fw.op("act", lambda e, tl=tl, G_=G_: e.activation(out=G_[:, 12:16], in_=ab[:, tl, 4:8], func=AF.Exp, scale=-1.0), [b_ab[tl]], [b_gb[tl]])
            fw.op("dve", lambda e, G_=G_: e.tensor_scalar(out=G_[:, 12:16], in0=G_[:, 12:16], scalar1=1.0, scalar2=None, op0=ALU.add), [b_gb[tl]], [b_gb[tl]])
            fw.op("dve", lambda e, G_=G_: e.reciprocal(out=G_[:, 4:8], in_=G_[:, 12:16]), [b_gb[tl]], [b_gb[tl]])
            fw.op("dve", lambda e, G_=G_: e.tensor_scalar(out=G_[:, 8:12], in0=G_[:, 4:8], scalar1=-1.0, scalar2=None, op0=ALU.mult), [b_gb[tl]], [b_gb[tl]])

            yield

        run_rolling([m2_proj_tile(tl) for tl in range(4)], 2, bg=wl[0])
        need_weights()
        for c in range(12):
            pf, bpf = ps.rot()
            for k in range(8):
                fw.op("pe", lambda e, k=k, c=c, pf=pf: e.matmul(pf[:, 0:512], lhsT=w2[:, k, O_GQ + c * 128:O_GQ + (c + 1) * 128], rhs=xnT[:, k, :],
                                                               start=(k == 0), stop=(k == 7)), [b_w2[k]] + b_xnT, [bpf])
            xi = c % 2
            fw.op("pool", lambda e, xi=xi, c=c: e.tensor_copy(out=Xc[xi][:, 0:3], in_=halo[:, c, 0:3]), [b_halo[c]], [b_Xc[xi]])
            fw.op("act", lambda e, xi=xi, pf=pf: e.copy(out=Xc[xi][:, 3:515], in_=pf[:, 0:512]), [bpf], [b_Xc[xi]])
            fw.op("pool", lambda e, xi=xi, c=c: e.tensor_copy(out=halo[:, c, 0:3], in_=Xc[xi][:, 512:515]), [b_Xc[xi]], [b_halo[c]])
            cw = lambda i, c=c: ppt[:, P_CW + c * 4 + i:P_CW + c * 4 + i + 1]
            fw.op("act", lambda e, xi=xi, cw=cw: e.activation(out=yacc[xi][:], in_=Xc[xi][:, 0:512], func=AF.Copy, scale=cw(0)), [b_Xc[xi], b_pp], [b_yacc[xi]])
            for i in (1, 2, 3):
                fw.op("dve", lambda e, xi=xi, cw=cw, i=i: e.scalar_tensor_tensor(out=yacc[xi][:], in0=Xc[xi][:, i:i + 512], scalar=cw(i), in1=yacc[xi][:],
                                                                                op0=ALU.mult, op1=ALU.add), [b_Xc[xi], b_pp, b_yacc[xi]], [b_yacc[xi]])
            if c > 0:
                fw.op("act", lambda e, xj=(c - 1) % 2, cj=c - 1: e.activation(out=Y[:, cj, :], in_=yacc[xj][:], func=AF.Silu), [b_yacc[(c - 1) % 2]], [b_Y[c - 1]])
        fw.op("act", lambda e: e.activation(out=Y[:, 11, :], in_=yacc[11 % 2][:], func=AF.Silu), [b_yacc[11 % 2]], [b_Y[11]])


        gens = [gdn_tile(sti, tl) for tl in range(4)]
        active = []
        nxt = 0
        while nxt < len(gens) or active:
            while len(active) < 2 and nxt < len(gens):
                active.append(gens[nxt]); nxt += 1
            for g in list(active):
                try:
                    next(g)
                except StopIteration:
                    active.remove(g)
                    break


def build_wout(nc, fw, ps, st, sb, s, x, w_out, idb, b_idb, MIX, b_mix, H, b_h):
    wo = sb(st, "wo", [128, 8, DM], BF16); b_wo = fw.bufs_n("wo", 8)
    stg = [(sb(st, f"wstg{i}", [128, 1024], F32), fw.buf(f"wstg{i}")) for i in range(2)]
    wl = [load_cast_gen(fw, st, sb, "wo", w_out, 8, DM, wo, b_wo, None, None, stg)]

    def need_weights():
        drain(wl[0])
        wl[0] = None
    mT = [sb(st, f"mT{i}", [128, 8, 128], BF16) for i in range(2)]; b_mT = fw.bufs_n("mT", 2)
    def wout_tile(t):
        r = t % 2
        fw.dma(H[:, t, :], x[s, t * 128:(t + 1) * 128, :], writes=[b_h[t]])
        pt_, bpt_ = ps.rot()
        ptb = pt_[:].bitcast(BF16).rearrange("p (k c) -> p k c", k=8)
        for k in range(8):
            fw.op("pe", lambda e, k=k, t=t, ptb=ptb: e.transpose(out=ptb[:, k, :], in_=MIX[:, t, k * 128:(k + 1) * 128], identity=idb[:]),
                  [b_mix[t], b_idb], [bpt_])
        fw.op("act", lambda e, r=r, ptb=ptb: e.copy(out=mT[r][:], in_=ptb), [bpt_], [b_mT[r]])
        yield
        need_weights()
        for half in range(2):
            po, bpo = ps.rot()
            for k in range(8):
                fw.op("pe", lambda e, k=k, r=r, po=po, half=half: e.matmul(po[:, 0:512], lhsT=mT[r][:, k, :], rhs=wo[:, k, half * 512:(half + 1) * 512],
                                                                          start=(k == 0), stop=(k == 7)), [b_mT[r], b_wo[k]], [bpo])
            fw.op("dve", lambda e, t=t, po=po, half=half: e.tensor_tensor(out=H[:, t, half * 512:(half + 1) * 512], in0=po[:, 0:512],
                                                                          in1=H[:, t, half * 512:(half + 1) * 512], op=ALU.add),
                  [bpo, b_h[t]], [b_h[t]])
            yield

    run_rolling([wout_tile(t) for t in range(NT)], 2, bg=wl[0])
    need_weights()


def build_mlp(nc, fw, ps, st, sb, s, w_up, w_down, ppt, b_pp, idb, b_idb, HNT, H, b_h, out, rstd_ops):
    NB = 4
    FB = DFF // NB
    junk = sb(st, "junk2", [128, DM], F32); b_junk = fw.buf("junk2")
    hn = [sb(st, f"hn{i}", [128, DM], BF16) for i in range(2)]; b_hn = fw.bufs_n("hn", 2)
    st4 = [sb(st, f"st4{i}", [128, 4], F32) for i in range(2)]; b_st4 = fw.bufs_n("st4", 2)
    b_hnT = fw.bufs_n("hnT", NT)
    wu = [sb(st, f"wu{i}", [128, 8, FB], BF16) for i in range(2)]; b_wu = [fw.bufs_n(f"wu{i}", 8) for i in range(2)]
    wd = [sb(st, f"wd{i}", [128, 8, DM], BF16) for i in range(2)]; b_wd = [fw.bufs_n(f"wd{i}", 8) for i in range(2)]
    stg = [(sb(st, f"mstg{i}", [128, 1024], F32), fw.buf(f"mstg{i}")) for i in range(3)]
    aT = sb(st, "aT", [128, 8, 512], BF16); b_aT = fw.bufs_n("aT", 8)
    rl = [sb(st, f"rl{i}", [128, 512], BF16) for i in range(2)]; b_rl = fw.bufs_n("rl", 2)

    def load_block(nb):
        r = nb % 2
        load_cast(fw, st, sb, "wu", w_up[:, nb * FB:(nb + 1) * FB], 8, FB, wu[r], b_wu[r], ppt[:, P_MN:P_MN + 8], b_pp, stg)
        load_cast(fw, st, sb, "wd", w_down[nb * FB:(nb + 1) * FB, :], 8, DM, wd[r], b_wd[r], None, None, stg)

    def load_block_gen(nb):
        r = nb % 2
        gain = ppt[:, P_MN:P_MN + 8]
        for k in range(8):
            stt, b_st = stg[k % len(stg)]
            fw.dma(stt[:, 0:FB], w_up[k * 128:(k + 1) * 128, nb * FB:(nb + 1) * FB], writes=[b_st])
            fw.op("pool", lambda e, k=k, stt=stt, r=r: e.tensor_scalar(out=wu[r][:, k, :], in0=stt[:, 0:FB], scalar1=gain[:, k:k + 1], scalar2=1.0,
                                                                      op0=ALU.mult, op1=ALU.mult), [b_st, b_pp], [b_wu[r][k]])
            yield
        for k in range(8):
            stt, b_st = stg[k % len(stg)]
            fw.dma(stt[:, 0:DM], w_down[nb * FB + k * 128:nb * FB + (k + 1) * 128, :], writes=[b_st])
            fw.op("pool", lambda e, k=k, stt=stt, r=r: e.tensor_copy(out=wd[r][:, k, :], in_=stt[:, 0:DM]), [b_st], [b_wd[r][k]])
            yield

    def step_loader(g):
        try:
            next(g)
            return g
        except StopIteration:
            return None

    def block0_loader():
        yield from load_cast_gen(fw, st, sb, "wu", w_up[:, 0:FB], 8, FB, wu[0], b_wu[0], ppt[:, P_MN:P_MN + 8], b_pp, stg)
        yield from load_cast_gen(fw, st, sb, "wd", w_down[0:FB, :], 8, DM, wd[0], b_wd[0], None, None, stg)
    wloader = block0_loader()
    def hn_tile(t):
        r = t % 2
        fw.op("act", lambda e, t=t, r=r: e.activation(out=hn[r][:], in_=H[:, t, :], func=AF.Square, accum_out=st4[r][:, 0:1]), [b_h[t]], [b_st4[r], b_hn[r]])
        yield
        rstd_ops(None, st4[r][:, 0:1], st4[r][:, 1:2], 1.0 / DM, [b_st4[r]], [b_st4[r]], st4[r][:, 2:3], b_st4[r])
        yield
        fw.op("dve", lambda e, t=t, r=r: e.tensor_scalar(out=hn[r][:], in0=H[:, t, :], scalar1=st4[r][:, 1:2], scalar2=None, op0=ALU.mult),
              [b_h[t], b_st4[r]], [b_hn[r]])
        yield
        pt_, bpt_ = ps.rot()
        ptb = pt_[:].bitcast(BF16).rearrange("p (k c) -> p k c", k=8)
        for k in range(8):
            fw.op("pe", lambda e, k=k, r=r, ptb=ptb: e.transpose(out=ptb[:, k, :], in_=hn[r][:, k * 128:(k + 1) * 128], identity=idb[:]), [b_hn[r], b_idb], [bpt_])
        fw.op("act", lambda e, t=t, ptb=ptb: e.copy(out=HNT[:, :, t * 128:(t + 1) * 128], in_=ptb), [bpt_], [b_hnT[t]])
        hn_done.add(t)
        yield

    def rolling_gen(gens, width):
        gens = list(gens)
        active = []
        nxt = 0
        while nxt < len(gens) or active:
            while len(active) < width and nxt < len(gens):
                active.append(gens[nxt]); nxt += 1
            for g_ in list(active):
                try:
                    next(g_)
                except StopIteration:
                    active.remove(g_)
                    break
            yield

    hn_done = set()
    wloader = run_rolling([hn_tile(t) for t in range(4)], 2, bg=wloader)
    drain(wloader)
    hn_bg = rolling_gen([hn_tile(t) for t in range(4, NT)], 2)

    def step_hn():
        nonlocal hn_bg
        if hn_bg is not None:
            try:
                next(hn_bg)
            except StopIteration:
                hn_bg = None
    rc = 0
    for nb in range(NB):
        r = nb % 2
        loader = load_block_gen(nb + 1) if nb + 1 < NB else None
        for g in range(4):
            while hn_bg is not None and not all(t_ in hn_done for t_ in range(g * 4, g * 4 + 4)):
                step_hn()
            for c in range(8):
                pu, bpu = ps.rot()
                for k in range(8):
                    fw.op("pe", lambda e, k=k, c=c, r=r, g=g, pu=pu: e.matmul(pu[:, 0:512], lhsT=wu[r][:, k, c * 128:(c + 1) * 128],
                                                                             rhs=HNT[:, k, g * 512:(g + 1) * 512], start=(k == 0), stop=(k == 7)),
                          [b_wu[r][k]] + b_hnT[g * 4:(g + 1) * 4], [bpu])
                ri = rc % 2
                rc += 1
                fw.op("act", lambda e, ri=ri, pu=pu: e.activation(out=rl[ri][:], in_=pu[:, 0:512], func=AF.Relu), [bpu], [b_rl[ri]])
                fw.op("act", lambda e, ri=ri, c=c: e.activation(out=aT[:, c, :], in_=rl[ri][:], func=AF.Square), [b_rl[ri]], [b_aT[c]])
                if loader is not None and c % 2 == 1:
                    loader = step_loader(loader)
                if nb == 0:
                    step_hn()
            for tl in range(4):
                t = g * 4 + tl
                for half in range(2):
                    po, bpo = ps.acc(tl % 2 * 2 + half)
                    for c in range(8):
                        fw.op("pe", lambda e, c=c, tl=tl, r=r, half=half, po=po: e.matmul(po[:, 0:512], lhsT=aT[:, c, tl * 128:(tl + 1) * 128],
                                                                                         rhs=wd[r][:, c, half * 512:(half + 1) * 512],
                                                                                         start=(c == 0), stop=(c == 7)), [b_aT[c], b_wd[r][c]], [bpo])
                    fw.op("dve", lambda e, t=t, po=po, half=half: e.tensor_tensor(out=H[:, t, half * 512:(half + 1) * 512], in0=po[:, 0:512],
                                                                                  in1=H[:, t, half * 512:(half + 1) * 512], op=ALU.add),
                          [bpo, b_h[t]], [b_h[t]])
                if nb == NB - 1:
                    fw.dma(out[s, t * 128:(t + 1) * 128, :], H[:, t, :], reads=[b_h[t]])
        while loader is not None:
            loader = step_loader(loader)


def host_prepare(inputs):
    f = lambda a: np.ascontiguousarray(np.asarray(a), dtype=np.float32)
    w_in = f(inputs["w_in"])[0]
    o = np.cumsum([0, 256, 256, 64, 512, 512, 512, 512, 4, 4])
    perm = np.concatenate([np.arange(o[0], o[3]), np.arange(o[7], o[9]), np.arange(o[6], o[7]), np.arange(o[3], o[6])])
    w_in_p = np.ascontiguousarray(w_in[:, perm])
    pp = np.zeros((128, NPP), np.float32)
    pp[:, P_AN:P_AN + 8] = f(inputs["attn_norm_w"])[0].reshape(8, 128).T
    pp[:, P_QLN:P_QLN + 2] = f(inputs["q_lat_norm_w"])[0].reshape(2, 128).T
    pp[:, P_KVLN:P_KVLN + 2] = f(inputs["kv_lat_norm_w"])[0].reshape(2, 128).T
    pp[:, P_MN:P_MN + 8] = f(inputs["mlp_norm_w"])[0].reshape(8, 128).T
    cw = f(inputs["conv_w"])[0]
    pp[:, P_CW:P_CW + 48] = cw.reshape(4, 12, 128).transpose(2, 1, 0).reshape(128, 48)
    bc = np.zeros((128, NBC), np.float32)
    bc[:, B_QN:B_QN + 192] = f(inputs["q_norm_w"])[0][None, :]
    bc[:, B_KN:B_KN + 192] = f(inputs["k_norm_w"])[0][None, :]
    bc[:, B_MO:B_MO + 512] = f(inputs["mla_out_norm_w"])[0].reshape(-1)[None, :]
    bc[:, B_GN:B_GN + 128] = f(inputs["gdn_norm_w"])[0][None, :]
    bc[:, B_AL:B_AL + 4] = f(inputs["a_log"])[0][None, :]
    bc[:, B_DT:B_DT + 4] = f(inputs["dt_bias"])[0][None, :]
    cst = np.zeros((128, NK), np.float32)
    i = np.arange(128)
    cst[:, K_ID:K_ID + 128] = np.eye(128)
    cst[:, K_U:K_U + 128] = (i[:, None] <= i[None, :])
    cst[:, K_MNEG:K_MNEG + 128] = np.where(i[None, :] >= i[:, None], 0.0, -30000.0)
    cst[:, K_STR:K_STR + 128] = (i[None, :] > i[:, None])
    cst[:, K_INC:K_INC + 128] = (i[None, :] >= i[:, None])
    cst[:, K_ONE:K_ONE + 128] = 1.0
    half = 32
    inv_freq = (10000.0 ** (-(np.arange(half, dtype=np.float32) / np.float32(half)))).astype(np.float32)
    cst[:, K_IF:K_IF + 32] = inv_freq[None, :]
    x = f(inputs["x"])
    pos = np.asarray(inputs["positions"]).astype(np.int32)
    shared = {
        "w_in": w_in_p, "w_uq": f(inputs["w_uq"])[0], "w_ukv": f(inputs["w_ukv"])[0], "w_out": f(inputs["w_out"])[0],
        "w_up": f(inputs["w_up"])[0], "w_down": f(inputs["w_down"])[0], "pp": pp, "bc": bc, "cst": cst,
    }
    in_maps = []
    for c in range(NCORES):
        m = dict(shared)
        m["x"] = np.ascontiguousarray(x[2 * c:2 * c + 2])
        m["pos"] = np.ascontiguousarray(pos[2 * c:2 * c + 2].reshape(2, NT, 128).transpose(0, 2, 1))
        in_maps.append(m)
    return in_maps


_NC_CACHE = {}


def kernel(**inputs):
    in_maps = host_prepare(inputs)
    if "nc" not in _NC_CACHE:
        _NC_CACHE["nc"] = build_program()
    nc = _NC_CACHE["nc"]
    res = run_bass_kernel_spmd(nc, in_maps, core_ids=list(range(NCORES)))
    outs = [res.results[c]["out"] for c in range(NCORES)]
    return np.concatenate(outs, axis=0).astype(np.float32)
```

```python
import numpy as np
from contextlib import ExitStack
import concourse.bass as bass
import concourse.mybir as mybir
from concourse.bass_utils import run_bass_kernel_spmd

F32 = mybir.dt.float32
BF16 = mybir.dt.bfloat16
I32 = mybir.dt.int32
AF = mybir.ActivationFunctionType
ALU = mybir.AluOpType
AX = mybir.AxisListType

NCORES = 8
GDN_STOP = 99
NEU_LEVELS = 6
GDN_MAXTILE = 99
M1_BG = False
NEU_PINGPONG = True
SEQ = 2048
DM = 1024
NT = 16
DFF = 4096
EPS = 1e-6
PI = float(np.pi)
C_QL, C_KVL, C_KPE, C_A, C_B, C_Z, C_G = 0, 256, 512, 576, 580, 584, 1096
NW1 = 576
NW2 = 2632 - 576
B_QN, B_KN, B_MO, B_GN, B_AL, B_DT, NBC = 0, 192, 384, 896, 1024, 1028, 1032
P_AN, P_QLN, P_KVLN, P_MN, P_CW, NPP = 0, 8, 10, 12, 20, 68
K_ID, K_U, K_MNEG, K_STR, K_INC, K_ONE, K_IF, NK = 0, 128, 256, 384, 512, 640, 768, 800


class Buf:
    __slots__ = ("name", "last_w", "readers", "psum")

    def __init__(self, name):
        self.name = name
        self.last_w = None
        self.readers = []
        self.psum = False


class Op:
    __slots__ = ("eng", "fn", "deps", "signal", "dma", "tok")


class Chan:
    __slots__ = ("sem", "count", "last")


class Fw:
    ENG = ("pe", "act", "dve", "pool", "sp")

    def __init__(self, nc, stack, nchan=24):
        self.nc = nc
        self.ops = {e: [] for e in self.ENG}
        self.esem = {e: stack.enter_context(nc.semaphore("s_" + e)) for e in self.ENG}
        self.ecnt = {e: 0 for e in self.ENG}
        self.known = {e: {} for e in self.ENG}
        self.chans = []
        for i in range(nchan):
            c = Chan()
            c.sem = stack.enter_context(nc.semaphore(f"dch{i}"))
            c.count = 0
            c.last = None
            self.chans.append(c)
        self.nextchan = 0
        self.autoflush = True
        self.bufs = []
        self.pass_dmas = []
        self.ninst = 0
        self.nwait = 0

    def buf(self, name="b"):
        b = Buf(name)
        self.bufs.append(b)
        return b

    def bufs_n(self, name, n):
        return [self.buf(f"{name}{i}") for i in range(n)]

    def op(self, eng, fn, reads=(), writes=(), dma=False, extra=()):
        if self.autoflush and len(self.ops[eng]) >= 900:
            self.flush()
        o = Op()
        o.eng = eng; o.fn = fn; o.signal = False; o.dma = dma; o.tok = None
        deps = list(extra)
        for b in reads:
            if b.last_w is not None:
                deps.append(b.last_w)
            if b.psum:
                deps.extend(r for r in b.readers if r.eng != eng)
        for b in writes:
            if b.last_w is not None:
                deps.append(b.last_w)
            deps.extend(b.readers)
        dd = []
        seen = set()
        for d in deps:
            if id(d) in seen or d is None:
                continue
            seen.add(id(d))
            if eng == "pe" and d.eng == "pe" and not d.dma and not dma:
                continue
            dd.append(d)
        o.deps = dd
        for b in reads:
            b.readers.append(o)
        for b in writes:
            b.last_w = o
            b.readers = []
        self.ops[eng].append(o)
        return o

    def maybe_flush(self, limit=900):
        if max(len(v) for v in self.ops.values()) >= limit:
            self.flush()

    def flush(self):
        for b in self.bufs:
            if b.last_w is not None and not b.last_w.dma:
                b.last_w.signal = True
            for r in b.readers:
                if not r.dma:
                    r.signal = True
        self._emit()

    def dma(self, out, in_, reads=(), writes=(), eng="sp"):
        ch = self.chans[self.nextchan]
        self.nextchan = (self.nextchan + 1) % len(self.chans)
        ch.count += 16
        o = self.op(eng, lambda e: e.dma_start(out=out, in_=in_), reads=reads, writes=writes, dma=True,
                    extra=[ch.last])
        o.tok = (ch.sem, ch.count)
        ch.last = o
        self.pass_dmas.append(o)
        return o

    def end_pass(self):
        self.autoflush = False
        lasts = {}
        for e in self.ENG:
            for o in reversed(self.ops[e]):
                if not o.dma:
                    lasts[e] = o
                    break
        dma_last = [c.last for c in self.chans if c.last is not None]
        for f in self.ENG:
            extra = [lasts[e] for e in lasts if e != f] + dma_last
            self.op(f, lambda e: e.nop(), extra=extra)
        self._emit()
        self.autoflush = True
        for b in self.bufs:
            b.last_w = None
            b.readers = []
        for c in self.chans:
            c.last = None
        self.pass_dmas = []

    def _emit(self):
        for e in self.ENG:
            for o in self.ops[e]:
                for d in o.deps:
                    if not d.dma:
                        d.signal = True
        for e in self.ENG:
            for o in self.ops[e]:
                if (not o.dma) and o.signal and o.tok is None:
                    self.ecnt[e] += 1
                    o.tok = (self.esem[e], self.ecnt[e])
        fw = self
        with self.nc.Block() as block:
            def run(ename):
                def body(eng):
                    known = fw.known[ename]
                    for o in fw.ops[ename]:
                        need = {}
                        for d in o.deps:
                            assert d.tok is not None, f"dep without token on {ename}"
                            sem, val = d.tok
                            k = id(sem)
                            if known.get(k, 0) >= val:
                                continue
                            if k not in need or need[k][1] < val:
                                need[k] = (sem, val)
                        for k, (sem, val) in need.items():
                            eng.wait_ge(sem, val)
                            known[k] = val
                            fw.nwait += 1
                        ins = o.fn(eng)
                        fw.ninst += 1
                        if o.dma:
                            ins.then_inc(o.tok[0], 16)
                        elif o.signal:
                            ins.then_inc(o.tok[0], 1)
                return body
            block.tensor(run("pe"))
            block.scalar(run("act"))
            block.vector(run("dve"))
            block.gpsimd(run("pool"))
            block.sync(run("sp"))
        for e in self.ENG:
            self.ops[e] = []


class PS:
    def __init__(self, nc, fw, stack):
        self.big = stack.enter_context(nc.psum_tensor("psbig", [128, 8, 512], F32))
        self.t = [self.big[:, i, :] for i in range(8)]
        self.b = [fw.buf(f"psb{i}") for i in range(8)]
        for b in self.b:
            b.psum = True
        self.rb = 0

    def rot(self):
        i = 4 + self.rb
        self.rb = (self.rb + 1) % 4
        return self.t[i], self.b[i]

    def acc(self, i):
        return self.t[i], self.b[i]


def build_program(stages=("m1", "m2", "wout", "mlp"), nseq=2, dbg=False):
    nc = bass.Bass("TRN2", target_bir_lowering=False)

    def din(name, shape, dt=F32):
        return nc.dram_tensor(name, list(shape), dt, kind="ExternalInput").ap()
    x = din("x", [2, SEQ, DM])
    pos = din("pos", [2, 128, NT], I32)
    w_in = din("w_in", [DM, 2632])
    w_uq = din("w_uq", [256, 768])
    w_ukv = din("w_ukv", [256, 1024])
    w_out = din("w_out", [DM, DM])
    w_up = din("w_up", [DM, DFF])
    w_down = din("w_down", [DFF, DM])
    pp_d = din("pp", [128, NPP])
    bc_d = din("bc", [128, NBC])
    cst_d = din("cst", [128, NK])
    out = nc.dram_tensor("out", [2, SEQ, DM], F32, kind="ExternalOutput").ap()
    dbg_mix = None
    if dbg:
        dbg_mix = nc.dram_tensor("dbg_mix", [2, 128, NT * DM], BF16, kind="ExternalOutput").ap()

    with ExitStack() as gs:
        fw = Fw(nc, gs)
        ps = PS(nc, fw, gs)

        cnt = [0]

        def sb(st, name, shape, dt):
            cnt[0] += 1
            return st.enter_context(nc.sbuf_tensor(f"sb{cnt[0]}_{name}", list(shape), dt))

        cst = sb(gs, "cst", [128, NK], F32); b_cst = fw.buf("cst")
        ppt = sb(gs, "ppt", [128, NPP], F32); b_pp = fw.buf("pp")
        bct = sb(gs, "bct", [128, NBC], F32); b_bc = fw.buf("bc")
        idb = sb(gs, "idb", [128, 128], BF16); b_idb = fw.buf("idb")
        incb = sb(gs, "incb", [128, 128], BF16); b_incb = fw.buf("incb")
        fw.dma(cst[:], cst_d, writes=[b_cst])
        fw.dma(ppt[:], pp_d, writes=[b_pp])
        fw.dma(bct[:], bc_d, writes=[b_bc])
        fw.op("dve", lambda e: e.tensor_copy(out=idb[:], in_=cst[:, K_ID:K_ID + 128]), [b_cst], [b_idb])
        fw.op("dve", lambda e: e.tensor_copy(out=incb[:], in_=cst[:, K_INC:K_INC + 128]), [b_cst], [b_incb])
        fw.op("dve", lambda e: e.tensor_scalar(out=bct[:, B_QN:B_QN + 192], in0=bct[:, B_QN:B_QN + 192],
                                               scalar1=float(192 ** -0.5), scalar2=None, op0=ALU.mult), [b_bc], [b_bc])
        idf = cst[:, K_ID:K_ID + 128]
        fw.end_pass()

        def rstd_ops(eng_unused, ssq_ap, out_ap, scale, reads, writes, tmp_ap, b_tmp):
            fw.op("act", lambda e: e.activation(out=tmp_ap, in_=ssq_ap, func=AF.Ln, scale=scale, bias=EPS), reads, [b_tmp])
            fw.op("act", lambda e: e.activation(out=out_ap, in_=tmp_ap, func=AF.Exp, scale=-0.5), [b_tmp], writes)

        for s in range(nseq):
            with ExitStack() as ss:
                MIXR = sb(ss, "mixr", [128, NT * DM], BF16)
                MIX = MIXR[:].rearrange("p (t f) -> p t f", t=NT)
                HNT = MIXR[:].rearrange("p (k t) -> p k t", k=8)
                b_mix = fw.bufs_n("mix", NT)
                cs = sb(ss, "cs", [128, NT, 64], F32); b_cs = fw.buf("cs")

                if "m2" not in stages:
                    fw.op("pool", lambda e: e.memset(MIX[:, :, 512:1024], 0.0), [], b_mix)
                if "m1" in stages:
                    with ExitStack() as p1:
                        build_m1(nc, fw, ps, p1, sb, s, x, pos, w_in, w_uq, w_ukv, cst, ppt, bct, idb, incb,
                                 b_cst, b_pp, b_bc, b_idb, b_incb, MIX, b_mix, cs, b_cs, rstd_ops)
                        fw.end_pass()
                else:
                    fw.op("pool", lambda e: e.memset(MIXR[:], 0.0), [], b_mix)
                    fw.end_pass()
                if "m2" in stages:
                    with ExitStack() as p2:
                        build_m2(nc, fw, ps, p2, sb, s, x, w_in, cst, ppt, bct, idb, b_cst, b_pp, b_bc, b_idb, MIX, b_mix, rstd_ops)
                        fw.end_pass()
                if dbg:
                    fw.dma(dbg_mix[s], MIXR[:], reads=b_mix)
                    fw.end_pass()
                with ExitStack() as hs:
                    H = sb(hs, "H", [128, NT, DM], F32)
                    b_h = fw.bufs_n("h", NT)
                    if "wout" in stages:
                        with ExitStack() as p3:
                            build_wout(nc, fw, ps, p3, sb, s, x, w_out, idb, b_idb, MIX, b_mix, H, b_h)
                            fw.end_pass()
                    if "mlp" in stages:
                        with ExitStack() as p4:
                            build_mlp(nc, fw, ps, p4, sb, s, w_up, w_down, ppt, b_pp, idb, b_idb, HNT, H, b_h, out, rstd_ops)
                            fw.end_pass()
                    else:
                        for t in range(NT):
                            fw.dma(out[s, t * 128:(t + 1) * 128, :], H[:, t, :], reads=[b_h[t]])
                        fw.end_pass()
        print("instructions", fw.ninst, "waits", fw.nwait)
    return nc


def load_cast_gen(fw, st, sb, name, w_ap, nk, ncols, dst, b_dst, gain=None, b_gain=None, stg=None, engs=("act", "dve")):
    sw = min(int(stg[0][0].shape[1]), ncols)
    n = 0
    for k in range(nk):
        for c0 in range(0, ncols, sw):
            c1 = min(ncols, c0 + sw)
            stt, b_st = stg[n % len(stg)]
            eng = engs[n % len(engs)]
            n += 1
            fw.dma(stt[:, 0:c1 - c0], w_ap[k * 128:(k + 1) * 128, c0:c1], writes=[b_st])
            rd = [b_st] + ([b_gain] if gain is not None else [])
            if eng == "act":
                if gain is not None:
                    fw.op("act", lambda e, k=k, stt=stt, c0=c0, c1=c1: e.activation(out=dst[:, k, c0:c1], in_=stt[:, 0:c1 - c0], func=AF.Copy, scale=gain[:, k:k + 1]),
                          rd, [b_dst[k]])
                else:
                    fw.op("act", lambda e, k=k, stt=stt, c0=c0, c1=c1: e.activation(out=dst[:, k, c0:c1], in_=stt[:, 0:c1 - c0], func=AF.Copy), rd, [b_dst[k]])
            elif eng == "dve":
                if gain is not None:
                    fw.op("dve", lambda e, k=k, stt=stt, c0=c0, c1=c1: e.tensor_scalar(out=dst[:, k, c0:c1], in0=stt[:, 0:c1 - c0], scalar1=gain[:, k:k + 1], scalar2=None, op0=ALU.mult),
                          rd, [b_dst[k]])
                else:
                    fw.op("dve", lambda e, k=k, stt=stt, c0=c0, c1=c1: e.tensor_copy(out=dst[:, k, c0:c1], in_=stt[:, 0:c1 - c0]), rd, [b_dst[k]])
            else:
                if gain is not None:
                    fw.op("pool", lambda e, k=k, stt=stt, c0=c0, c1=c1: e.tensor_scalar(out=dst[:, k, c0:c1], in0=stt[:, 0:c1 - c0], scalar1=gain[:, k:k + 1], scalar2=1.0,
                                                                                     op0=ALU.mult, op1=ALU.mult), rd, [b_dst[k]])
                else:
                    fw.op("pool", lambda e, k=k, stt=stt, c0=c0, c1=c1: e.tensor_copy(out=dst[:, k, c0:c1], in_=stt[:, 0:c1 - c0]), rd, [b_dst[k]])
            yield


def load_cast(*a, **kw):
    for _ in load_cast_gen(*a, **kw):
        pass


def drain(g):
    if g is not None:
        for _ in g:
            pass


def run_rolling(gens, width=2, bg=None):
    gens = list(gens)
    active = []
    nxt = 0
    while nxt < len(gens) or active:
        while len(active) < width and nxt < len(gens):
            active.append(gens[nxt]); nxt += 1
        for g in list(active):
            try:
                next(g)
            except StopIteration:
                active.remove(g)
                break
        if bg is not None:
            try:
                next(bg)
            except StopIteration:
                bg = None
    return bg


def build_m1(nc, fw, ps, st, sb, s, x, pos, w_in, w_uq, w_ukv, cst, ppt, bct, idb, incb,
             b_cst, b_pp, b_bc, b_idb, b_incb, MIX, b_mix, cs, b_cs, rstd_ops):
    w1 = sb(st, "w1", [128, 8, NW1], BF16); b_w1 = fw.bufs_n("w1", 8)
    wuq = sb(st, "wuq", [128, 2, 768], BF16); b_wuq = fw.bufs_n("wuq", 2)
    wukv = sb(st, "wukv", [128, 2, 1024], BF16); b_wukv = fw.bufs_n("wukv", 2)
    stg = [(sb(st, f"stg{i}", [128, 1024], F32), fw.buf(f"stg{i}")) for i in range(2)]
    def m1_loader():
        yield from load_cast_gen(fw, st, sb, "w1", w_in[:, 0:NW1], 8, NW1, w1, b_w1, ppt[:, P_AN:P_AN + 8], b_pp, stg)
        yield from load_cast_gen(fw, st, sb, "wuq", w_uq, 2, 768, wuq, b_wuq, ppt[:, P_QLN:P_QLN + 2], b_pp, stg)
        yield from load_cast_gen(fw, st, sb, "wukv", w_ukv, 2, 1024, wukv, b_wukv, ppt[:, P_KVLN:P_KVLN + 2], b_pp, stg)
    wl = [m1_loader()]

    def need_weights():
        drain(wl[0])
        wl[0] = None

    posi = sb(st, "posi", [128, NT], I32); b_posi = fw.buf("posi")
    ang = sb(st, "ang", [128, NT, 32], F32); b_ang = fw.buf("ang")
    kq = sb(st, "kq", [128, NT, 32], F32); b_kq = fw.buf("kq")
    kqi = sb(st, "kqi", [128, NT, 32], I32); b_kqi = fw.buf("kqi")
    posf = sb(st, "posf", [128, NT], F32); b_posf = fw.buf("posf")
    fw.dma(posi[:], pos[s], writes=[b_posi])
    fw.op("dve", lambda e: e.tensor_copy(out=posf[:], in_=posi[:]), [b_posi], [b_posf])
    invf = cst[:, K_IF:K_IF + 32]
    fw.op("dve", lambda e: e.tensor_tensor(out=ang[:], in0=posf[:].unsqueeze(2).to_broadcast([128, NT, 32]),
                                           in1=invf.unsqueeze(1).to_broadcast([128, NT, 32]), op=ALU.mult),
          [b_posf, b_cst], [b_ang])
    fw.op("dve", lambda e: e.tensor_scalar(out=kq[:], in0=ang[:], scalar1=float(1.0 / (2 * PI)), scalar2=None, op0=ALU.mult), [b_ang], [b_kq])
    fw.op("dve", lambda e: e.tensor_copy(out=kqi[:], in_=kq[:]), [b_kq], [b_kqi])
    fw.op("dve", lambda e: e.tensor_copy(out=kq[:], in_=kqi[:]), [b_kqi], [b_kq])
    C1 = 6.28125
    C2 = float(2 * np.pi - 6.28125)
    fw.op("dve", lambda e: e.scalar_tensor_tensor(out=ang[:], in0=kq[:], scalar=-C1, in1=ang[:], op0=ALU.mult, op1=ALU.add), [b_kq, b_ang], [b_ang])
    fw.op("dve", lambda e: e.scalar_tensor_tensor(out=ang[:], in0=kq[:], scalar=-C2, in1=ang[:], op0=ALU.mult, op1=ALU.add), [b_kq, b_ang], [b_ang])
    fw.op("dve", lambda e: e.tensor_scalar(out=kq[:], in0=ang[:], scalar1=PI, scalar2=None, op0=ALU.is_gt), [b_ang], [b_kq])
    fw.op("dve", lambda e: e.scalar_tensor_tensor(out=ang[:], in0=kq[:], scalar=-2 * PI, in1=ang[:], op0=ALU.mult, op1=ALU.add), [b_kq, b_ang], [b_ang])
    fw.op("dve", lambda e: e.tensor_scalar(out=kq[:], in0=ang[:], scalar1=-PI, scalar2=None, op0=ALU.is_lt), [b_ang], [b_kq])
    fw.op("dve", lambda e: e.scalar_tensor_tensor(out=ang[:], in0=kq[:], scalar=2 * PI, in1=ang[:], op0=ALU.mult, op1=ALU.add), [b_kq, b_ang], [b_ang])
    fw.op("dve", lambda e: e.tensor_scalar(out=ang[:], in0=ang[:], scalar1=PI, scalar2=-PI, op0=ALU.min, op1=ALU.max), [b_ang], [b_ang])
    fw.op("act", lambda e: e.activation(out=cs[:, :, 32:64], in_=ang[:], func=AF.Sin), [b_ang], [b_cs])
    fw.op("act", lambda e: e.activation(out=kq[:], in_=ang[:], func=AF.Abs), [b_ang], [b_kq])
    fw.op("dve", lambda e: e.tensor_scalar(out=kq[:], in0=kq[:], scalar1=-1.0, scalar2=PI / 2, op0=ALU.mult, op1=ALU.add), [b_kq], [b_kq])
    fw.op("act", lambda e: e.activation(out=cs[:, :, 0:32], in_=kq[:], func=AF.Sin), [b_kq], [b_cs])
    cs2 = sb(st, "cs2", [128, NT, 128], F32); b_cs2 = fw.buf("cs2")
    fw.op("pool", lambda e: e.tensor_copy(out=cs2[:, :, 0:32], in_=cs[:, :, 0:32]), [b_cs], [b_cs2])
    fw.op("pool", lambda e: e.tensor_copy(out=cs2[:, :, 32:64], in_=cs[:, :, 0:32]), [b_cs], [b_cs2])
    fw.op("dve", lambda e: e.tensor_scalar(out=cs2[:, :, 64:96], in0=cs[:, :, 32:64], scalar1=-1.0, scalar2=None, op0=ALU.mult), [b_cs], [b_cs2])
    fw.op("pool", lambda e: e.tensor_copy(out=cs2[:, :, 96:128], in_=cs[:, :, 32:64]), [b_cs], [b_cs2])

    KT = sb(st, "KT", [128, 4, SEQ], BF16); b_kt = fw.bufs_n("kt", NT)
    KR = sb(st, "KR", [128, SEQ], BF16); b_kr = fw.bufs_n("kr", NT)
    fw.op("pool", lambda e: e.memset(KR[:], 0.0), [], b_kr)
    V = sb(st, "V", [128, NT, 4, 132], BF16); b_v = fw.bufs_n("v", NT)
    fw.op("pool", lambda e: e.memset(V[:], 1.0), [], b_v)
    xt = [sb(st, f"xt{i}", [128, DM], F32) for i in range(2)]; b_xt = fw.bufs_n("xt", 2)
    junk = sb(st, "junk", [128, DM], F32); b_junk = fw.buf("junk")
    xn = [sb(st, f"xn{i}", [128, DM], BF16) for i in range(2)]; b_xn = fw.bufs_n("xn", 2)
    xnT = [sb(st, f"xnT{i}", [128, 8, 128], BF16) for i in range(2)]; b_xnT = fw.bufs_n("xnT", 2)
    st8 = [sb(st, f"st8{i}", [128, 16], F32) for i in range(2)]; b_st8 = fw.bufs_n("st8", 2)
    tm8 = [sb(st, f"tm8{i}", [128, 16], F32) for i in range(2)]; b_tm8 = fw.bufs_n("tm8", 2)
    latn = [sb(st, f"latn{i}", [128, 512], BF16) for i in range(2)]; b_latn = fw.bufs_n("latn", 2)
    latT = [sb(st, f"latT{i}", [128, 4, 128], BF16) for i in range(2)]; b_latT = fw.bufs_n("latT", 2)
    kpe = [sb(st, f"kpe{i}", [128, 64], F32) for i in range(2)]; b_kpe = fw.bufs_n("kpe", 2)
    rtmp = [sb(st, f"rtmp{i}", [128, 4, 4, 32], F32) for i in range(2)]; b_rtmp = fw.bufs_n("rtmp", 2)
    krb = [sb(st, f"krb{i}", [128, 64], BF16) for i in range(2)]; b_krb = fw.bufs_n("krb", 2)
    qf = [sb(st, f"qf{i}", [128, 768], F32) for i in range(2)]; b_qf = fw.bufs_n("qf", 2)
    sq = [sb(st, f"sq{i}", [128, 1024], F32) for i in range(2)]; b_sq = fw.bufs_n("sq", 2)
    qb = [sb(st, f"qb{i}", [128, 4, 192], BF16) for i in range(2)]; b_qb = fw.bufs_n("qb", 2)
    kvf = [sb(st, f"kvf{i}", [128, 1024], F32) for i in range(2)]; b_kvf = fw.bufs_n("kvf", 2)
    kb = [sb(st, f"kb{i}", [128, 4, 128], BF16) for i in range(2)]; b_kb = fw.bufs_n("kb", 2)
    QT2 = [sb(st, f"QT{i}", [128, 4, 512], BF16) for i in range(2)]; b_qt2 = [fw.bufs_n(f"qt{i}", 4) for i in range(2)]
    QR2 = [sb(st, f"QR{i}", [128, 4, 512], BF16) for i in range(2)]; b_qr2 = [fw.bufs_n(f"qr{i}", 4) for i in range(2)]
    for i_ in range(2):
        fw.op("pool", lambda e, i_=i_: e.memset(QR2[i_][:], 0.0), [], b_qr2[i_])
    PT = [sb(st, f"PT{i}", [128, 512], BF16) for i in range(3)]; b_pt = fw.bufs_n("pt", 3)
    of = [sb(st, f"of{i}", [128, 128], F32) for i in range(2)]; b_of = fw.bufs_n("of", 2)
    ost = [sb(st, f"ost{i}", [128, 4], F32) for i in range(2)]; b_ost = fw.bufs_n("ost", 2)
    ptc = 0
    ofc = 0

    def proj_tile(t, tl, sti):
        nonlocal ptc, ofc
        t = sti * 4 + tl
        r = t % 2
        fw.dma(xt[r][:], x[s, t * 128:(t + 1) * 128, :], writes=[b_xt[r]])
        fw.op("act", lambda e, r=r: e.activation(out=junk[:], in_=xt[r][:], func=AF.Square, accum_out=st8[r][:, 0:1]),
              [b_xt[r]], [b_st8[r]])
        rstd_ops(None, st8[r][:, 0:1], st8[r][:, 1:2], 1.0 / DM, [b_st8[r]], [b_st8[r]], tm8[r][:, 0:1], b_tm8[r])
        fw.op("dve", lambda e, r=r: e.tensor_scalar(out=xn[r][:], in0=xt[r][:], scalar1=st8[r][:, 1:2], scalar2=None, op0=ALU.mult),
              [b_xt[r], b_st8[r]], [b_xn[r]])
        pt_, bpt_ = ps.rot()
        ptb = pt_[:].bitcast(BF16).rearrange("p (k c) -> p k c", k=8)
        for k in range(8):
            fw.op("pe", lambda e, k=k, r=r, ptb=ptb: e.transpose(out=ptb[:, k, :], in_=xn[r][:, k * 128:(k + 1) * 128], identity=idb[:]),
                  [b_xn[r], b_idb], [bpt_])
        fw.op("dve", lambda e, r=r, ptb=ptb: e.tensor_copy(out=xnT[r][:], in_=ptb), [bpt_], [b_xnT[r]])
        yield
        need_weights()
        pl, bpl = ps.rot()
        for k in range(8):
            fw.op("pe", lambda e, k=k, r=r, pl=pl: e.matmul(pl[:, 0:512], lhsT=xnT[r][:, k, :], rhs=w1[:, k, 0:512], start=(k == 0), stop=(k == 7)),
                  [b_xnT[r], b_w1[k]], [bpl])
        pk, bpk = ps.rot()
        for k in range(8):
            fw.op("pe", lambda e, k=k, r=r, pk=pk: e.matmul(pk[:, 0:64], lhsT=xnT[r][:, k, :], rhs=w1[:, k, 512:576], start=(k == 0), stop=(k == 7)),
                  [b_xnT[r], b_w1[k]], [bpk])
        fw.op("act", lambda e, r=r, pl=pl: e.activation(out=junk[:, 0:256], in_=pl[:, 0:256], func=AF.Square, accum_out=st8[r][:, 2:3]),
              [bpl], [b_st8[r]])
        fw.op("act", lambda e, r=r, pl=pl: e.activation(out=junk[:, 256:512], in_=pl[:, 256:512], func=AF.Square, accum_out=st8[r][:, 3:4]),
              [bpl], [b_st8[r]])
        fw.op("act", lambda e, r=r, pk=pk: e.activation(out=junk[:, 512:576], in_=pk[:, 0:64], func=AF.Square, accum_out=st8[r][:, 4:5]),
              [bpk], [b_st8[r]])
        rstd_ops(None, st8[r][:, 2:4], st8[r][:, 5:7], 1.0 / 256, [b_st8[r]], [b_st8[r]], tm8[r][:, 2:4], b_tm8[r])
        rstd_ops(None, st8[r][:, 4:5], st8[r][:, 7:8], 1.0 / 64, [b_st8[r]], [b_st8[r]], tm8[r][:, 4:5], b_tm8[r])
        fw.op("dve", lambda e, r=r, pl=pl: e.tensor_scalar(out=latn[r][:, 0:256], in0=pl[:, 0:256], scalar1=st8[r][:, 5:6], scalar2=None, op0=ALU.mult),
              [bpl, b_st8[r]], [b_latn[r]])
        fw.op("dve", lambda e, r=r, pl=pl: e.tensor_scalar(out=latn[r][:, 256:512], in0=pl[:, 256:512], scalar1=st8[r][:, 6:7], scalar2=None, op0=ALU.mult),
              [bpl, b_st8[r]], [b_latn[r]])
        fw.op("dve", lambda e, r=r, pk=pk: e.scalar_tensor_tensor(out=kpe[r][:], in0=pk[:, 0:64], scalar=st8[r][:, 7:8], in1=bct[:, B_KN + 128:B_KN + 192],
                                                                   op0=ALU.mult, op1=ALU.mult),
              [bpk, b_st8[r], b_bc], [b_kpe[r]])
        Rf = rtmp[r][:].rearrange("p a b c -> p (a b c)")
        CCt, SNt, SPt = cs2[:, t, 0:64], cs2[:, t, 64:96], cs2[:, t, 96:128]
        fw.op("pool", lambda e, r=r, Rf=Rf, CCt=CCt: e.tensor_tensor(out=Rf[:, 0:64], in0=kpe[r][:, 0:64], in1=CCt, op=ALU.mult), [b_kpe[r], b_cs2], [b_rtmp[r]])
        fw.op("dve", lambda e, r=r, Rf=Rf, SNt=SNt: e.tensor_tensor(out=Rf[:, 64:96], in0=kpe[r][:, 32:64], in1=SNt, op=ALU.mult), [b_kpe[r], b_cs2], [b_rtmp[r]])
        fw.op("dve", lambda e, r=r, Rf=Rf, SPt=SPt: e.tensor_tensor(out=Rf[:, 96:128], in0=kpe[r][:, 0:32], in1=SPt, op=ALU.mult), [b_kpe[r], b_cs2], [b_rtmp[r]])
        fw.op("pool", lambda e, r=r, Rf=Rf: e.tensor_tensor(out=krb[r][:, 0:64], in0=Rf[:, 0:64], in1=Rf[:, 64:128], op=ALU.add), [b_rtmp[r]], [b_krb[r]])
        yield
        pt2, bpt2 = ps.rot()
        pt2b = pt2[:].bitcast(BF16).rearrange("p (k c) -> p k c", k=8)
        for c in range(4):
            fw.op("pe", lambda e, c=c, r=r, pt2b=pt2b: e.transpose(out=pt2b[:, c, :], in_=latn[r][:, c * 128:(c + 1) * 128], identity=idb[:]),
                  [b_latn[r], b_idb], [bpt2])
        fw.op("pe", lambda e, r=r, pt2b=pt2b: e.transpose(out=pt2b[0:64, 4, :], in_=krb[r][:, 0:64], identity=idb[:]),
              [b_krb[r], b_idb], [bpt2])
        fw.op("act", lambda e, r=r, pt2b=pt2b: e.copy(out=latT[r][:], in_=pt2b[:, 0:4, :]), [bpt2], [b_latT[r]])
        fw.op("act", lambda e, t=t, pt2b=pt2b: e.copy(out=KR[0:64, t * 128:(t + 1) * 128], in_=pt2b[0:64, 4, :]), [bpt2], [b_kr[t]])
        yield
        pq0, bpq0 = ps.rot()
        pq1, bpq1 = ps.rot()
        for c in range(2):
            fw.op("pe", lambda e, c=c, r=r, pq0=pq0: e.matmul(pq0[:, 0:512], lhsT=latT[r][:, c, :], rhs=wuq[:, c, 0:512], start=(c == 0), stop=(c == 1)),
                  [b_latT[r], b_wuq[c]], [bpq0])
        for c in range(2):
            fw.op("pe", lambda e, c=c, r=r, pq1=pq1: e.matmul(pq1[:, 0:256], lhsT=latT[r][:, c, :], rhs=wuq[:, c, 512:768], start=(c == 0), stop=(c == 1)),
                  [b_latT[r], b_wuq[c]], [bpq1])
        fw.op("act", lambda e, r=r, pq0=pq0: e.copy(out=qf[r][:, 0:512], in_=pq0[:, 0:512]), [bpq0], [b_qf[r]])
        fw.op("act", lambda e, r=r, pq1=pq1: e.copy(out=qf[r][:, 512:768], in_=pq1[:, 0:256]), [bpq1], [b_qf[r]])
        pv0, bpv0 = ps.rot()
        pv1, bpv1 = ps.rot()
        for hh, (pv, bpv) in enumerate(((pv0, bpv0), (pv1, bpv1))):
            for c in range(2):
                fw.op("pe", lambda e, c=c, r=r, pv=pv, hh=hh: e.matmul(pv[:, 0:512], lhsT=latT[r][:, 2 + c, :], rhs=wukv[:, c, hh * 512:(hh + 1) * 512],
                                                                      start=(c == 0), stop=(c == 1)),
                      [b_latT[r], b_wukv[c]], [bpv])
            fw.op("dve", lambda e, r=r, pv=pv, hh=hh: e.tensor_copy(out=kvf[r][:, hh * 512:(hh + 1) * 512], in_=pv[:, 0:512]), [bpv], [b_kvf[r]])
        yield
        q3 = qf[r][:].rearrange("p (h d) -> p h d", h=4)
        s3 = sq[r][:, 0:768].rearrange("p (h d) -> p h d", h=4)
        fw.op("pool", lambda e, r=r: e.tensor_tensor(out=sq[r][:, 0:768], in0=qf[r][:], in1=qf[r][:], op=ALU.mult), [b_qf[r]], [b_sq[r]])
        fw.op("dve", lambda e, r=r, s3=s3: e.tensor_reduce(out=st8[r][:, 8:12], in_=s3[:, :, 0:128], axis=AX.X, op=ALU.add), [b_sq[r]], [b_st8[r]])
        fw.op("dve", lambda e, r=r, s3=s3: e.tensor_reduce(out=st8[r][:, 12:16], in_=s3[:, :, 128:192], axis=AX.X, op=ALU.add), [b_sq[r]], [b_st8[r]])
        rstd_ops(None, st8[r][:, 8:12], tm8[r][:, 8:12], 1.0 / 128, [b_st8[r]], [b_tm8[r]], st8[r][:, 8:12], b_st8[r])
        rstd_ops(None, st8[r][:, 12:16], tm8[r][:, 12:16], 1.0 / 64, [b_st8[r]], [b_tm8[r]], st8[r][:, 12:16], b_st8[r])
        yield
        s4 = sq[r][:, 0:768].rearrange("p (h d) -> p h d", h=4)
        fw.op("dve", lambda e, r=r, q3=q3, s4=s4: e.tensor_tensor(out=s4[:, :, 0:128], in0=q3[:, :, 0:128],
                                                                  in1=tm8[r][:, 8:12].unsqueeze(2).to_broadcast([128, 4, 128]), op=ALU.mult),
              [b_qf[r], b_tm8[r]], [b_sq[r]])
        fw.op("pool", lambda e, r=r, s4=s4: e.tensor_tensor(out=qb[r][:, :, 0:128], in0=s4[:, :, 0:128],
                                                            in1=bct[:, B_QN:B_QN + 128].unsqueeze(1).to_broadcast([128, 4, 128]), op=ALU.mult),
              [b_sq[r], b_bc], [b_qb[r]])
        fw.op("dve", lambda e, r=r, q3=q3, s4=s4: e.tensor_tensor(out=s4[:, :, 128:192], in0=q3[:, :, 128:192],
                                                                  in1=tm8[r][:, 12:16].unsqueeze(2).to_broadcast([128, 4, 64]), op=ALU.mult),
              [b_qf[r], b_tm8[r]], [b_sq[r]])
        fw.op("pool", lambda e, r=r, s4=s4: e.tensor_tensor(out=s4[:, :, 128:192], in0=s4[:, :, 128:192],
                                                            in1=bct[:, B_QN + 128:B_QN + 192].unsqueeze(1).to_broadcast([128, 4, 64]), op=ALU.mult),
              [b_sq[r], b_bc], [b_sq[r]])
        A4 = Rf[:, 0:256].rearrange("p (h d) -> p h d", h=4)
        B4 = Rf[:, 256:512].rearrange("p (h d) -> p h d", h=4)
        fw.op("pool", lambda e, s4=s4, A4=A4, CCt=CCt: e.tensor_tensor(out=A4, in0=s4[:, :, 128:192], in1=CCt.unsqueeze(1).to_broadcast([128, 4, 64]), op=ALU.mult),
              [b_sq[r], b_cs2], [b_rtmp[r]])
        fw.op("dve", lambda e, s4=s4, B4=B4, SNt=SNt: e.tensor_tensor(out=B4[:, :, 0:32], in0=s4[:, :, 160:192], in1=SNt.unsqueeze(1).to_broadcast([128, 4, 32]), op=ALU.mult),
              [b_sq[r], b_cs2], [b_rtmp[r]])
        fw.op("dve", lambda e, s4=s4, B4=B4, SPt=SPt: e.tensor_tensor(out=B4[:, :, 32:64], in0=s4[:, :, 128:160], in1=SPt.unsqueeze(1).to_broadcast([128, 4, 32]), op=ALU.mult),
              [b_sq[r], b_cs2], [b_rtmp[r]])
        fw.op("pool", lambda e, r=r, A4=A4, B4=B4: e.tensor_tensor(out=qb[r][:, :, 128:192], in0=A4, in1=B4, op=ALU.add), [b_rtmp[r]], [b_qb[r]])
        yield
        pt3, bpt3 = ps.rot()
        pt3b = pt3[:].bitcast(BF16).rearrange("p (k c) -> p k c", k=8)
        for h in range(4):
            fw.op("pe", lambda e, h=h, r=r, pt3b=pt3b: e.transpose(out=pt3b[:, h, :], in_=qb[r][:, h, 0:128], identity=idb[:]), [b_qb[r], b_idb], [bpt3])
        for h in range(4):
            fw.op("pe", lambda e, h=h, r=r, pt3b=pt3b: e.transpose(out=pt3b[0:64, 4 + h, :], in_=qb[r][:, h, 128:192], identity=idb[:]), [b_qb[r], b_idb], [bpt3])
        fw.op("act", lambda e, tl=tl, pt3b=pt3b: e.copy(out=QT2[sti % 2][:, :, tl * 128:(tl + 1) * 128], in_=pt3b[:, 0:4, :]), [bpt3], [b_qt2[sti % 2][tl]])
        fw.op("act", lambda e, tl=tl, pt3b=pt3b: e.copy(out=QR2[sti % 2][0:64, :, tl * 128:(tl + 1) * 128], in_=pt3b[0:64, 4:8, :]), [bpt3], [b_qr2[sti % 2][tl]])
        yield
        k3 = kvf[r][:].rearrange("p (h d) -> p h d", h=4)
        sk = sq[r][:, 0:512].rearrange("p (h d) -> p h d", h=4)
        fw.op("pool", lambda e, k3=k3, sk=sk: e.tensor_tensor(out=sk, in0=k3[:, :, 0:128], in1=k3[:, :, 0:128], op=ALU.mult), [b_kvf[r]], [b_sq[r]])
        fw.op("dve", lambda e, r=r, sk=sk: e.tensor_reduce(out=st8[r][:, 8:12], in_=sk, axis=AX.X, op=ALU.add), [b_sq[r]], [b_st8[r]])
        rstd_ops(None, st8[r][:, 8:12], tm8[r][:, 8:12], 1.0 / 128, [b_st8[r]], [b_tm8[r]], st8[r][:, 8:12], b_st8[r])
        fw.op("dve", lambda e, r=r, k3=k3, sk=sk: e.tensor_tensor(out=sk, in0=k3[:, :, 0:128],
                                                                  in1=tm8[r][:, 8:12].unsqueeze(2).to_broadcast([128, 4, 128]), op=ALU.mult),
              [b_kvf[r], b_tm8[r]], [b_sq[r]])
        fw.op("pool", lambda e, r=r, sk=sk: e.tensor_tensor(out=kb[r][:], in0=sk,
                                                            in1=bct[:, B_KN:B_KN + 128].unsqueeze(1).to_broadcast([128, 4, 128]), op=ALU.mult),
              [b_sq[r], b_bc], [b_kb[r]])
        fw.op("dve", lambda e, t=t, k3=k3: e.tensor_copy(out=V[:, t, :, 0:128], in_=k3[:, :, 128:256]), [b_kvf[r]], [b_v[t]])
        pt4, bpt4 = ps.rot()
        pt4b = pt4[:].bitcast(BF16).rearrange("p (k c) -> p k c", k=8)
        for h in range(4):
            fw.op("pe", lambda e, h=h, r=r, pt4b=pt4b: e.transpose(out=pt4b[:, h, :], in_=kb[r][:, h, :], identity=idb[:]), [b_kb[r], b_idb], [bpt4])
        fw.op("act", lambda e, t=t, pt4b=pt4b: e.copy(out=KT[:, :, t * 128:(t + 1) * 128], in_=pt4b[:, 0:4, :]), [bpt4], [b_kt[t]])

        yield

    def attn(sti):
        nonlocal ptc, ofc
        qp = sti % 2
        nkt = sti * 4 + 4
        for h in range(4):
            oacc = [ps.acc(i) for i in range(4)]
            def issue_qk(j, h=h):
                c0 = max(j, sti * 4) - sti * 4
                ncol = (4 - c0) * 128
                sp_, bsp_ = ps.rot()
                rds = [b_kt[j], b_kr[j]] + [b_qt2[qp][i] for i in range(c0, 4)] + [b_qr2[qp][i] for i in range(c0, 4)]
                fw.op("pe", lambda e, h=h, j=j, c0=c0, ncol=ncol, sp_=sp_, qp=qp: e.matmul(sp_[:, 0:ncol], lhsT=KT[:, h, j * 128:(j + 1) * 128],
                                                                                   rhs=QT2[qp][:, h, c0 * 128:512], start=True, stop=False), rds, [bsp_])
                fw.op("pe", lambda e, h=h, j=j, c0=c0, ncol=ncol, sp_=sp_, qp=qp: e.matmul(sp_[:, 0:ncol], lhsT=KR[:, j * 128:(j + 1) * 128],
                                                                                   rhs=QR2[qp][:, h, c0 * 128:512], start=False, stop=True), rds, [bsp_])
                return sp_, bsp_, c0, ncol

            nxt = issue_qk(0)
            for j in range(nkt):
                sp_, bsp_, c0, ncol = nxt
                if j + 1 < nkt:
                    nxt = issue_qk(j + 1)
                pi = ptc % 3
                ptc += 1
                fw.op("act", lambda e, pi=pi, ncol=ncol, sp_=sp_: e.activation(out=PT[pi][:, 0:ncol], in_=sp_[:, 0:ncol], func=AF.Exp), [bsp_], [b_pt[pi]])
                if j >= sti * 4:
                    fw.op("pool", lambda e, pi=pi: e.tensor_tensor(out=PT[pi][:, 0:128], in0=PT[pi][:, 0:128], in1=incb[:], op=ALU.mult),
                          [b_pt[pi], b_incb], [b_pt[pi]])
                for qi in range(c0, 4):
                    po, bpo = oacc[qi]
                    fw.op("pe", lambda e, pi=pi, qi=qi, c0=c0, j=j, h=h, po=po, sti=sti: e.matmul(po[:, 0:129], lhsT=PT[pi][:, (qi - c0) * 128:(qi - c0 + 1) * 128],
                                                                                        rhs=V[:, j, h, 0:129], start=(j == 0), stop=(j == sti * 4 + qi)),
                          [b_pt[pi], b_v[j]], [bpo])
                if M1_BG:
                    yield
            for qi in range(4):
                t = sti * 4 + qi
                po, bpo = oacc[qi]
                oi = ofc % 2
                ofc += 1
                fw.op("dve", lambda e, oi=oi, po=po: e.reciprocal(out=ost[oi][:, 0:1], in_=po[:, 128:129]), [bpo], [b_ost[oi]])
                fw.op("dve", lambda e, oi=oi, po=po: e.tensor_scalar(out=of[oi][:], in0=po[:, 0:128], scalar1=ost[oi][:, 0:1], scalar2=None, op0=ALU.mult),
                      [bpo, b_ost[oi]], [b_of[oi]])
                fw.op("act", lambda e, oi=oi: e.activation(out=junk[:, 0:128], in_=of[oi][:], func=AF.Square, accum_out=ost[oi][:, 1:2]),
                      [b_of[oi]], [b_ost[oi]])
                rstd_ops(None, ost[oi][:, 1:2], ost[oi][:, 2:3], 1.0 / 128, [b_ost[oi]], [b_ost[oi]], ost[oi][:, 3:4], b_ost[oi])
                fw.op("dve", lambda e, oi=oi, t=t, h=h: e.scalar_tensor_tensor(out=MIX[:, t, h * 128:(h + 1) * 128], in0=of[oi][:], scalar=ost[oi][:, 2:3],
                                                                              in1=bct[:, B_MO + h * 128:B_MO + (h + 1) * 128], op0=ALU.mult, op1=ALU.mult),
                      [b_of[oi], b_ost[oi], b_bc], [b_mix[t]])


                yield

    def run_strands(strands, bg=None, bg_steps=1):
        strands = list(strands)
        while strands:
            for g in list(strands):
                try:
                    next(g)
                except StopIteration:
                    strands.remove(g)
            if bg is not None:
                for _ in range(bg_steps):
                    try:
                        next(bg)
                    except StopIteration:
                        bg = None
                        break
        return bg

    for sti in range(4):
        run_rolling([proj_tile(sti * 4 + tl, tl, sti) for tl in range(4)], 2, bg=wl[0])
        need_weights()
        run_strands([attn(sti)])


def build_m2(nc, fw, ps, st, sb, s, x, w_in, cst, ppt, bct, idb, b_cst, b_pp, b_bc, b_idb, MIX, b_mix, rstd_ops):
    idf = cst[:, K_ID:K_ID + 128]
    Uf = cst[:, K_U:K_U + 128]
    onesf = cst[:, K_ONE:K_ONE + 128]
    mneg = cst[:, K_MNEG:K_MNEG + 128]
    strf = cst[:, K_STR:K_STR + 128]
    w2 = sb(st, "w2", [128, 8, NW2], BF16); b_w2 = fw.bufs_n("w2", 8)
    stg = [(sb(st, f"stg2{i}", [128, NW2 // 2], F32), fw.buf(f"stg2{i}")) for i in range(2)]
    wl = [load_cast_gen(fw, st, sb, "w2", w_in[:, NW1:NW1 + NW2], 8, NW2, w2, b_w2, ppt[:, P_AN:P_AN + 8], b_pp, stg)]

    def need_weights():
        drain(wl[0])
        wl[0] = None
    O_AB, O_Z, O_GQ = 0, 8, 520
    xt = [sb(st, f"xt{i}", [128, DM], F32) for i in range(2)]; b_xt = fw.bufs_n("xt", 2)
    junk = sb(st, "junk", [128, DM], F32)
    xn = [sb(st, f"xn{i}", [128, DM], BF16) for i in range(2)]; b_xn = fw.bufs_n("xn", 2)
    xnT = sb(st, "xnT", [128, 8, 512], BF16); b_xnT = fw.bufs_n("xnT", 4)
    st8 = [sb(st, f"st8{i}", [128, 4], F32) for i in range(2)]; b_st8 = fw.bufs_n("st8", 2)
    ab = sb(st, "ab", [128, 4, 8], F32); b_ab = fw.bufs_n("ab", 4)
    gb = sb(st, "gb", [128, 4, 16], F32); b_gb = fw.bufs_n("gb", 4)
    zsg = sb(st, "zsg", [128, 4, 512], F32); b_zsg = fw.bufs_n("zsg", 4)
    Xc = [sb(st, f"Xc{i}", [128, 515], F32) for i in range(2)]; b_Xc = fw.bufs_n("Xc", 2)
    yacc = [sb(st, f"yacc{i}", [128, 512], F32) for i in range(2)]; b_yacc = fw.bufs_n("yacc", 2)
    halo = sb(st, "halo", [128, 12, 4], F32); b_halo = fw.bufs_n("halo", 12)
    Y = sb(st, "Y", [128, 12, 512], BF16); b_Y = fw.bufs_n("Y", 12)
    Sf = sb(st, "Sf", [128, 4, 128], F32); b_Sf = fw.bufs_n("Sf", 4)
    Sb = sb(st, "Sb", [128, 4, 128], BF16); b_Sb = fw.bufs_n("Sb", 4)
    fw.op("pool", lambda e: e.memset(halo[:], 0.0), [], b_halo)
    fw.op("pool", lambda e: e.memset(Sf[:], 0.0), [], b_Sf)
    fw.op("pool", lambda e: e.memset(Sb[:], 0.0), [], b_Sb)
    def two(f):
        return [f(0), f(1)]
    Gs2 = two(lambda q: sb(st, f"Gs{q}", [128, 24], F32)); b_Gs2 = fw.bufs_n("Gs", 2)
    sc2 = two(lambda q: sb(st, f"sc{q}", [128, 40], F32)); b_sc2 = fw.bufs_n("sc", 2)
    so2 = two(lambda q: sb(st, f"so{q}", [128, 12], F32)); b_so2 = fw.bufs_n("so", 2)
    g3f2 = two(lambda q: sb(st, f"g3f{q}", [128, 3, 4], F32)); b_g3f2 = fw.bufs_n("g3f", 2)
    g3b2 = two(lambda q: sb(st, f"g3b{q}", [128, 3, 4], BF16)); b_g3b2 = fw.bufs_n("g3b", 2)
    gr2 = two(lambda q: sb(st, f"gr{q}", [128, 8], F32)); b_gr2 = fw.bufs_n("gr", 2)
    NS = 2 if NEU_PINGPONG else 7

    def per_head(name, shape, dt):
        arrs = two(lambda q: sb(st, f"{name}{q}", [128, 4] + list(shape[1:]), dt))
        views = [[arrs[q][:, h] for h in range(4)] for q in range(2)]
        return views, two(lambda q: fw.bufs_n(f"{name}{q}", 4)), arrs

    def per_head_ns(name, shape, dt):
        arrs = two(lambda q: [sb(st, f"{name}{q}_{i}", [128, 4] + list(shape[1:]), dt) for i in range(NS)])
        views = [[[arrs[q][i][:, h] for i in range(NS)] for h in range(4)] for q in range(2)]
        return views, two(lambda q: [fw.bufs_n(f"{name}{q}{h}", NS) for h in range(4)]), arrs
    k6s, b_k6s, k6A = per_head("k6", [128, 6, 128], BF16)
    kqTs, b_kqTs, kqTA = per_head("kqT", [128, 3, 128], BF16)
    Ug3s, b_Ugs, Ug3A = per_head("Ug3", [128, 3, 128], BF16)
    tDs, b_tDs, tDA = per_head("tD", [128, 128], F32)
    dTs, b_dTs, dTA = per_head("dT", [128, 128], F32)
    dSs, b_dSs, dSA = per_head("dS", [128, 128], F32)
    Mms, b_Mms, MmA = per_head_ns("Mm", [128, 128], BF16)
    MmTs, b_MmTs, MmTA = per_head_ns("MmT", [128, 128], BF16)
    Pms, b_Pms, PmA = per_head_ns("Pm", [128, 128], BF16)
    aTs, b_aTs, aTA = per_head("attT", [128, 128], BF16)
    Ubs, b_Ubs, UbA = per_head("Ub", [128, 128], F32)
    WTs, b_WTs, WTA = per_head("WT", [128, 128], BF16)
    vns, b_vns, vnA = per_head("vn", [128, 128], BF16)
    ubf = sb(st, "ubf", [128, 128], BF16); b_ubf = fw.buf("ubf")
    onb = sb(st, "onb", [128, 128], BF16); b_onb = fw.buf("onb")
    fw.op("dve", lambda e: e.tensor_copy(out=ubf[:], in_=Uf), [b_cst], [b_ubf])
    fw.op("dve", lambda e: e.tensor_copy(out=onb[:], in_=onesf), [b_cst], [b_onb])
    stage = [0]

    def bank(h):
        i = (stage[0] % 2) * 4 + h
        return ps.t[i], ps.b[i]

    def next_stage():
        stage[0] += 1

    s8_done = set()

    def gdn_tile(sti, tl):
        sel = tl % 2
        Gs, sc, so, g3f, g3b, gr = Gs2[sel], sc2[sel], so2[sel], g3f2[sel], g3b2[sel], gr2[sel]
        b_Gs, b_sc, b_so, b_g3f, b_g3b, b_gr = b_Gs2[sel], b_sc2[sel], b_so2[sel], b_g3f2[sel], b_g3b2[sel], b_gr2[sel]
        k6, kqT, Ug3, tD, dT, dS, Mm, MmT, Pm, aT, Ub, WT, vn = [x_[sel] for x_ in (k6s, kqTs, Ug3s, tDs, dTs, dSs, Mms, MmTs, Pms, aTs, Ubs, WTs, vns)]
        b_k6, b_kqT, b_Ug, b_tD, b_dT, b_dS, b_Mm, b_MmT, b_Pm, b_aT, b_Ub, b_WT, b_vn = [x_[sel] for x_ in (b_k6s, b_kqTs, b_Ugs, b_tDs, b_dTs, b_dSs, b_Mms, b_MmTs, b_Pms, b_aTs, b_Ubs, b_WTs, b_vns)]

        def bank(h):
            return ps.t[sel * 4 + h], ps.b[sel * 4 + h]
        bpall = [ps.b[sel * 4 + h] for h in range(4)]
        G4 = ps.big[:, sel * 4:(sel + 1) * 4, :]
        G4b = G4.bitcast(BF16)
        k6a, kqTa, Ug3a, tDa, dTa, dSa, aTa, Uba, WTa = k6A[sel], kqTA[sel], Ug3A[sel], tDA[sel], dTA[sel], dSA[sel], aTA[sel], UbA[sel], WTA[sel]
        Mma, MmTa, Pma = MmA[sel], MmTA[sel], PmA[sel]

        def bc(ap4):
            return ap4.unsqueeze(2).to_broadcast([128, 4, 128])

        def allb(bl, i=None):
            return [bl[h] if i is None else bl[h][i] for h in range(4)]
        t = sti * 4 + tl
        cols = slice(tl * 128, (tl + 1) * 128)
        G_ = gb[:, tl, :]
        if t > GDN_MAXTILE:
            return
        fw.op("dve", lambda e, G_=G_: e.tensor_copy(out=g3b[:, 0, :], in_=G_[:, 0:4]), [b_gb[tl]], [b_g3b])
        fw.op("dve", lambda e: e.tensor_copy(out=g3f[:, 0, :], in_=g3b[:, 0, :]), [b_g3b], [b_g3f])
        fw.op("dve", lambda e, G_=G_: e.tensor_tensor(out=gr[:, 0:4], in0=G_[:, 0:4], in1=g3f[:, 0, :], op=ALU.subtract), [b_gb[tl], b_g3f], [b_gr])
        fw.op("dve", lambda e: e.tensor_copy(out=g3b[:, 1, :], in_=gr[:, 0:4]), [b_gr], [b_g3b])
        fw.op("dve", lambda e: e.tensor_copy(out=g3f[:, 1, :], in_=g3b[:, 1, :]), [b_g3b], [b_g3f])
        fw.op("dve", lambda e: e.tensor_tensor(out=gr[:, 4:8], in0=gr[:, 0:4], in1=g3f[:, 1, :], op=ALU.subtract), [b_gr, b_g3f], [b_gr])
        fw.op("dve", lambda e: e.tensor_copy(out=g3b[:, 2, :], in_=gr[:, 4:8]), [b_gr], [b_g3b])
        fw.op("dve", lambda e: e.tensor_copy(out=g3f[:, 2, :], in_=g3b[:, 2, :]), [b_g3b], [b_g3f])
        pg, bpg = ps.rot()
        for i in range(3):
            fw.op("pe", lambda e, pg=pg, i=i: e.matmul(pg[:, 0:4], lhsT=ubf[:], rhs=g3b[:, i, :], start=(i == 0), stop=(i == 2)), [b_ubf, b_g3b], [bpg])
        for i in range(3):
            fw.op("pe", lambda e, pg=pg, i=i: e.matmul(pg[:, 4:8], lhsT=onb[:], rhs=g3b[:, i, :], start=(i == 0), stop=(i == 2), skip_group_check=True), [b_onb, b_g3b], [bpg])
        fw.op("dve", lambda e, pg=pg: e.tensor_copy(out=Gs[:, 0:8], in_=pg[:, 0:8]), [bpg], [b_Gs])
        fw.op("act", lambda e: e.activation(out=Gs[:, 8:12], in_=Gs[:, 0:4], func=AF.Exp), [b_Gs], [b_Gs])
        fw.op("dve", lambda e: e.tensor_tensor(out=Gs[:, 20:24], in0=Gs[:, 4:8], in1=Gs[:, 0:4], op=ALU.subtract), [b_Gs], [b_Gs])
        fw.op("act", lambda e: e.activation(out=Gs[:, 12:16], in_=Gs[:, 20:24], func=AF.Exp), [b_Gs], [b_Gs])
        fw.op("act", lambda e: e.activation(out=Gs[:, 16:20], in_=Gs[:, 4:8], func=AF.Exp), [b_Gs], [b_Gs])
        yield
        B1 = [bank(h) for h in range(4)]
        for h in range(4):
            p1, bp1 = B1[h]
            p1b = p1[:].bitcast(BF16).rearrange("p (k c) -> p k c", k=8)
            for i, c in enumerate((h, 4 + h, 8 + h)):
                fw.op("pe", lambda e, p1b=p1b, i=i, c=c, cols=cols: e.transpose(out=p1b[:, i, :], in_=Y[:, c, cols], identity=idb[:]), [b_Y[c], b_idb], [bp1])
            fw.op("act", lambda e, p1b=p1b, h=h: e.activation(out=junk[:, 0:128], in_=p1b[:, 0, :], func=AF.Square, accum_out=sc[:, h:h + 1]), [bp1], [b_sc])
            fw.op("act", lambda e, p1b=p1b, h=h: e.activation(out=junk[:, 128:256], in_=p1b[:, 1, :], func=AF.Square, accum_out=sc[:, 4 + h:5 + h]), [bp1], [b_sc])
        fw.op("act", lambda e: e.activation(out=sc[:, 8:16], in_=sc[:, 0:8], func=AF.Ln, bias=EPS), [b_sc], [b_sc])
        fw.op("act", lambda e: e.activation(out=sc[:, 16:24], in_=sc[:, 8:16], func=AF.Exp, scale=-0.5), [b_sc], [b_sc])
        fw.op("dve", lambda e: e.tensor_tensor(out=sc[:, 24:28], in0=sc[:, 20:24], in1=Gs[:, 8:12], op=ALU.mult), [b_sc, b_Gs], [b_sc])
        fw.op("dve", lambda e: e.tensor_tensor(out=sc[:, 28:32], in0=sc[:, 20:24], in1=Gs[:, 12:16], op=ALU.mult), [b_sc, b_Gs], [b_sc])
        fw.op("dve", lambda e: e.tensor_scalar(out=sc[:, 32:36], in0=sc[:, 16:20], scalar1=float(128 ** -0.5), scalar2=None, op0=ALU.mult), [b_sc], [b_sc])
        fw.op("dve", lambda e: e.tensor_tensor(out=sc[:, 36:40], in0=sc[:, 32:36], in1=Gs[:, 8:12], op=ALU.mult), [b_sc, b_Gs], [b_sc])
        kps4, qps4, vps4 = G4b[:, :, 128:256], G4b[:, :, 0:128], G4b[:, :, 256:384]
        fw.op("act", lambda e: e.copy(out=k6a[:, :, 5, :], in_=vps4), bpall, b_k6)
        for i_, (src4, c0_) in enumerate(((kps4, 20), (kps4, 24), (kps4, 28), (qps4, 32), (qps4, 36))):
            fw.op("dve", lambda e, i_=i_, src4=src4, c0_=c0_: e.tensor_tensor(out=k6a[:, :, i_, :], in0=src4, in1=bc(sc[:, c0_:c0_ + 4]), op=ALU.mult), bpall + [b_sc], b_k6)
        for i in range(3):
            fw.op("pool", lambda e, i=i: e.tensor_tensor(out=Ug3a[:, :, i, :], in0=Uf.unsqueeze(1).to_broadcast([128, 4, 128]), in1=bc(g3f[:, i, :]), op=ALU.mult),
                  [b_cst, b_g3f], b_Ug)
        if GDN_STOP <= 1:
            return
        yield
        for h in range(4):
            p2, bp2 = bank(h)
            p2b = p2[:].bitcast(BF16).rearrange("p (k c) -> p k c", k=8)
            for i, src in enumerate((0, 3, 4)):
                fw.op("pe", lambda e, p2b=p2b, i=i, src=src, h=h: e.transpose(out=p2b[:, i, :], in_=k6[h][:, src, :], identity=idb[:]), [b_k6[h], b_idb], [bp2])
        fw.op("act", lambda e: e.copy(out=kqTa[:].rearrange("p h k c -> p h (k c)"), in_=G4b[:, :, 0:384]), bpall, b_kqT)
        if GDN_STOP <= 2:
            return
        yield
        for h in range(4):
            p3, bp3 = bank(h)
            fw.op("pe", lambda e, p3=p3, h=h: e.matmul(p3[:, 0:256], lhsT=kqT[h][:, 0, :], rhs=kqT[h][:, 0:2, :], start=True, stop=True), [b_kqT[h]], [bp3])
            for i in range(3):
                fw.op("pe", lambda e, p3=p3, h=h, i=i: e.matmul(p3[:, 256:384], lhsT=onb[:], rhs=Ug3[h][:, i, :], start=(i == 0), stop=(i == 2), skip_group_check=True), [b_onb, b_Ug[h]], [bp3])
            fw.op("dve", lambda e, p3=p3, h=h: e.scalar_tensor_tensor(out=tD[h][:], in0=p3[:, 256:384], scalar=Gs[:, h:h + 1], in1=mneg, op0=ALU.subtract, op1=ALU.add),
                  [bp3, b_Gs, b_cst], [b_tD[h]])
        fw.op("act", lambda e: e.activation(out=dTa[:], in_=tDa[:], func=AF.Exp), b_tD, b_dT)
        fw.op("pool", lambda e: e.tensor_tensor(out=dSa[:], in0=dTa[:], in1=strf.unsqueeze(1).to_broadcast([128, 4, 128]), op=ALU.mult), b_dT + [b_cst], b_dS)
        for h in range(4):
            p3, bp3 = bank(h)
            fw.op("dve", lambda e, p3=p3, h=h, G_=G_: e.scalar_tensor_tensor(out=Mm[h][0][:], in0=p3[:, 0:128], scalar=G_[:, 8 + h:9 + h], in1=dS[h][:], op0=ALU.mult, op1=ALU.mult),
                  [bp3, b_gb[tl], b_dS[h]], [b_Mm[h][0]])
        fw.op("dve", lambda e: e.tensor_tensor(out=aTa[:], in0=G4[:, :, 128:256], in1=dTa[:], op=ALU.mult), bpall + b_dT, b_aT)
        fw.op("pool", lambda e: e.tensor_tensor(out=Pma[0][:], in0=Mma[0][:], in1=idb[:].unsqueeze(1).to_broadcast([128, 4, 128]), op=ALU.add), allb(b_Mm, 0) + [b_idb], allb(b_Pm, 0))
        if GDN_STOP <= 3:
            return
        yield
        for h in range(4):
            p4, bp4 = bank(h)
            p4b = p4[:].bitcast(BF16)
            fw.op("pe", lambda e, p4b=p4b, h=h: e.transpose(out=p4b[:, 0:128], in_=Mm[h][0][:], identity=idb[:]), [b_Mm[h][0], b_idb], [bp4])
        fw.op("act", lambda e: e.copy(out=MmTa[0][:], in_=G4b[:, :, 0:128]), bpall, allb(b_MmT, 0))
        if GDN_STOP <= 4:
            return
        for lvl in range(NEU_LEVELS):
            a, b = (lvl % 2, (lvl + 1) % 2) if NEU_PINGPONG else (lvl, lvl + 1)
            yield
            for h in range(4):
                p5, bp5 = bank(h)
                if lvl < NEU_LEVELS - 1:
                    fw.op("pe", lambda e, p5=p5, h=h, a=a: e.matmul(p5[:, 0:128], lhsT=MmT[h][a][:], rhs=Mm[h][a][:], start=True, stop=True), [b_MmT[h][a], b_Mm[h][a]], [bp5])
                fw.op("pe", lambda e, p5=p5, h=h, a=a: e.matmul(p5[:, 128:256], lhsT=Mm[h][a][:], rhs=MmT[h][a][:], start=True, stop=True), [b_MmT[h][a], b_Mm[h][a]], [bp5])
            ev = "act"
            if lvl < NEU_LEVELS - 1:
                if ev == "act":
                    fw.op("act", lambda e, b=b: e.copy(out=Mma[b][:], in_=G4[:, :, 0:128]), bpall, allb(b_Mm, b))
                else:
                    fw.op("dve", lambda e, b=b: e.tensor_copy(out=Mma[b][:], in_=G4[:, :, 0:128]), bpall, allb(b_Mm, b))
            if ev == "act":
                fw.op("act", lambda e, b=b: e.copy(out=MmTa[b][:], in_=G4[:, :, 128:256]), bpall, allb(b_MmT, b))
            else:
                fw.op("dve", lambda e, b=b: e.tensor_copy(out=MmTa[b][:], in_=G4[:, :, 128:256]), bpall, allb(b_MmT, b))
            yield
            for h in range(4):
                p6, bp6 = bank(h)
                fw.op("pe", lambda e, p6=p6, h=h, a=a, b=b: e.matmul(p6[:, 0:128], lhsT=MmT[h][b][:], rhs=Pm[h][a][:], start=True, stop=True), [b_MmT[h][b], b_Pm[h][a]], [bp6])
            fw.op("dve", lambda e, a=a, b=b: e.tensor_tensor(out=Pma[b][:], in0=G4[:, :, 0:128], in1=Pma[a][:], op=ALU.add), bpall + allb(b_Pm, a), allb(b_Pm, b))
        PF = (NEU_LEVELS % 2) if NEU_PINGPONG else NEU_LEVELS
        if GDN_STOP <= 5:
            return
        yield
        for h in range(4):
            p7, bp7 = bank(h)
            fw.op("pe", lambda e, p7=p7, h=h: e.matmul(p7[:, 0:128], lhsT=Pm[h][PF][:], rhs=k6[h][:, 5, :], start=True, stop=True), [b_Pm[h][PF], b_k6[h]], [bp7])
            fw.op("pe", lambda e, p7=p7, h=h: e.matmul(p7[:, 128:256], lhsT=k6[h][:, 1, :], rhs=Pm[h][PF][:], start=True, stop=True), [b_Pm[h][PF], b_k6[h]], [bp7])
        fw.op("dve", lambda e, G_=G_: e.tensor_tensor(out=Uba[:], in0=G4[:, :, 0:128], in1=bc(G_[:, 4:8]), op=ALU.mult), bpall + [b_gb[tl]], b_Ub)
        fw.op("dve", lambda e: e.tensor_copy(out=WTa[:], in_=G4[:, :, 128:256]), bpall, b_WT)
        yield
        while t > 0 and (t - 1) not in s8_done:
            yield
        for h in range(4):
            p8, bp8 = bank(h)
            fw.op("pe", lambda e, p8=p8, h=h: e.matmul(p8[:, 0:128], lhsT=WT[h][:], rhs=Sb[:, h, :], start=True, stop=True), [b_WT[h], b_Sb[h]], [bp8])
            fw.op("dve", lambda e, p8=p8, h=h, G_=G_: e.scalar_tensor_tensor(out=vn[h][:], in0=p8[:, 0:128], scalar=G_[:, 8 + h:9 + h], in1=Ub[h][:], op0=ALU.mult, op1=ALU.add),
                  [bp8, b_gb[tl], b_Ub[h]], [b_vn[h]])
        yield
        B9 = [bank(h) for h in range(4)]
        for h in range(4):
            p9, bp9 = B9[h]
            fw.op("pe", lambda e, p9=p9, h=h: e.matmul(p9[:, 0:128], lhsT=kqT[h][:, 2, :], rhs=Sb[:, h, :], start=True, stop=False), [b_kqT[h], b_Sb[h]], [bp9])
            fw.op("pe", lambda e, p9=p9, h=h: e.matmul(p9[:, 0:128], lhsT=aT[h][:], rhs=vn[h][:], start=False, stop=True), [b_aT[h], b_vn[h]], [bp9])
            fw.op("pe", lambda e, p9=p9, h=h: e.matmul(p9[:, 128:256], lhsT=k6[h][:, 2, :], rhs=vn[h][:], start=True, stop=True), [b_k6[h], b_vn[h]], [bp9])
            fw.op("act", lambda e, p9=p9, h=h: e.activation(out=junk[:, 0:128], in_=p9[:, 0:128], func=AF.Square, accum_out=so[:, h:h + 1]), [bp9], [b_so])
            fw.op("dve", lambda e, p9=p9, h=h: e.scalar_tensor_tensor(out=Sf[:, h, :], in0=Sf[:, h, :], scalar=Gs[:, 16 + h:17 + h], in1=p9[:, 128:256], op0=ALU.mult, op1=ALU.add),
                  [bp9, b_Gs, b_Sf[h]], [b_Sf[h]])
        fw.op("pool", lambda e: e.tensor_copy(out=Sb[:], in_=Sf[:]), b_Sf, b_Sb)
        fw.op("act", lambda e: e.activation(out=so[:, 4:8], in_=so[:, 0:4], func=AF.Ln, scale=1.0 / 128, bias=EPS), [b_so], [b_so])
        fw.op("act", lambda e: e.activation(out=so[:, 8:12], in_=so[:, 4:8], func=AF.Exp, scale=-0.5), [b_so], [b_so])
        for h in range(4):
            p9, bp9 = B9[h]
            fw.op("dve", lambda e, p9=p9, h=h, t=t, tl=tl: e.scalar_tensor_tensor(out=MIX[:, t, 512 + h * 128:512 + (h + 1) * 128], in0=p9[:, 0:128], scalar=so[:, 8 + h:9 + h],
                                                                               in1=zsg[:, tl, h * 128:(h + 1) * 128], op0=ALU.mult, op1=ALU.mult),
                  [bp9, b_so, b_zsg[tl]], [b_mix[t]])
        s8_done.add(t)


    for sti in range(4):
        if sti * 4 > GDN_MAXTILE:
            break
        def m2_proj_tile(tl, sti=sti):
            t = sti * 4 + tl
            r = t % 2
            fw.dma(xt[r][:], x[s, t * 128:(t + 1) * 128, :], writes=[b_xt[r]])
            fw.op("act", lambda e, r=r: e.activation(out=xn[r][:], in_=xt[r][:], func=AF.Square, accum_out=st8[r][:, 0:1]), [b_xt[r]], [b_st8[r], b_xn[r]])
            yield
            rstd_ops(None, st8[r][:, 0:1], st8[r][:, 1:2], 1.0 / DM, [b_st8[r]], [b_st8[r]], st8[r][:, 2:3], b_st8[r])
            fw.op("dve", lambda e, r=r: e.tensor_scalar(out=xn[r][:], in0=xt[r][:], scalar1=st8[r][:, 1:2], scalar2=None, op0=ALU.mult),
                  [b_xt[r], b_st8[r]], [b_xn[r]])
            yield
            pt_, bpt_ = ps.rot()
            ptb = pt_[:].bitcast(BF16).rearrange("p (k c) -> p k c", k=8)
            for k in range(8):
                fw.op("pe", lambda e, k=k, r=r, ptb=ptb: e.transpose(out=ptb[:, k, :], in_=xn[r][:, k * 128:(k + 1) * 128], identity=idb[:]),
                      [b_xn[r], b_idb], [bpt_])
            fw.op("dve", lambda e, tl=tl, ptb=ptb: e.tensor_copy(out=xnT[:, :, tl * 128:(tl + 1) * 128], in_=ptb), [bpt_], [b_xnT[tl]])
            yield
            need_weights()
            pa, bpa = ps.rot()
            for k in range(8):
                fw.op("pe", lambda e, k=k, tl=tl, pa=pa: e.matmul(pa[:, 0:8], lhsT=xnT[:, k, tl * 128:(tl + 1) * 128], rhs=w2[:, k, O_AB:O_AB + 8],
                                                                 start=(k == 0), stop=(k == 7)), [b_xnT[tl], b_w2[k]], [bpa])
            fw.op("act", lambda e, tl=tl, pa=pa: e.copy(out=ab[:, tl, :], in_=pa[:, 0:8]), [bpa], [b_ab[tl]])
            yield
            pz, bpz = ps.rot()
            for k in range(8):
                fw.op("pe", lambda e, k=k, tl=tl, pz=pz: e.matmul(pz[:, 0:512], lhsT=xnT[:, k, tl * 128:(tl + 1) * 128], rhs=w2[:, k, O_Z:O_Z + 512],
                                                                 start=(k == 0), stop=(k == 7)), [b_xnT[tl], b_w2[k]], [bpz])
            fw.op("act", lambda e, tl=tl, pz=pz: e.activation(out=zsg[:, tl, :], in_=pz[:, 0:512], func=AF.Silu), [bpz], [b_zsg[tl]])
            z3 = zsg[:, tl, :].rearrange("p (h d) -> p h d", h=4)
            fw.op("pool", lambda e, z3=z3: e.tensor_tensor(out=z3, in0=z3, in1=bct[:, B_GN:B_GN + 128].unsqueeze(1).to_broadcast([128, 4, 128]), op=ALU.mult),
                  [b_zsg[tl], b_bc], [b_zsg[tl]])
            yield
            G_ = gb[:, tl, :]
            fw.op("dve", lambda e, tl=tl, G_=G_: e.tensor_tensor(out=G_[:, 12:16], in0=ab[:, tl, 0:4], in1=bct[:, B_DT:B_DT + 4], op=ALU.add), [b_ab[tl], b_bc], [b_gb[tl]])
            fw.op("act", lambda e, G_=G_: e.activation(out=G_[:, 12:16], in_=G_[:, 12:16], func=AF.Exp), [b_gb[tl]], [b_gb[tl]])
            fw.op("act", lambda e, G_=G_: e.activation(out=G_[:, 12:16], in_=G_[:, 12:16], func=AF.Ln, bias=1.0), [b_gb[tl]], [b_gb[tl]])
            fw.op("act", lambda e, G_=G_: e.activation(out=G_[:, 8:12], in_=bct[:, B_AL:B_AL + 4], func=AF.Exp), [b_bc], [b_gb[tl]])
            fw.op("dve", lambda e, G_=G_: e.scalar_tensor_tensor(out=G_[:, 0:4], in0=G_[:, 12:16], scalar=-1.0, in1=G_[:, 8:12], op0=ALU.mult, op1=ALU.mult),
                  [b_gb[tl]], [b_gb[tl]])
            fw.op("act", lambda e, tl=tl, G_=G_: e.activation(out=G_[:, 12:16], in_=ab[:, tl, 4:8], func=AF.Exp, scale=-1.0), [b_ab[tl]], [b_gb[tl]])
            fw.op("dve", lambda e, G_=G_: e.tensor_scalar(out=G_[:, 12:16], in0=G_[:, 12:16], scalar1=1.0, scalar2=None, op0=ALU.add), [b_gb[tl]], [b_gb[tl]])
            fw.op("dve", lambda e, G_=G_: e.reciprocal(out=G_[:, 4:8], in_=G_[:, 12:16]), [b_gb[tl]], [b_gb[tl]])
            fw.op("dve", lambda e, G_=G_: e.tensor_scalar(out=G_[:, 8:12], in0=G_[:, 4:8], scalar1=-1.0, scalar2=None, op0=ALU.mult), [b_gb[tl]], [b_gb[tl]])

            yield

        run_rolling([m2_proj_tile(tl) for tl in range(4)], 2, bg=wl[0])
        need_weights()
        for c in range(12):
            pf, bpf = ps.rot()
            for k in range(8):
                fw.op("pe", lambda e, k=k, c=c, pf=pf: e.matmul(pf[:, 0:512], lhsT=w2[:, k, O_GQ + c * 128:O_GQ + (c + 1) * 128], rhs=xnT[:, k, :],
                                                               start=(k == 0), stop=(k == 7)), [b_w2[k]] + b_xnT, [bpf])
            xi = c % 2
            fw.op("pool", lambda e, xi=xi, c=c: e.tensor_copy(out=Xc[xi][:, 0:3], in_=halo[:, c, 0:3]), [b_halo[c]], [b_Xc[xi]])
            fw.op("act", lambda e, xi=xi, pf=pf: e.copy(out=Xc[xi][:, 3:515], in_=pf[:, 0:512]), [bpf], [b_Xc[xi]])
            fw.op("pool", lambda e, xi=xi, c=c: e.tensor_copy(out=halo[:, c, 0:3], in_=Xc[xi][:, 512:515]), [b_Xc[xi]], [b_halo[c]])
            cw = lambda i, c=c: ppt[:, P_CW + c * 4 + i:P_CW + c * 4 + i + 1]
            fw.op("act", lambda e, xi=xi, cw=cw: e.activation(out=yacc[xi][:], in_=Xc[xi][:, 0:512], func=AF.Copy, scale=cw(0)), [b_Xc[xi], b_pp], [b_yacc[xi]])
            for i in (1, 2, 3):
                fw.op("dve", lambda e, xi=xi, cw=cw, i=i: e.scalar_tensor_tensor(out=yacc[xi][:], in0=Xc[xi][:, i:i + 512], scalar=cw(i), in1=yacc[xi][:],
                                                                                op0=ALU.mult, op1=ALU.add), [b_Xc[xi], b_pp, b_yacc[xi]], [b_yacc[xi]])
            if c > 0:
                fw.op("act", lambda e, xj=(c - 1) % 2, cj=c - 1: e.activation(out=Y[:, cj, :], in_=yacc[xj][:], func=AF.Silu), [b_yacc[(c - 1) % 2]], [b_Y[c - 1]])
        fw.op("act", lambda e: e.activation(out=Y[:, 11, :], in_=yacc[11 % 2][:], func=AF.Silu), [b_yacc[11 % 2]], [b_Y[11]])


        gens = [gdn_tile(sti, tl) for tl in range(4)]
        active = []
        nxt = 0
        while nxt < len(gens) or active:
            while len(active) < 2 and nxt < len(gens):
                active.append(gens[nxt]); nxt += 1
            for g in list(active):
                try:
                    next(g)
                except StopIteration:
                    active.remove(g)
                    break


def build_wout(nc, fw, ps, st, sb, s, x, w_out, idb, b_idb, MIX, b_mix, H, b_h):
    wo = sb(st, "wo", [128, 8, DM], BF16); b_wo = fw.bufs_n("wo", 8)
    stg = [(sb(st, f"wstg{i}", [128, 1024], F32), fw.buf(f"wstg{i}")) for i in range(2)]
    wl = [load_cast_gen(fw, st, sb, "wo", w_out, 8, DM, wo, b_wo, None, None, stg)]

    def need_weights():
        drain(wl[0])
        wl[0] = None
    mT = [sb(st, f"mT{i}", [128, 8, 128], BF16) for i in range(2)]; b_mT = fw.bufs_n("mT", 2)
    def wout_tile(t):
        r = t % 2
        fw.dma(H[:, t, :], x[s, t * 128:(t + 1) * 128, :], writes=[b_h[t]])
        pt_, bpt_ = ps.rot()
        ptb = pt_[:].bitcast(BF16).rearrange("p (k c) -> p k c", k=8)
        for k in range(8):
            fw.op("pe", lambda e, k=k, t=t, ptb=ptb: e.transpose(out=ptb[:, k, :], in_=MIX[:, t, k * 128:(k + 1) * 128], identity=idb[:]),
                  [b_mix[t], b_idb], [bpt_])
        fw.op("act", lambda e, r=r, ptb=ptb: e.copy(out=mT[r][:], in_=ptb), [bpt_], [b_mT[r]])
        yield
        need_weights()
        for half in range(2):
            po, bpo = ps.rot()
            for k in range(8):
                fw.op("pe", lambda e, k=k, r=r, po=po, half=half: e.matmul(po[:, 0:512], lhsT=mT[r][:, k, :], rhs=wo[:, k, half * 512:(half + 1) * 512],
                                                                          start=(k == 0), stop=(k == 7)), [b_mT[r], b_wo[k]], [bpo])
            fw.op("dve", lambda e, t=t, po=po, half=half: e.tensor_tensor(out=H[:, t, half * 512:(half + 1) * 512], in0=po[:, 0:512],
                                                                          in1=H[:, t, half * 512:(half + 1) * 512], op=ALU.add),
                  [bpo, b_h[t]], [b_h[t]])
            yield

    run_rolling([wout_tile(t) for t in range(NT)], 2, bg=wl[0])
    need_weights()


def build_mlp(nc, fw, ps, st, sb, s, w_up, w_down, ppt, b_pp, idb, b_idb, HNT, H, b_h, out, rstd_ops):
    NB = 4
    FB = DFF // NB
    junk = sb(st, "junk2", [128, DM], F32); b_junk = fw.buf("junk2")
    hn = [sb(st, f"hn{i}", [128, DM], BF16) for i in range(2)]; b_hn = fw.bufs_n("hn", 2)
    st4 = [sb(st, f"st4{i}", [128, 4], F32) for i in range(2)]; b_st4 = fw.bufs_n("st4", 2)
    b_hnT = fw.bufs_n("hnT", NT)
    wu = [sb(st, f"wu{i}", [128, 8, FB], BF16) for i in range(2)]; b_wu = [fw.bufs_n(f"wu{i}", 8) for i in range(2)]
    wd = [sb(st, f"wd{i}", [128, 8, DM], BF16) for i in range(2)]; b_wd = [fw.bufs_n(f"wd{i}", 8) for i in range(2)]
    stg = [(sb(st, f"mstg{i}", [128, 1024], F32), fw.buf(f"mstg{i}")) for i in range(3)]
    aT = sb(st, "aT", [128, 8, 512], BF16); b_aT = fw.bufs_n("aT", 8)
    rl = [sb(st, f"rl{i}", [128, 512], BF16) for i in range(2)]; b_rl = fw.bufs_n("rl", 2)

    def load_block(nb):
        r = nb % 2
        load_cast(fw, st, sb, "wu", w_up[:, nb * FB:(nb + 1) * FB], 8, FB, wu[r], b_wu[r], ppt[:, P_MN:P_MN + 8], b_pp, stg)
        load_cast(fw, st, sb, "wd", w_down[nb * FB:(nb + 1) * FB, :], 8, DM, wd[r], b_wd[r], None, None, stg)

    def load_block_gen(nb):
        r = nb % 2
        gain = ppt[:, P_MN:P_MN + 8]
        for k in range(8):
            stt, b_st = stg[k % len(stg)]
            fw.dma(stt[:, 0:FB], w_up[k * 128:(k + 1) * 128, nb * FB:(nb + 1) * FB], writes=[b_st])
            fw.op("pool", lambda e, k=k, stt=stt, r=r: e.tensor_scalar(out=wu[r][:, k, :], in0=stt[:, 0:FB], scalar1=gain[:, k:k + 1], scalar2=1.0,
                                                                      op0=ALU.mult, op1=ALU.mult), [b_st, b_pp], [b_wu[r][k]])
            yield
        for k in range(8):
            stt, b_st = stg[k % len(stg)]
            fw.dma(stt[:, 0:DM], w_down[nb * FB + k * 128:nb * FB + (k + 1) * 128, :], writes=[b_st])
            fw.op("pool", lambda e, k=k, stt=stt, r=r: e.tensor_copy(out=wd[r][:, k, :], in_=stt[:, 0:DM]), [b_st], [b_wd[r][k]])
            yield

    def step_loader(g):
        try:
            next(g)
            return g
        except StopIteration:
            return None

    def block0_loader():
        yield from load_cast_gen(fw, st, sb, "wu", w_up[:, 0:FB], 8, FB, wu[0], b_wu[0], ppt[:, P_MN:P_MN + 8], b_pp, stg)
        yield from load_cast_gen(fw, st, sb, "wd", w_down[0:FB, :], 8, DM, wd[0], b_wd[0], None, None, stg)
    wloader = block0_loader()
    def hn_tile(t):
        r = t % 2
        fw.op("act", lambda e, t=t, r=r: e.activation(out=hn[r][:], in_=H[:, t, :], func=AF.Square, accum_out=st4[r][:, 0:1]), [b_h[t]], [b_st4[r], b_hn[r]])
        yield
        rstd_ops(None, st4[r][:, 0:1], st4[r][:, 1:2], 1.0 / DM, [b_st4[r]], [b_st4[r]], st4[r][:, 2:3], b_st4[r])
        yield
        fw.op("dve", lambda e, t=t, r=r: e.tensor_scalar(out=hn[r][:], in0=H[:, t, :], scalar1=st4[r][:, 1:2], scalar2=None, op0=ALU.mult),
              [b_h[t], b_st4[r]], [b_hn[r]])
        yield
        pt_, bpt_ = ps.rot()
        ptb = pt_[:].bitcast(BF16).rearrange("p (k c) -> p k c", k=8)
        for k in range(8):
            fw.op("pe", lambda e, k=k, r=r, ptb=ptb: e.transpose(out=ptb[:, k, :], in_=hn[r][:, k * 128:(k + 1) * 128], identity=idb[:]), [b_hn[r], b_idb], [bpt_])
        fw.op("act", lambda e, t=t, ptb=ptb: e.copy(out=HNT[:, :, t * 128:(t + 1) * 128], in_=ptb), [bpt_], [b_hnT[t]])
        hn_done.add(t)
        yield

    def rolling_gen(gens, width):
        gens = list(gens)
        active = []
        nxt = 0
        while nxt < len(gens) or active:
            while len(active) < width and nxt < len(gens):
                active.append(gens[nxt]); nxt += 1
            for g_ in list(active):
                try:
                    next(g_)
                except StopIteration:
                    active.remove(g_)
                    break
            yield

    hn_done = set()
    wloader = run_rolling([hn_tile(t) for t in range(4)], 2, bg=wloader)
    drain(wloader)
    hn_bg = rolling_gen([hn_tile(t) for t in range(4, NT)], 2)

    def step_hn():
        nonlocal hn_bg
        if hn_bg is not None:
            try:
                next(hn_bg)
            except StopIteration:
                hn_bg = None
    rc = 0
    for nb in range(NB):
        r = nb % 2
        loader = load_block_gen(nb + 1) if nb + 1 < NB else None
        for g in range(4):
            while hn_bg is not None and not all(t_ in hn_done for t_ in range(g * 4, g * 4 + 4)):
                step_hn()
            for c in range(8):
                pu, bpu = ps.rot()
                for k in range(8):
                    fw.op("pe", lambda e, k=k, c=c, r=r, g=g, pu=pu: e.matmul(pu[:, 0:512], lhsT=wu[r][:, k, c * 128:(c + 1) * 128],
                                                                             rhs=HNT[:, k, g * 512:(g + 1) * 512], start=(k == 0), stop=(k == 7)),
                          [b_wu[r][k]] + b_hnT[g * 4:(g + 1) * 4], [bpu])
                ri = rc % 2
                rc += 1
                fw.op("act", lambda e, ri=ri, pu=pu: e.activation(out=rl[ri][:], in_=pu[:, 0:512], func=AF.Relu), [bpu], [b_rl[ri]])
                fw.op("act", lambda e, ri=ri, c=c: e.activation(out=aT[:, c, :], in_=rl[ri][:], func=AF.Square), [b_rl[ri]], [b_aT[c]])
                if loader is not None and c % 2 == 1:
                    loader = step_loader(loader)
                if nb == 0:
                    step_hn()
            for tl in range(4):
                t = g * 4 + tl
                for half in range(2):
                    po, bpo = ps.acc(tl % 2 * 2 + half)
                    for c in range(8):
                        fw.op("pe", lambda e, c=c, tl=tl, r=r, half=half, po=po: e.matmul(po[:, 0:512], lhsT=aT[:, c, tl * 128:(tl + 1) * 128],
                                                                                         rhs=wd[r][:, c, half * 512:(half + 1) * 512],
                                                                                         start=(c == 0), stop=(c == 7)), [b_aT[c], b_wd[r][c]], [bpo])
                    fw.op("dve", lambda e, t=t, po=po, half=half: e.tensor_tensor(out=H[:, t, half * 512:(half + 1) * 512], in0=po[:, 0:512],
                                                                                  in1=H[:, t, half * 512:(half + 1) * 512], op=ALU.add),
                          [bpo, b_h[t]], [b_h[t]])
                if nb == NB - 1:
                    fw.dma(out[s, t * 128:(t + 1) * 128, :], H[:, t, :], reads=[b_h[t]])
        while loader is not None:
            loader = step_loader(loader)


def host_prepare(inputs):
    f = lambda a: np.ascontiguousarray(np.asarray(a), dtype=np.float32)
    w_in = f(inputs["w_in"])[0]
    o = np.cumsum([0, 256, 256, 64, 512, 512, 512, 512, 4, 4])
    perm = np.concatenate([np.arange(o[0], o[3]), np.arange(o[7], o[9]), np.arange(o[6], o[7]), np.arange(o[3], o[6])])
    w_in_p = np.ascontiguousarray(w_in[:, perm])
    pp = np.zeros((128, NPP), np.float32)
    pp[:, P_AN:P_AN + 8] = f(inputs["attn_norm_w"])[0].reshape(8, 128).T
    pp[:, P_QLN:P_QLN + 2] = f(inputs["q_lat_norm_w"])[0].reshape(2, 128).T
    pp[:, P_KVLN:P_KVLN + 2] = f(inputs["kv_lat_norm_w"])[0].reshape(2, 128).T
    pp[:, P_MN:P_MN + 8] = f(inputs["mlp_norm_w"])[0].reshape(8, 128).T
    cw = f(inputs["conv_w"])[0]
    pp[:, P_CW:P_CW + 48] = cw.reshape(4, 12, 128).transpose(2, 1, 0).reshape(128, 48)
    bc = np.zeros((128, NBC), np.float32)
    bc[:, B_QN:B_QN + 192] = f(inputs["q_norm_w"])[0][None, :]
    bc[:, B_KN:B_KN + 192] = f(inputs["k_norm_w"])[0][None, :]
    bc[:, B_MO:B_MO + 512] = f(inputs["mla_out_norm_w"])[0].reshape(-1)[None, :]
    bc[:, B_GN:B_GN + 128] = f(inputs["gdn_norm_w"])[0][None, :]
    bc[:, B_AL:B_AL + 4] = f(inputs["a_log"])[0][None, :]
    bc[:, B_DT:B_DT + 4] = f(inputs["dt_bias"])[0][None, :]
    cst = np.zeros((128, NK), np.float32)
    i = np.arange(128)
    cst[:, K_ID:K_ID + 128] = np.eye(128)
    cst[:, K_U:K_U + 128] = (i[:, None] <= i[None, :])
    cst[:, K_MNEG:K_MNEG + 128] = np.where(i[None, :] >= i[:, None], 0.0, -30000.0)
    cst[:, K_STR:K_STR + 128] = (i[None, :] > i[:, None])
    cst[:, K_INC:K_INC + 128] = (i[None, :] >= i[:, None])
    cst[:, K_ONE:K_ONE + 128] = 1.0
    half = 32
    inv_freq = (10000.0 ** (-(np.arange(half, dtype=np.float32) / np.float32(half)))).astype(np.float32)
    cst[:, K_IF:K_IF + 32] = inv_freq[None, :]
    x = f(inputs["x"])
    pos = np.asarray(inputs["positions"]).astype(np.int32)
    shared = {
        "w_in": w_in_p, "w_uq": f(inputs["w_uq"])[0], "w_ukv": f(inputs["w_ukv"])[0], "w_out": f(inputs["w_out"])[0],
        "w_up": f(inputs["w_up"])[0], "w_down": f(inputs["w_down"])[0], "pp": pp, "bc": bc, "cst": cst,
    }
    in_maps = []
    for c in range(NCORES):
        m = dict(shared)
        m["x"] = np.ascontiguousarray(x[2 * c:2 * c + 2])
        m["pos"] = np.ascontiguousarray(pos[2 * c:2 * c + 2].reshape(2, NT, 128).transpose(0, 2, 1))
        in_maps.append(m)
    return in_maps


_NC_CACHE = {}


def kernel(**inputs):
    in_maps = host_prepare(inputs)
    if "nc" not in _NC_CACHE:
        _NC_CACHE["nc"] = build_program()
    nc = _NC_CACHE["nc"]
    res = run_bass_kernel_spmd(nc, in_maps, core_ids=list(range(NCORES)))
    outs = [res.results[c]["out"] for c in range(NCORES)]
    return np.concatenate(outs, axis=0).astype(np.float32)
```

```python
import numpy as np
from contextlib import ExitStack
import concourse.bass as bass
import concourse.mybir as mybir
from concourse.bass_utils import run_bass_kernel_spmd

F32 = mybir.dt.float32
BF16 = mybir.dt.bfloat16
I32 = mybir.dt.int32
AF = mybir.ActivationFunctionType
ALU = mybir.AluOpType
AX = mybir.AxisListType

NCORES = 8
GDN_STOP = 99
NEU_LEVELS = 6
GDN_MAXTILE = 99
M1_BG = False
NEU_PINGPONG = True
SEQ = 2048
DM = 1024
NT = 16
DFF = 4096
EPS = 1e-6
PI = float(np.pi)
C_QL, C_KVL, C_KPE, C_A, C_B, C_Z, C_G = 0, 256, 512, 576, 580, 584, 1096
NW1 = 576
NW2 = 2632 - 576
B_QN, B_KN, B_MO, B_GN, B_AL, B_DT, NBC = 0, 192, 384, 896, 1024, 1028, 1032
P_AN, P_QLN, P_KVLN, P_MN, P_CW, NPP = 0, 8, 10, 12, 20, 68
K_ID, K_U, K_MNEG, K_STR, K_INC, K_ONE, K_IF, NK = 0, 128, 256, 384, 512, 640, 768, 800


class Buf:
    __slots__ = ("name", "last_w", "readers", "psum")

    def __init__(self, name):
        self.name = name
        self.last_w = None
        self.readers = []
        self.psum = False


class Op:
    __slots__ = ("eng", "fn", "deps", "signal", "dma", "tok")


class Chan:
    __slots__ = ("sem", "count", "last")


class Fw:
    ENG = ("pe", "act", "dve", "pool", "sp")

    def __init__(self, nc, stack, nchan=24):
        self.nc = nc
        self.ops = {e: [] for e in self.ENG}
        self.esem = {e: stack.enter_context(nc.semaphore("s_" + e)) for e in self.ENG}
        self.ecnt = {e: 0 for e in self.ENG}
        self.known = {e: {} for e in self.ENG}
        self.chans = []
        for i in range(nchan):
            c = Chan()
            c.sem = stack.enter_context(nc.semaphore(f"dch{i}"))
            c.count = 0
            c.last = None
            self.chans.append(c)
        self.nextchan = 0
        self.autoflush = True
        self.bufs = []
        self.pass_dmas = []
        self.ninst = 0
        self.nwait = 0

    def buf(self, name="b"):
        b = Buf(name)
        self.bufs.append(b)
        return b

    def bufs_n(self, name, n):
        return [self.buf(f"{name}{i}") for i in range(n)]

    def op(self, eng, fn, reads=(), writes=(), dma=False, extra=()):
        if self.autoflush and len(self.ops[eng]) >= 900:
            self.flush()
        o = Op()
        o.eng = eng; o.fn = fn; o.signal = False; o.dma = dma; o.tok = None
        deps = list(extra)
        for b in reads:
            if b.last_w is not None:
                deps.append(b.last_w)
            if b.psum:
                deps.extend(r for r in b.readers if r.eng != eng)
        for b in writes:
            if b.last_w is not None:
                deps.append(b.last_w)
            deps.extend(b.readers)
        dd = []
        seen = set()
        for d in deps:
            if id(d) in seen or d is None:
                continue
            seen.add(id(d))
            if eng == "pe" and d.eng == "pe" and not d.dma and not dma:
                continue
            dd.append(d)
        o.deps = dd
        for b in reads:
            b.readers.append(o)
        for b in writes:
            b.last_w = o
            b.readers = []
        self.ops[eng].append(o)
        return o

    def maybe_flush(self, limit=900):
        if max(len(v) for v in self.ops.values()) >= limit:
            self.flush()

    def flush(self):
        for b in self.bufs:
            if b.last_w is not None and not b.last_w.dma:
                b.last_w.signal = True
            for r in b.readers:
                if not r.dma:
                    r.signal = True
        self._emit()

    def dma(self, out, in_, reads=(), writes=(), eng="sp"):
        ch = self.chans[self.nextchan]
        self.nextchan = (self.nextchan + 1) % len(self.chans)
        ch.count += 16
        o = self.op(eng, lambda e: e.dma_start(out=out, in_=in_), reads=reads, writes=writes, dma=True,
                    extra=[ch.last])
        o.tok = (ch.sem, ch.count)
        ch.last = o
        self.pass_dmas.append(o)
        return o

    def end_pass(self):
        self.autoflush = False
        lasts = {}
        for e in self.ENG:
            for o in reversed(self.ops[e]):
                if not o.dma:
                    lasts[e] = o
                    break
        dma_last = [c.last for c in self.chans if c.last is not None]
        for f in self.ENG:
            extra = [lasts[e] for e in lasts if e != f] + dma_last
            self.op(f, lambda e: e.nop(), extra=extra)
        self._emit()
        self.autoflush = True
        for b in self.bufs:
            b.last_w = None
            b.readers = []
        for c in self.chans:
            c.last = None
        self.pass_dmas = []

    def _emit(self):
        for e in self.ENG:
            for o in self.ops[e]:
                for d in o.deps:
                    if not d.dma:
                        d.signal = True
        for e in self.ENG:
            for o in self.ops[e]:
                if (not o.dma) and o.signal and o.tok is None:
                    self.ecnt[e] += 1
                    o.tok = (self.esem[e], self.ecnt[e])
        fw = self
        with self.nc.Block() as block:
            def run(ename):
                def body(eng):
                    known = fw.known[ename]
                    for o in fw.ops[ename]:
                        need = {}
                        for d in o.deps:
                            assert d.tok is not None, f"dep without token on {ename}"
                            sem, val = d.tok
                            k = id(sem)
                            if known.get(k, 0) >= val:
                                continue
                            if k not in need or need[k][1] < val:
                                need[k] = (sem, val)
                        for k, (sem, val) in need.items():
                            eng.wait_ge(sem, val)
                            known[k] = val
                            fw.nwait += 1
                        ins = o.fn(eng)
                        fw.ninst += 1
                        if o.dma:
                            ins.then_inc(o.tok[0], 16)
                        elif o.signal:
                            ins.then_inc(o.tok[0], 1)
                return body
            block.tensor(run("pe"))
            block.scalar(run("act"))
            block.vector(run("dve"))
            block.gpsimd(run("pool"))
            block.sync(run("sp"))
        for e in self.ENG:
            self.ops[e] = []


class PS:
    def __init__(self, nc, fw, stack):
        self.big = stack.enter_context(nc.psum_tensor("psbig", [128, 8, 512], F32))
        self.t = [self.big[:, i, :] for i in range(8)]
        self.b = [fw.buf(f"psb{i}") for i in range(8)]
        for b in self.b:
            b.psum = True
        self.rb = 0

    def rot(self):
        i = 4 + self.rb
        self.rb = (self.rb + 1) % 4
        return self.t[i], self.b[i]

    def acc(self, i):
        return self.t[i], self.b[i]


def build_program(stages=("m1", "m2", "wout", "mlp"), nseq=2, dbg=False):
    nc = bass.Bass("TRN2", target_bir_lowering=False)

    def din(name, shape, dt=F32):
        return nc.dram_tensor(name, list(shape), dt, kind="ExternalInput").ap()
    x = din("x", [2, SEQ, DM])
    pos = din("pos", [2, 128, NT], I32)
    w_in = din("w_in", [DM, 2632])
    w_uq = din("w_uq", [256, 768])
    w_ukv = din("w_ukv", [256, 1024])
    w_out = din("w_out", [DM, DM])
    w_up = din("w_up", [DM, DFF])
    w_down = din("w_down", [DFF, DM])
    pp_d = din("pp", [128, NPP])
    bc_d = din("bc", [128, NBC])
    cst_d = din("cst", [128, NK])
    out = nc.dram_tensor("out", [2, SEQ, DM], F32, kind="ExternalOutput").ap()
    dbg_mix = None
    if dbg:
        dbg_mix = nc.dram_tensor("dbg_mix", [2, 128, NT * DM], BF16, kind="ExternalOutput").ap()

    with ExitStack() as gs:
        fw = Fw(nc, gs)
        ps = PS(nc, fw, gs)

        cnt = [0]

        def sb(st, name, shape, dt):
            cnt[0] += 1
            return st.enter_context(nc.sbuf_tensor(f"sb{cnt[0]}_{name}", list(shape), dt))

        cst = sb(gs, "cst", [128, NK], F32); b_cst = fw.buf("cst")
        ppt = sb(gs, "ppt", [128, NPP], F32); b_pp = fw.buf("pp")
        bct = sb(gs, "bct", [128, NBC], F32); b_bc = fw.buf("bc")
        idb = sb(gs, "idb", [128, 128], BF16); b_idb = fw.buf("idb")
        incb = sb(gs, "incb", [128, 128], BF16); b_incb = fw.buf("incb")
        fw.dma(cst[:], cst_d, writes=[b_cst])
        fw.dma(ppt[:], pp_d, writes=[b_pp])
        fw.dma(bct[:], bc_d, writes=[b_bc])
        fw.op("dve", lambda e: e.tensor_copy(out=idb[:], in_=cst[:, K_ID:K_ID + 128]), [b_cst], [b_idb])
        fw.op("dve", lambda e: e.tensor_copy(out=incb[:], in_=cst[:, K_INC:K_INC + 128]), [b_cst], [b_incb])
        fw.op("dve", lambda e: e.tensor_scalar(out=bct[:, B_QN:B_QN + 192], in0=bct[:, B_QN:B_QN + 192],
                                               scalar1=float(192 ** -0.5), scalar2=None, op0=ALU.mult), [b_bc], [b_bc])
        idf = cst[:, K_ID:K_ID + 128]
        fw.end_pass()

        def rstd_ops(eng_unused, ssq_ap, out_ap, scale, reads, writes, tmp_ap, b_tmp):
            fw.op("act", lambda e: e.activation(out=tmp_ap, in_=ssq_ap, func=AF.Ln, scale=scale, bias=EPS), reads, [b_tmp])
            fw.op("act", lambda e: e.activation(out=out_ap, in_=tmp_ap, func=AF.Exp, scale=-0.5), [b_tmp], writes)

        for s in range(nseq):
            with ExitStack() as ss:
                MIXR = sb(ss, "mixr", [128, NT * DM], BF16)
                MIX = MIXR[:].rearrange("p (t f) -> p t f", t=NT)
                HNT = MIXR[:].rearrange("p (k t) -> p k t", k=8)
                b_mix = fw.bufs_n("mix", NT)
                cs = sb(ss, "cs", [128, NT, 64], F32); b_cs = fw.buf("cs")

                if "m2" not in stages:
                    fw.op("pool", lambda e: e.memset(MIX[:, :, 512:1024], 0.0), [], b_mix)
                if "m1" in stages:
                    with ExitStack() as p1:
                        build_m1(nc, fw, ps, p1, sb, s, x, pos, w_in, w_uq, w_ukv, cst, ppt, bct, idb, incb,
                                 b_cst, b_pp, b_bc, b_idb, b_incb, MIX, b_mix, cs, b_cs, rstd_ops)
                        fw.end_pass()
                else:
                    fw.op("pool", lambda e: e.memset(MIXR[:], 0.0), [], b_mix)
                    fw.end_pass()
                if "m2" in stages:
                    with ExitStack() as p2:
                        build_m2(nc, fw, ps, p2, sb, s, x, w_in, cst, ppt, bct, idb, b_cst, b_pp, b_bc, b_idb, MIX, b_mix, rstd_ops)
                        fw.end_pass()
                if dbg:
                    fw.dma(dbg_mix[s], MIXR[:], reads=b_mix)
                    fw.end_pass()
                with ExitStack() as hs:
                    H = sb(hs, "H", [128, NT, DM], F32)
                    b_h = fw.bufs_n("h", NT)
                    if "wout" in stages:
                        with ExitStack() as p3:
                            build_wout(nc, fw, ps, p3, sb, s, x, w_out, idb, b_idb, MIX, b_mix, H, b_h)
                            fw.end_pass()
                    if "mlp" in stages:
                        with ExitStack() as p4:
                            build_mlp(nc, fw, ps, p4, sb, s, w_up, w_down, ppt, b_pp, idb, b_idb, HNT, H, b_h, out, rstd_ops)
                            fw.end_pass()
                    else:
                        for t in range(NT):
                            fw.dma(out[s, t * 128:(t + 1) * 128, :], H[:, t, :], reads=[b_h[t]])
                        fw.end_pass()
        print("instructions", fw.ninst, "waits", fw.nwait)
    return nc


def load_cast_gen(fw, st, sb, name, w_ap, nk, ncols, dst, b_dst, gain=None, b_gain=None, stg=None, engs=("act", "dve")):
    sw = min(int(stg[0][0].shape[1]), ncols)
    n = 0
    for k in range(nk):
        for c0 in range(0, ncols, sw):
            c1 = min(ncols, c0 + sw)
            stt, b_st = stg[n % len(stg)]
            eng = engs[n % len(engs)]
            n += 1
            fw.dma(stt[:, 0:c1 - c0], w_ap[k * 128:(k + 1) * 128, c0:c1], writes=[b_st])
            rd = [b_st] + ([b_gain] if gain is not None else [])
            if eng == "act":
                if gain is not None:
                    fw.op("act", lambda e, k=k, stt=stt, c0=c0, c1=c1: e.activation(out=dst[:, k, c0:c1], in_=stt[:, 0:c1 - c0], func=AF.Copy, scale=gain[:, k:k + 1]),
                          rd, [b_dst[k]])
                else:
                    fw.op("act", lambda e, k=k, stt=stt, c0=c0, c1=c1: e.activation(out=dst[:, k, c0:c1], in_=stt[:, 0:c1 - c0], func=AF.Copy), rd, [b_dst[k]])
            elif eng == "dve":
                if gain is not None:
                    fw.op("dve", lambda e, k=k, stt=stt, c0=c0, c1=c1: e.tensor_scalar(out=dst[:, k, c0:c1], in0=stt[:, 0:c1 - c0], scalar1=gain[:, k:k + 1], scalar2=None, op0=ALU.mult),
                          rd, [b_dst[k]])
                else:
                    fw.op("dve", lambda e, k=k, stt=stt, c0=c0, c1=c1: e.tensor_copy(out=dst[:, k, c0:c1], in_=stt[:, 0:c1 - c0]), rd, [b_dst[k]])
            else:
                if gain is not None:
                    fw.op("pool", lambda e, k=k, stt=stt, c0=c0, c1=c1: e.tensor_scalar(out=dst[:, k, c0:c1], in0=stt[:, 0:c1 - c0], scalar1=gain[:, k:k + 1], scalar2=1.0,
                                                                                     op0=ALU.mult, op1=ALU.mult), rd, [b_dst[k]])
                else:
                    fw.op("pool", lambda e, k=k, stt=stt, c0=c0, c1=c1: e.tensor_copy(out=dst[:, k, c0:c1], in_=stt[:, 0:c1 - c0]), rd, [b_dst[k]])
            yield


def load_cast(*a, **kw):
    for _ in load_cast_gen(*a, **kw):
        pass


def drain(g):
    if g is not None:
        for _ in g:
            pass


def run_rolling(gens, width=2, bg=None):
    gens = list(gens)
    active = []
    nxt = 0
    while nxt < len(gens) or active:
        while len(active) < width and nxt < len(gens):
            active.append(gens[nxt]); nxt += 1
        for g in list(active):
            try:
                next(g)
            except StopIteration:
                active.remove(g)
                break
        if bg is not None:
            try:
                next(bg)
            except StopIteration:
                bg = None
    return bg


def build_m1(nc, fw, ps, st, sb, s, x, pos, w_in, w_uq, w_ukv, cst, ppt, bct, idb, incb,
             b_cst, b_pp, b_bc, b_idb, b_incb, MIX, b_mix, cs, b_cs, rstd_ops):
    w1 = sb(st, "w1", [128, 8, NW1], BF16); b_w1 = fw.bufs_n("w1", 8)
    wuq = sb(st, "wuq", [128, 2, 768], BF16); b_wuq = fw.bufs_n("wuq", 2)
    wukv = sb(st, "wukv", [128, 2, 1024], BF16); b_wukv = fw.bufs_n("wukv", 2)
    stg = [(sb(st, f"stg{i}", [128, 1024], F32), fw.buf(f"stg{i}")) for i in range(2)]
    def m1_loader():
        yield from load_cast_gen(fw, st, sb, "w1", w_in[:, 0:NW1], 8, NW1, w1, b_w1, ppt[:, P_AN:P_AN + 8], b_pp, stg)
        yield from load_cast_gen(fw, st, sb, "wuq", w_uq, 2, 768, wuq, b_wuq, ppt[:, P_QLN:P_QLN + 2], b_pp, stg)
        yield from load_cast_gen(fw, st, sb, "wukv", w_ukv, 2, 1024, wukv, b_wukv, ppt[:, P_KVLN:P_KVLN + 2], b_pp, stg)
    wl = [m1_loader()]

    def need_weights():
        drain(wl[0])
        wl[0] = None

    posi = sb(st, "posi", [128, NT], I32); b_posi = fw.buf("posi")
    ang = sb(st, "ang", [128, NT, 32], F32); b_ang = fw.buf("ang")
    kq = sb(st, "kq", [128, NT, 32], F32); b_kq = fw.buf("kq")
    kqi = sb(st, "kqi", [128, NT, 32], I32); b_kqi = fw.buf("kqi")
    posf = sb(st, "posf", [128, NT], F32); b_posf = fw.buf("posf")
    fw.dma(posi[:], pos[s], writes=[b_posi])
    fw.op("dve", lambda e: e.tensor_copy(out=posf[:], in_=posi[:]), [b_posi], [b_posf])
    invf = cst[:, K_IF:K_IF + 32]
    fw.op("dve", lambda e: e.tensor_tensor(out=ang[:], in0=posf[:].unsqueeze(2).to_broadcast([128, NT, 32]),
                                           in1=invf.unsqueeze(1).to_broadcast([128, NT, 32]), op=ALU.mult),
          [b_posf, b_cst], [b_ang])
    fw.op("dve", lambda e: e.tensor_scalar(out=kq[:], in0=ang[:], scalar1=float(1.0 / (2 * PI)), scalar2=None, op0=ALU.mult), [b_ang], [b_kq])
    fw.op("dve", lambda e: e.tensor_copy(out=kqi[:], in_=kq[:]), [b_kq], [b_kqi])
    fw.op("dve", lambda e: e.tensor_copy(out=kq[:], in_=kqi[:]), [b_kqi], [b_kq])
    C1 = 6.28125
    C2 = float(2 * np.pi - 6.28125)
    fw.op("dve", lambda e: e.scalar_tensor_tensor(out=ang[:], in0=kq[:], scalar=-C1, in1=ang[:], op0=ALU.mult, op1=ALU.add), [b_kq, b_ang], [b_ang])
    fw.op("dve", lambda e: e.scalar_tensor_tensor(out=ang[:], in0=kq[:], scalar=-C2, in1=ang[:], op0=ALU.mult, op1=ALU.add), [b_kq, b_ang], [b_ang])
    fw.op("dve", lambda e: e.tensor_scalar(out=kq[:], in0=ang[:], scalar1=PI, scalar2=None, op0=ALU.is_gt), [b_ang], [b_kq])
    fw.op("dve", lambda e: e.scalar_tensor_tensor(out=ang[:], in0=kq[:], scalar=-2 * PI, in1=ang[:], op0=ALU.mult, op1=ALU.add), [b_kq, b_ang], [b_ang])
    fw.op("dve", lambda e: e.tensor_scalar(out=kq[:], in0=ang[:], scalar1=-PI, scalar2=None, op0=ALU.is_lt), [b_ang], [b_kq])
    fw.op("dve", lambda e: e.scalar_tensor_tensor(out=ang[:], in0=kq[:], scalar=2 * PI, in1=ang[:], op0=ALU.mult, op1=ALU.add), [b_kq, b_ang], [b_ang])
    fw.op("dve", lambda e: e.tensor_scalar(out=ang[:], in0=ang[:], scalar1=PI, scalar2=-PI, op0=ALU.min, op1=ALU.max), [b_ang], [b_ang])
    fw.op("act", lambda e: e.activation(out=cs[:, :, 32:64], in_=ang[:], func=AF.Sin), [b_ang], [b_cs])
    fw.op("act", lambda e: e.activation(out=kq[:], in_=ang[:], func=AF.Abs), [b_ang], [b_kq])
    fw.op("dve", lambda e: e.tensor_scalar(out=kq[:], in0=kq[:], scalar1=-1.0, scalar2=PI / 2, op0=ALU.mult, op1=ALU.add), [b_kq], [b_kq])
    fw.op("act", lambda e: e.activation(out=cs[:, :, 0:32], in_=kq[:], func=AF.Sin), [b_kq], [b_cs])
    cs2 = sb(st, "cs2", [128, NT, 128], F32); b_cs2 = fw.buf("cs2")
    fw.op("pool", lambda e: e.tensor_copy(out=cs2[:, :, 0:32], in_=cs[:, :, 0:32]), [b_cs], [b_cs2])
    fw.op("pool", lambda e: e.tensor_copy(out=cs2[:, :, 32:64], in_=cs[:, :, 0:32]), [b_cs], [b_cs2])
    fw.op("dve", lambda e: e.tensor_scalar(out=cs2[:, :, 64:96], in0=cs[:, :, 32:64], scalar1=-1.0, scalar2=None, op0=ALU.mult), [b_cs], [b_cs2])
    fw.op("pool", lambda e: e.tensor_copy(out=cs2[:, :, 96:128], in_=cs[:, :, 32:64]), [b_cs], [b_cs2])

    KT = sb(st, "KT", [128, 4, SEQ], BF16); b_kt = fw.bufs_n("kt", NT)
    KR = sb(st, "KR", [128, SEQ], BF16); b_kr = fw.bufs_n("kr", NT)
    fw.op("pool", lambda e: e.memset(KR[:], 0.0), [], b_kr)
    V = sb(st, "V", [128, NT, 4, 132], BF16); b_v = fw.bufs_n("v", NT)
    fw.op("pool", lambda e: e.memset(V[:], 1.0), [], b_v)
    xt = [sb(st, f"xt{i}", [128, DM], F32) for i in range(2)]; b_xt = fw.bufs_n("xt", 2)
    junk = sb(st, "junk", [128, DM], F32); b_junk = fw.buf("junk")
    xn = [sb(st, f"xn{i}", [128, DM], BF16) for i in range(2)]; b_xn = fw.bufs_n("xn", 2)
    xnT = [sb(st, f"xnT{i}", [128, 8, 128], BF16) for i in range(2)]; b_xnT = fw.bufs_n("xnT", 2)
    st8 = [sb(st, f"st8{i}", [128, 16], F32) for i in range(2)]; b_st8 = fw.bufs_n("st8", 2)
    tm8 = [sb(st, f"tm8{i}", [128, 16], F32) for i in range(2)]; b_tm8 = fw.bufs_n("tm8", 2)
    latn = [sb(st, f"latn{i}", [128, 512], BF16) for i in range(2)]; b_latn = fw.bufs_n("latn", 2)
    latT = [sb(st, f"latT{i}", [128, 4, 128], BF16) for i in range(2)]; b_latT = fw.bufs_n("latT", 2)
    kpe = [sb(st, f"kpe{i}", [128, 64], F32) for i in range(2)]; b_kpe = fw.bufs_n("kpe", 2)
    rtmp = [sb(st, f"rtmp{i}", [128, 4, 4, 32], F32) for i in range(2)]; b_rtmp = fw.bufs_n("rtmp", 2)
    krb = [sb(st, f"krb{i}", [128, 64], BF16) for i in range(2)]; b_krb = fw.bufs_n("krb", 2)
    qf = [sb(st, f"qf{i}", [128, 768], F32) for i in range(2)]; b_qf = fw.bufs_n("qf", 2)
    sq = [sb(st, f"sq{i}", [128, 1024], F32) for i in range(2)]; b_sq = fw.bufs_n("sq", 2)
    qb = [sb(st, f"qb{i}", [128, 4, 192], BF16) for i in range(2)]; b_qb = fw.bufs_n("qb", 2)
    kvf = [sb(st, f"kvf{i}", [128, 1024], F32) for i in range(2)]; b_kvf = fw.bufs_n("kvf", 2)
    kb = [sb(st, f"kb{i}", [128, 4, 128], BF16) for i in range(2)]; b_kb = fw.bufs_n("kb", 2)
    QT2 = [sb(st, f"QT{i}", [128, 4, 512], BF16) for i in range(2)]; b_qt2 = [fw.bufs_n(f"qt{i}", 4) for i in range(2)]
    QR2 = [sb(st, f"QR{i}", [128, 4, 512], BF16) for i in range(2)]; b_qr2 = [fw.bufs_n(f"qr{i}", 4) for i in range(2)]
    for i_ in range(2):
        fw.op("pool", lambda e, i_=i_: e.memset(QR2[i_][:], 0.0), [], b_qr2[i_])
    PT = [sb(st, f"PT{i}", [128, 512], BF16) for i in range(3)]; b_pt = fw.bufs_n("pt", 3)
    of = [sb(st, f"of{i}", [128, 128], F32) for i in range(2)]; b_of = fw.bufs_n("of", 2)
    ost = [sb(st, f"ost{i}", [128, 4], F32) for i in range(2)]; b_ost = fw.bufs_n("ost", 2)
    ptc = 0
    ofc = 0

    def proj_tile(t, tl, sti):
        nonlocal ptc, ofc
        t = sti * 4 + tl
        r = t % 2
        fw.dma(xt[r][:], x[s, t * 128:(t + 1) * 128, :], writes=[b_xt[r]])
        fw.op("act", lambda e, r=r: e.activation(out=junk[:], in_=xt[r][:], func=AF.Square, accum_out=st8[r][:, 0:1]),
              [b_xt[r]], [b_st8[r]])
        rstd_ops(None, st8[r][:, 0:1], st8[r][:, 1:2], 1.0 / DM, [b_st8[r]], [b_st8[r]], tm8[r][:, 0:1], b_tm8[r])
        fw.op("dve", lambda e, r=r: e.tensor_scalar(out=xn[r][:], in0=xt[r][:], scalar1=st8[r][:, 1:2], scalar2=None, op0=ALU.mult),
              [b_xt[r], b_st8[r]], [b_xn[r]])
        pt_, bpt_ = ps.rot()
        ptb = pt_[:].bitcast(BF16).rearrange("p (k c) -> p k c", k=8)
        for k in range(8):
            fw.op("pe", lambda e, k=k, r=r, ptb=ptb: e.transpose(out=ptb[:, k, :], in_=xn[r][:, k * 128:(k + 1) * 128], identity=idb[:]),
                  [b_xn[r], b_idb], [bpt_])
        fw.op("dve", lambda e, r=r, ptb=ptb: e.tensor_copy(out=xnT[r][:], in_=ptb), [bpt_], [b_xnT[r]])
        yield
        need_weights()
        pl, bpl = ps.rot()
        for k in range(8):
            fw.op("pe", lambda e, k=k, r=r, pl=pl: e.matmul(pl[:, 0:512], lhsT=xnT[r][:, k, :], rhs=w1[:, k, 0:512], start=(k == 0), stop=(k == 7)),
                  [b_xnT[r], b_w1[k]], [bpl])
        pk, bpk = ps.rot()
        for k in range(8):
            fw.op("pe", lambda e, k=k, r=r, pk=pk: e.matmul(pk[:, 0:64], lhsT=xnT[r][:, k, :], rhs=w1[:, k, 512:576], start=(k == 0), stop=(k == 7)),
                  [b_xnT[r], b_w1[k]], [bpk])
        fw.op("act", lambda e, r=r, pl=pl: e.activation(out=junk[:, 0:256], in_=pl[:, 0:256], func=AF.Square, accum_out=st8[r][:, 2:3]),
              [bpl], [b_st8[r]])
        fw.op("act", lambda e, r=r, pl=pl: e.activation(out=junk[:, 256:512], in_=pl[:, 256:512], func=AF.Square, accum_out=st8[r][:, 3:4]),
              [bpl], [b_st8[r]])
        fw.op("act", lambda e, r=r, pk=pk: e.activation(out=junk[:, 512:576], in_=pk[:, 0:64], func=AF.Square, accum_out=st8[r][:, 4:5]),
              [bpk], [b_st8[r]])
        rstd_ops(None, st8[r][:, 2:4], st8[r][:, 5:7], 1.0 / 256, [b_st8[r]], [b_st8[r]], tm8[r][:, 2:4], b_tm8[r])
        rstd_ops(None, st8[r][:, 4:5], st8[r][:, 7:8], 1.0 / 64, [b_st8[r]], [b_st8[r]], tm8[r][:, 4:5], b_tm8[r])
        fw.op("dve", lambda e, r=r, pl=pl: e.tensor_scalar(out=latn[r][:, 0:256], in0=pl[:, 0:256], scalar1=st8[r][:, 5:6], scalar2=None, op0=ALU.mult),
              [bpl, b_st8[r]], [b_latn[r]])
        fw.op("dve", lambda e, r=r, pl=pl: e.tensor_scalar(out=latn[r][:, 256:512], in0=pl[:, 256:512], scalar1=st8[r][:, 6:7], scalar2=None, op0=ALU.mult),
              [bpl, b_st8[r]], [b_latn[r]])
        fw.op("dve", lambda e, r=r, pk=pk: e.scalar_tensor_tensor(out=kpe[r][:], in0=pk[:, 0:64], scalar=st8[r][:, 7:8], in1=bct[:, B_KN + 128:B_KN + 192],
                                                                   op0=ALU.mult, op1=ALU.mult),
              [bpk, b_st8[r], b_bc], [b_kpe[r]])
        Rf = rtmp[r][:].rearrange("p a b c -> p (a b c)")
        CCt, SNt, SPt = cs2[:, t, 0:64], cs2[:, t, 64:96], cs2[:, t, 96:128]
        fw.op("pool", lambda e, r=r, Rf=Rf, CCt=CCt: e.tensor_tensor(out=Rf[:, 0:64], in0=kpe[r][:, 0:64], in1=CCt, op=ALU.mult), [b_kpe[r], b_cs2], [b_rtmp[r]])
        fw.op("dve", lambda e, r=r, Rf=Rf, SNt=SNt: e.tensor_tensor(out=Rf[:, 64:96], in0=kpe[r][:, 32:64], in1=SNt, op=ALU.mult), [b_kpe[r], b_cs2], [b_rtmp[r]])
        fw.op("dve", lambda e, r=r, Rf=Rf, SPt=SPt: e.tensor_tensor(out=Rf[:, 96:128], in0=kpe[r][:, 0:32], in1=SPt, op=ALU.mult), [b_kpe[r], b_cs2], [b_rtmp[r]])
        fw.op("pool", lambda e, r=r, Rf=Rf: e.tensor_tensor(out=krb[r][:, 0:64], in0=Rf[:, 0:64], in1=Rf[:, 64:128], op=ALU.add), [b_rtmp[r]], [b_krb[r]])
        yield
        pt2, bpt2 = ps.rot()
        pt2b = pt2[:].bitcast(BF16).rearrange("p (k c) -> p k c", k=8)
        for c in range(4):
            fw.op("pe", lambda e, c=c, r=r, pt2b=pt2b: e.transpose(out=pt2b[:, c, :], in_=latn[r][:, c * 128:(c + 1) * 128], identity=idb[:]),
                  [b_latn[r], b_idb], [bpt2])
        fw.op("pe", lambda e, r=r, pt2b=pt2b: e.transpose(out=pt2b[0:64, 4, :], in_=krb[r][:, 0:64], identity=idb[:]),
              [b_krb[r], b_idb], [bpt2])
        fw.op("act", lambda e, r=r, pt2b=pt2b: e.copy(out=latT[r][:], in_=pt2b[:, 0:4, :]), [bpt2], [b_latT[r]])
        fw.op("act", lambda e, t=t, pt2b=pt2b: e.copy(out=KR[0:64, t * 128:(t + 1) * 128], in_=pt2b[0:64, 4, :]), [bpt2], [b_kr[t]])
        yield
        pq0, bpq0 = ps.rot()
        pq1, bpq1 = ps.rot()
        for c in range(2):
            fw.op("pe", lambda e, c=c, r=r, pq0=pq0: e.matmul(pq0[:, 0:512], lhsT=latT[r][:, c, :], rhs=wuq[:, c, 0:512], start=(c == 0), stop=(c == 1)),
                  [b_latT[r], b_wuq[c]], [bpq0])
        for c in range(2):
            fw.op("pe", lambda e, c=c, r=r, pq1=pq1: e.matmul(pq1[:, 0:256], lhsT=latT[r][:, c, :], rhs=wuq[:, c, 512:768], start=(c == 0), stop=(c == 1)),
                  [b_latT[r], b_wuq[c]], [bpq1])
        fw.op("act", lambda e, r=r, pq0=pq0: e.copy(out=qf[r][:, 0:512], in_=pq0[:, 0:512]), [bpq0], [b_qf[r]])
        fw.op("act", lambda e, r=r, pq1=pq1: e.copy(out=qf[r][:, 512:768], in_=pq1[:, 0:256]), [bpq1], [b_qf[r]])
        pv0, bpv0 = ps.rot()
        pv1, bpv1 = ps.rot()
        for hh, (pv, bpv) in enumerate(((pv0, bpv0), (pv1, bpv1))):
            for c in range(2):
                fw.op("pe", lambda e, c=c, r=r, pv=pv, hh=hh: e.matmul(pv[:, 0:512], lhsT=latT[r][:, 2 + c, :], rhs=wukv[:, c, hh * 512:(hh + 1) * 512],
                                                                      start=(c == 0), stop=(c == 1)),
                      [b_latT[r], b_wukv[c]], [bpv])
            fw.op("dve", lambda e, r=r, pv=pv, hh=hh: e.tensor_copy(out=kvf[r][:, hh * 512:(hh + 1) * 512], in_=pv[:, 0:512]), [bpv], [b_kvf[r]])
        yield
        q3 = qf[r][:].rearrange("p (h d) -> p h d", h=4)
        s3 = sq[r][:, 0:768].rearrange("p (h d) -> p h d", h=4)
        fw.op("dve", lambda e, r=r: e.tensor_tensor(out=sq[r][:, 0:768], in0=qf[r][:], in1=qf[r][:], op=ALU.mult), [b_qf[r]], [b_sq[r]])
        fw.op("dve", lambda e, r=r, s3=s3: e.tensor_reduce(out=st8[r][:, 8:12], in_=s3[:, :, 0:128], axis=AX.X, op=ALU.add), [b_sq[r]], [b_st8[r]])
        fw.op("dve", lambda e, r=r, s3=s3: e.tensor_reduce(out=st8[r][:, 12:16], in_=s3[:, :, 128:192], axis=AX.X, op=ALU.add), [b_sq[r]], [b_st8[r]])
        rstd_ops(None, st8[r][:, 8:12], tm8[r][:, 8:12], 1.0 / 128, [b_st8[r]], [b_tm8[r]], st8[r][:, 8:12], b_st8[r])
        rstd_ops(None, st8[r][:, 12:16], tm8[r][:, 12:16], 1.0 / 64, [b_st8[r]], [b_tm8[r]], st8[r][:, 12:16], b_st8[r])
        yield
        s4 = sq[r][:, 0:768].rearrange("p (h d) -> p h d", h=4)
        fw.op("dve", lambda e, r=r, q3=q3, s4=s4: e.tensor_tensor(out=s4[:, :, 0:128], in0=q3[:, :, 0:128],
                                                                  in1=tm8[r][:, 8:12].unsqueeze(2).to_broadcast([128, 4, 128]), op=ALU.mult),
              [b_qf[r], b_tm8[r]], [b_sq[r]])
        fw.op("dve", lambda e, r=r, s4=s4: e.tensor_tensor(out=qb[r][:, :, 0:128], in0=s4[:, :, 0:128],
                                                            in1=bct[:, B_QN:B_QN + 128].unsqueeze(1).to_broadcast([128, 4, 128]), op=ALU.mult),
              [b_sq[r], b_bc], [b_qb[r]])
        fw.op("dve", lambda e, r=r, q3=q3, s4=s4: e.tensor_tensor(out=s4[:, :, 128:192], in0=q3[:, :, 128:192],
                                                                  in1=tm8[r][:, 12:16].unsqueeze(2).to_broadcast([128, 4, 64]), op=ALU.mult),
              [b_qf[r], b_tm8[r]], [b_sq[r]])
        fw.op("dve", lambda e, r=r, s4=s4: e.tensor_tensor(out=s4[:, :, 128:192], in0=s4[:, :, 128:192],
                                                            in1=bct[:, B_QN + 128:B_QN + 192].unsqueeze(1).to_broadcast([128, 4, 64]), op=ALU.mult),
              [b_sq[r], b_bc], [b_sq[r]])
        A4 = Rf[:, 0:256].rearrange("p (h d) -> p h d", h=4)
        B4 = Rf[:, 256:512].rearrange("p (h d) -> p h d", h=4)
        fw.op("pool", lambda e, s4=s4, A4=A4, CCt=CCt: e.tensor_tensor(out=A4, in0=s4[:, :, 128:192], in1=CCt.unsqueeze(1).to_broadcast([128, 4, 64]), op=ALU.mult),
              [b_sq[r], b_cs2], [b_rtmp[r]])
        fw.op("dve", lambda e, s4=s4, B4=B4, SNt=SNt: e.tensor_tensor(out=B4[:, :, 0:32], in0=s4[:, :, 160:192], in1=SNt.unsqueeze(1).to_broadcast([128, 4, 32]), op=ALU.mult),
              [b_sq[r], b_cs2], [b_rtmp[r]])
        fw.op("dve", lambda e, s4=s4, B4=B4, SPt=SPt: e.tensor_tensor(out=B4[:, :, 32:64], in0=s4[:, :, 128:160], in1=SPt.unsqueeze(1).to_broadcast([128, 4, 32]), op=ALU.mult),
              [b_sq[r], b_cs2], [b_rtmp[r]])
        fw.op("pool", lambda e, r=r, A4=A4, B4=B4: e.tensor_tensor(out=qb[r][:, :, 128:192], in0=A4, in1=B4, op=ALU.add), [b_rtmp[r]], [b_qb[r]])
        yield
        pt3, bpt3 = ps.rot()
        pt3b = pt3[:].bitcast(BF16).rearrange("p (k c) -> p k c", k=8)
        for h in range(4):
            fw.op("pe", lambda e, h=h, r=r, pt3b=pt3b: e.transpose(out=pt3b[:, h, :], in_=qb[r][:, h, 0:128], identity=idb[:]), [b_qb[r], b_idb], [bpt3])
        for h in range(4):
            fw.op("pe", lambda e, h=h, r=r, pt3b=pt3b: e.transpose(out=pt3b[0:64, 4 + h, :], in_=qb[r][:, h, 128:192], identity=idb[:]), [b_qb[r], b_idb], [bpt3])
        fw.op("act", lambda e, tl=tl, pt3b=pt3b: e.copy(out=QT2[sti % 2][:, :, tl * 128:(tl + 1) * 128], in_=pt3b[:, 0:4, :]), [bpt3], [b_qt2[sti % 2][tl]])
        fw.op("act", lambda e, tl=tl, pt3b=pt3b: e.copy(out=QR2[sti % 2][0:64, :, tl * 128:(tl + 1) * 128], in_=pt3b[0:64, 4:8, :]), [bpt3], [b_qr2[sti % 2][tl]])
        yield
        k3 = kvf[r][:].rearrange("p (h d) -> p h d", h=4)
        sk = sq[r][:, 0:512].rearrange("p (h d) -> p h d", h=4)
        fw.op("dve", lambda e, k3=k3, sk=sk: e.tensor_tensor(out=sk, in0=k3[:, :, 0:128], in1=k3[:, :, 0:128], op=ALU.mult), [b_kvf[r]], [b_sq[r]])
        fw.op("dve", lambda e, r=r, sk=sk: e.tensor_reduce(out=st8[r][:, 8:12], in_=sk, axis=AX.X, op=ALU.add), [b_sq[r]], [b_st8[r]])
        rstd_ops(None, st8[r][:, 8:12], tm8[r][:, 8:12], 1.0 / 128, [b_st8[r]], [b_tm8[r]], st8[r][:, 8:12], b_st8[r])
        fw.op("dve", lambda e, r=r, k3=k3, sk=sk: e.tensor_tensor(out=sk, in0=k3[:, :, 0:128],
                                                                  in1=tm8[r][:, 8:12].unsqueeze(2).to_broadcast([128, 4, 128]), op=ALU.mult),
              [b_kvf[r], b_tm8[r]], [b_sq[r]])
        fw.op("dve", lambda e, r=r, sk=sk: e.tensor_tensor(out=kb[r][:], in0=sk,
                                                            in1=bct[:, B_KN:B_KN + 128].unsqueeze(1).to_broadcast([128, 4, 128]), op=ALU.mult),
              [b_sq[r], b_bc], [b_kb[r]])
        fw.op("dve", lambda e, t=t, k3=k3: e.tensor_copy(out=V[:, t, :, 0:128], in_=k3[:, :, 128:256]), [b_kvf[r]], [b_v[t]])
        pt4, bpt4 = ps.rot()
        pt4b = pt4[:].bitcast(BF16).rearrange("p (k c) -> p k c", k=8)
        for h in range(4):
            fw.op("pe", lambda e, h=h, r=r, pt4b=pt4b: e.transpose(out=pt4b[:, h, :], in_=kb[r][:, h, :], identity=idb[:]), [b_kb[r], b_idb], [bpt4])
        fw.op("act", lambda e, t=t, pt4b=pt4b: e.copy(out=KT[:, :, t * 128:(t + 1) * 128], in_=pt4b[:, 0:4, :]), [bpt4], [b_kt[t]])

        yield

    def attn(sti):
        nonlocal ptc, ofc
        qp = sti % 2
        nkt = sti * 4 + 4
        for h in range(4):
            oacc = [ps.acc(i) for i in range(4)]
            def issue_qk(j, h=h):
                c0 = max(j, sti * 4) - sti * 4
                ncol = (4 - c0) * 128
                sp_, bsp_ = ps.rot()
                rds = [b_kt[j], b_kr[j]] + [b_qt2[qp][i] for i in range(c0, 4)] + [b_qr2[qp][i] for i in range(c0, 4)]
                fw.op("pe", lambda e, h=h, j=j, c0=c0, ncol=ncol, sp_=sp_, qp=qp: e.matmul(sp_[:, 0:ncol], lhsT=KT[:, h, j * 128:(j + 1) * 128],
                                                                                   rhs=QT2[qp][:, h, c0 * 128:512], start=True, stop=False), rds, [bsp_])
                fw.op("pe", lambda e, h=h, j=j, c0=c0, ncol=ncol, sp_=sp_, qp=qp: e.matmul(sp_[:, 0:ncol], lhsT=KR[:, j * 128:(j + 1) * 128],
                                                                                   rhs=QR2[qp][:, h, c0 * 128:512], start=False, stop=True), rds, [bsp_])
                return sp_, bsp_, c0, ncol

            nxt = issue_qk(0)
            for j in range(nkt):
                sp_, bsp_, c0, ncol = nxt
                if j + 1 < nkt:
                    nxt = issue_qk(j + 1)
                pi = ptc % 3
                ptc += 1
                fw.op("act", lambda e, pi=pi, ncol=ncol, sp_=sp_: e.activation(out=PT[pi][:, 0:ncol], in_=sp_[:, 0:ncol], func=AF.Exp), [bsp_], [b_pt[pi]])
                if j >= sti * 4:
                    fw.op("pool", lambda e, pi=pi: e.tensor_tensor(out=PT[pi][:, 0:128], in0=PT[pi][:, 0:128], in1=incb[:], op=ALU.mult),
                          [b_pt[pi], b_incb], [b_pt[pi]])
                for qi in range(c0, 4):
                    po, bpo = oacc[qi]
                    fw.op("pe", lambda e, pi=pi, qi=qi, c0=c0, j=j, h=h, po=po, sti=sti: e.matmul(po[:, 0:129], lhsT=PT[pi][:, (qi - c0) * 128:(qi - c0 + 1) * 128],
                                                                                        rhs=V[:, j, h, 0:129], start=(j == 0), stop=(j == sti * 4 + qi)),
                          [b_pt[pi], b_v[j]], [bpo])
                if M1_BG:
                    yield
            for qi in range(4):
                t = sti * 4 + qi
                po, bpo = oacc[qi]
                oi = ofc % 2
                ofc += 1
                fw.op("dve", lambda e, oi=oi, po=po: e.reciprocal(out=ost[oi][:, 0:1], in_=po[:, 128:129]), [bpo], [b_ost[oi]])
                fw.op("dve", lambda e, oi=oi, po=po: e.tensor_scalar(out=of[oi][:], in0=po[:, 0:128], scalar1=ost[oi][:, 0:1], scalar2=None, op0=ALU.mult),
                      [bpo, b_ost[oi]], [b_of[oi]])
                fw.op("act", lambda e, oi=oi: e.activation(out=junk[:, 0:128], in_=of[oi][:], func=AF.Square, accum_out=ost[oi][:, 1:2]),
                      [b_of[oi]], [b_ost[oi]])
                rstd_ops(None, ost[oi][:, 1:2], ost[oi][:, 2:3], 1.0 / 128, [b_ost[oi]], [b_ost[oi]], ost[oi][:, 3:4], b_ost[oi])
                fw.op("dve", lambda e, oi=oi, t=t, h=h: e.scalar_tensor_tensor(out=MIX[:, t, h * 128:(h + 1) * 128], in0=of[oi][:], scalar=ost[oi][:, 2:3],
                                                                              in1=bct[:, B_MO + h * 128:B_MO + (h + 1) * 128], op0=ALU.mult, op1=ALU.mult),
                      [b_of[oi], b_ost[oi], b_bc], [b_mix[t]])


                yield

    def run_strands(strands, bg=None, bg_steps=1):
        strands = list(strands)
        while strands:
            for g in list(strands):
                try:
                    next(g)
                except StopIteration:
                    strands.remove(g)
            if bg is not None:
                for _ in range(bg_steps):
                    try:
                        next(bg)
                    except StopIteration:
                        bg = None
                        break
        return bg

    for sti in range(4):
        run_rolling([proj_tile(sti * 4 + tl, tl, sti) for tl in range(4)], 2, bg=wl[0])
        need_weights()
        run_strands([attn(sti)])


def build_m2(nc, fw, ps, st, sb, s, x, w_in, cst, ppt, bct, idb, b_cst, b_pp, b_bc, b_idb, MIX, b_mix, rstd_ops):
    idf = cst[:, K_ID:K_ID + 128]
    Uf = cst[:, K_U:K_U + 128]
    onesf = cst[:, K_ONE:K_ONE + 128]
    mneg = cst[:, K_MNEG:K_MNEG + 128]
    strf = cst[:, K_STR:K_STR + 128]
    w2 = sb(st, "w2", [128, 8, NW2], BF16); b_w2 = fw.bufs_n("w2", 8)
    stg = [(sb(st, f"stg2{i}", [128, NW2 // 2], F32), fw.buf(f"stg2{i}")) for i in range(2)]
    wl = [load_cast_gen(fw, st, sb, "w2", w_in[:, NW1:NW1 + NW2], 8, NW2, w2, b_w2, ppt[:, P_AN:P_AN + 8], b_pp, stg)]

    def need_weights():
        drain(wl[0])
        wl[0] = None
    O_AB, O_Z, O_GQ = 0, 8, 520
    xt = [sb(st, f"xt{i}", [128, DM], F32) for i in range(2)]; b_xt = fw.bufs_n("xt", 2)
    junk = sb(st, "junk", [128, DM], F32)
    xn = [sb(st, f"xn{i}", [128, DM], BF16) for i in range(2)]; b_xn = fw.bufs_n("xn", 2)
    xnT = sb(st, "xnT", [128, 8, 512], BF16); b_xnT = fw.bufs_n("xnT", 4)
    st8 = [sb(st, f"st8{i}", [128, 4], F32) for i in range(2)]; b_st8 = fw.bufs_n("st8", 2)
    ab = sb(st, "ab", [128, 4, 8], F32); b_ab = fw.bufs_n("ab", 4)
    gb = sb(st, "gb", [128, 4, 16], F32); b_gb = fw.bufs_n("gb", 4)
    zsg = sb(st, "zsg", [128, 4, 512], F32); b_zsg = fw.bufs_n("zsg", 4)
    Xc = [sb(st, f"Xc{i}", [128, 515], F32) for i in range(2)]; b_Xc = fw.bufs_n("Xc", 2)
    yacc = [sb(st, f"yacc{i}", [128, 512], F32) for i in range(2)]; b_yacc = fw.bufs_n("yacc", 2)
    halo = sb(st, "halo", [128, 12, 4], F32); b_halo = fw.bufs_n("halo", 12)
    Y = sb(st, "Y", [128, 12, 512], BF16); b_Y = fw.bufs_n("Y", 12)
    Sf = sb(st, "Sf", [128, 4, 128], F32); b_Sf = fw.bufs_n("Sf", 4)
    Sb = sb(st, "Sb", [128, 4, 128], BF16); b_Sb = fw.bufs_n("Sb", 4)
    fw.op("pool", lambda e: e.memset(halo[:], 0.0), [], b_halo)
    fw.op("pool", lambda e: e.memset(Sf[:], 0.0), [], b_Sf)
    fw.op("pool", lambda e: e.memset(Sb[:], 0.0), [], b_Sb)
    def two(f):
        return [f(0), f(1)]
    Gs2 = two(lambda q: sb(st, f"Gs{q}", [128, 24], F32)); b_Gs2 = fw.bufs_n("Gs", 2)
    sc2 = two(lambda q: sb(st, f"sc{q}", [128, 40], F32)); b_sc2 = fw.bufs_n("sc", 2)
    so2 = two(lambda q: sb(st, f"so{q}", [128, 12], F32)); b_so2 = fw.bufs_n("so", 2)
    g3f2 = two(lambda q: sb(st, f"g3f{q}", [128, 3, 4], F32)); b_g3f2 = fw.bufs_n("g3f", 2)
    g3b2 = two(lambda q: sb(st, f"g3b{q}", [128, 3, 4], BF16)); b_g3b2 = fw.bufs_n("g3b", 2)
    gr2 = two(lambda q: sb(st, f"gr{q}", [128, 8], F32)); b_gr2 = fw.bufs_n("gr", 2)
    NS = 2 if NEU_PINGPONG else 7

    def per_head(name, shape, dt):
        arrs = two(lambda q: sb(st, f"{name}{q}", [128, 4] + list(shape[1:]), dt))
        views = [[arrs[q][:, h] for h in range(4)] for q in range(2)]
        return views, two(lambda q: fw.bufs_n(f"{name}{q}", 4)), arrs

    def per_head_ns(name, shape, dt):
        arrs = two(lambda q: [sb(st, f"{name}{q}_{i}", [128, 4] + list(shape[1:]), dt) for i in range(NS)])
        views = [[[arrs[q][i][:, h] for i in range(NS)] for h in range(4)] for q in range(2)]
        return views, two(lambda q: [fw.bufs_n(f"{name}{q}{h}", NS) for h in range(4)]), arrs
    k6s, b_k6s, k6A = per_head("k6", [128, 6, 128], BF16)
    kqTs, b_kqTs, kqTA = per_head("kqT", [128, 3, 128], BF16)
    Ug3s, b_Ugs, Ug3A = per_head("Ug3", [128, 3, 128], BF16)
    tDs, b_tDs, tDA = per_head("tD", [128, 128], F32)
    dTs, b_dTs, dTA = per_head("dT", [128, 128], F32)
    dSs, b_dSs, dSA = per_head("dS", [128, 128], F32)
    Mms, b_Mms, MmA = per_head_ns("Mm", [128, 128], BF16)
    MmTs, b_MmTs, MmTA = per_head_ns("MmT", [128, 128], BF16)
    Pms, b_Pms, PmA = per_head_ns("Pm", [128, 128], BF16)
    aTs, b_aTs, aTA = per_head("attT", [128, 128], BF16)
    Ubs, b_Ubs, UbA = per_head("Ub", [128, 128], F32)
    WTs, b_WTs, WTA = per_head("WT", [128, 128], BF16)
    vns, b_vns, vnA = per_head("vn", [128, 128], BF16)
    ubf = sb(st, "ubf", [128, 128], BF16); b_ubf = fw.buf("ubf")
    onb = sb(st, "onb", [128, 128], BF16); b_onb = fw.buf("onb")
    fw.op("dve", lambda e: e.tensor_copy(out=ubf[:], in_=Uf), [b_cst], [b_ubf])
    fw.op("dve", lambda e: e.tensor_copy(out=onb[:], in_=onesf), [b_cst], [b_onb])
    stage = [0]

    def bank(h):
        i = (stage[0] % 2) * 4 + h
        return ps.t[i], ps.b[i]

    def next_stage():
        stage[0] += 1

    s8_done = set()

    def gdn_tile(sti, tl):
        sel = tl % 2
        Gs, sc, so, g3f, g3b, gr = Gs2[sel], sc2[sel], so2[sel], g3f2[sel], g3b2[sel], gr2[sel]
        b_Gs, b_sc, b_so, b_g3f, b_g3b, b_gr = b_Gs2[sel], b_sc2[sel], b_so2[sel], b_g3f2[sel], b_g3b2[sel], b_gr2[sel]
        k6, kqT, Ug3, tD, dT, dS, Mm, MmT, Pm, aT, Ub, WT, vn = [x_[sel] for x_ in (k6s, kqTs, Ug3s, tDs, dTs, dSs, Mms, MmTs, Pms, aTs, Ubs, WTs, vns)]
        b_k6, b_kqT, b_Ug, b_tD, b_dT, b_dS, b_Mm, b_MmT, b_Pm, b_aT, b_Ub, b_WT, b_vn = [x_[sel] for x_ in (b_k6s, b_kqTs, b_Ugs, b_tDs, b_dTs, b_dSs, b_Mms, b_MmTs, b_Pms, b_aTs, b_Ubs, b_WTs, b_vns)]

        def bank(h):
            return ps.t[sel * 4 + h], ps.b[sel * 4 + h]
        bpall = [ps.b[sel * 4 + h] for h in range(4)]
        G4 = ps.big[:, sel * 4:(sel + 1) * 4, :]
        G4b = G4.bitcast(BF16)
        k6a, kqTa, Ug3a, tDa, dTa, dSa, aTa, Uba, WTa = k6A[sel], kqTA[sel], Ug3A[sel], tDA[sel], dTA[sel], dSA[sel], aTA[sel], UbA[sel], WTA[sel]
        Mma, MmTa, Pma = MmA[sel], MmTA[sel], PmA[sel]

        def bc(ap4):
            return ap4.unsqueeze(2).to_broadcast([128, 4, 128])

        def allb(bl, i=None):
            return [bl[h] if i is None else bl[h][i] for h in range(4)]
        t = sti * 4 + tl
        cols = slice(tl * 128, (tl + 1) * 128)
        G_ = gb[:, tl, :]
        if t > GDN_MAXTILE:
            return
        fw.op("dve", lambda e, G_=G_: e.tensor_copy(out=g3b[:, 0, :], in_=G_[:, 0:4]), [b_gb[tl]], [b_g3b])
        fw.op("dve", lambda e: e.tensor_copy(out=g3f[:, 0, :], in_=g3b[:, 0, :]), [b_g3b], [b_g3f])
        fw.op("dve", lambda e, G_=G_: e.tensor_tensor(out=gr[:, 0:4], in0=G_[:, 0:4], in1=g3f[:, 0, :], op=ALU.subtract), [b_gb[tl], b_g3f], [b_gr])
        fw.op("dve", lambda e: e.tensor_copy(out=g3b[:, 1, :], in_=gr[:, 0:4]), [b_gr], [b_g3b])
        fw.op("dve", lambda e: e.tensor_copy(out=g3f[:, 1, :], in_=g3b[:, 1, :]), [b_g3b], [b_g3f])
        fw.op("dve", lambda e: e.tensor_tensor(out=gr[:, 4:8], in0=gr[:, 0:4], in1=g3f[:, 1, :], op=ALU.subtract), [b_gr, b_g3f], [b_gr])
        fw.op("dve", lambda e: e.tensor_copy(out=g3b[:, 2, :], in_=gr[:, 4:8]), [b_gr], [b_g3b])
        fw.op("dve", lambda e: e.tensor_copy(out=g3f[:, 2, :], in_=g3b[:, 2, :]), [b_g3b], [b_g3f])
        pg, bpg = ps.rot()
        for i in range(3):
            fw.op("pe", lambda e, pg=pg, i=i: e.matmul(pg[:, 0:4], lhsT=ubf[:], rhs=g3b[:, i, :], start=(i == 0), stop=(i == 2)), [b_ubf, b_g3b], [bpg])
        for i in range(3):
            fw.op("pe", lambda e, pg=pg, i=i: e.matmul(pg[:, 4:8], lhsT=onb[:], rhs=g3b[:, i, :], start=(i == 0), stop=(i == 2), skip_group_check=True), [b_onb, b_g3b], [bpg])
        fw.op("dve", lambda e, pg=pg: e.tensor_copy(out=Gs[:, 0:8], in_=pg[:, 0:8]), [bpg], [b_Gs])
        fw.op("act", lambda e: e.activation(out=Gs[:, 8:12], in_=Gs[:, 0:4], func=AF.Exp), [b_Gs], [b_Gs])
        fw.op("dve", lambda e: e.tensor_tensor(out=Gs[:, 20:24], in0=Gs[:, 4:8], in1=Gs[:, 0:4], op=ALU.subtract), [b_Gs], [b_Gs])
        fw.op("act", lambda e: e.activation(out=Gs[:, 12:16], in_=Gs[:, 20:24], func=AF.Exp), [b_Gs], [b_Gs])
        fw.op("act", lambda e: e.activation(out=Gs[:, 16:20], in_=Gs[:, 4:8], func=AF.Exp), [b_Gs], [b_Gs])
        yield
        B1 = [bank(h) for h in range(4)]
        for h in range(4):
            p1, bp1 = B1[h]
            p1b = p1[:].bitcast(BF16).rearrange("p (k c) -> p k c", k=8)
            for i, c in enumerate((h, 4 + h, 8 + h)):
                fw.op("pe", lambda e, p1b=p1b, i=i, c=c, cols=cols: e.transpose(out=p1b[:, i, :], in_=Y[:, c, cols], identity=idb[:]), [b_Y[c], b_idb], [bp1])
            fw.op("act", lambda e, p1b=p1b, h=h: e.activation(out=junk[:, 0:128], in_=p1b[:, 0, :], func=AF.Square, accum_out=sc[:, h:h + 1]), [bp1], [b_sc])
            fw.op("act", lambda e, p1b=p1b, h=h: e.activation(out=junk[:, 128:256], in_=p1b[:, 1, :], func=AF.Square, accum_out=sc[:, 4 + h:5 + h]), [bp1], [b_sc])
        fw.op("act", lambda e: e.activation(out=sc[:, 8:16], in_=sc[:, 0:8], func=AF.Ln, bias=EPS), [b_sc], [b_sc])
        fw.op("act", lambda e: e.activation(out=sc[:, 16:24], in_=sc[:, 8:16], func=AF.Exp, scale=-0.5), [b_sc], [b_sc])
        fw.op("dve", lambda e: e.tensor_tensor(out=sc[:, 24:28], in0=sc[:, 20:24], in1=Gs[:, 8:12], op=ALU.mult), [b_sc, b_Gs], [b_sc])
        fw.op("dve", lambda e: e.tensor_tensor(out=sc[:, 28:32], in0=sc[:, 20:24], in1=Gs[:, 12:16], op=ALU.mult), [b_sc, b_Gs], [b_sc])
        fw.op("dve", lambda e: e.tensor_scalar(out=sc[:, 32:36], in0=sc[:, 16:20], scalar1=float(128 ** -0.5), scalar2=None, op0=ALU.mult), [b_sc], [b_sc])
        fw.op("dve", lambda e: e.tensor_tensor(out=sc[:, 36:40], in0=sc[:, 32:36], in1=Gs[:, 8:12], op=ALU.mult), [b_sc, b_Gs], [b_sc])
        kps4, qps4, vps4 = G4b[:, :, 128:256], G4b[:, :, 0:128], G4b[:, :, 256:384]
        fw.op("act", lambda e: e.copy(out=k6a[:, :, 5, :], in_=vps4), bpall, b_k6)
        for i_, (src4, c0_) in enumerate(((kps4, 20), (kps4, 24), (kps4, 28), (qps4, 32), (qps4, 36))):
            fw.op("dve", lambda e, i_=i_, src4=src4, c0_=c0_: e.tensor_tensor(out=k6a[:, :, i_, :], in0=src4, in1=bc(sc[:, c0_:c0_ + 4]), op=ALU.mult), bpall + [b_sc], b_k6)
        for i in range(3):
            fw.op("pool", lambda e, i=i: e.tensor_tensor(out=Ug3a[:, :, i, :], in0=Uf.unsqueeze(1).to_broadcast([128, 4, 128]), in1=bc(g3f[:, i, :]), op=ALU.mult),
                  [b_cst, b_g3f], b_Ug)
        if GDN_STOP <= 1:
            return
        yield
        for h in range(4):
            p2, bp2 = bank(h)
            p2b = p2[:].bitcast(BF16).rearrange("p (k c) -> p k c", k=8)
            for i, src in enumerate((0, 3, 4)):
                fw.op("pe", lambda e, p2b=p2b, i=i, src=src, h=h: e.transpose(out=p2b[:, i, :], in_=k6[h][:, src, :], identity=idb[:]), [b_k6[h], b_idb], [bp2])
        fw.op("act", lambda e: e.copy(out=kqTa[:].rearrange("p h k c -> p h (k c)"), in_=G4b[:, :, 0:384]), bpall, b_kqT)
        if GDN_STOP <= 2:
            return
        yield
        for h in range(4):
            p3, bp3 = bank(h)
            fw.op("pe", lambda e, p3=p3, h=h: e.matmul(p3[:, 0:256], lhsT=kqT[h][:, 0, :], rhs=kqT[h][:, 0:2, :], start=True, stop=True), [b_kqT[h]], [bp3])
            for i in range(3):
                fw.op("pe", lambda e, p3=p3, h=h, i=i: e.matmul(p3[:, 256:384], lhsT=onb[:], rhs=Ug3[h][:, i, :], start=(i == 0), stop=(i == 2), skip_group_check=True), [b_onb, b_Ug[h]], [bp3])
            fw.op("dve", lambda e, p3=p3, h=h: e.scalar_tensor_tensor(out=tD[h][:], in0=p3[:, 256:384], scalar=Gs[:, h:h + 1], in1=mneg, op0=ALU.subtract, op1=ALU.add),
                  [bp3, b_Gs, b_cst], [b_tD[h]])
        fw.op("act", lambda e: e.activation(out=dTa[:], in_=tDa[:], func=AF.Exp), b_tD, b_dT)
        fw.op("pool", lambda e: e.tensor_tensor(out=dSa[:], in0=dTa[:], in1=strf.unsqueeze(1).to_broadcast([128, 4, 128]), op=ALU.mult), b_dT + [b_cst], b_dS)
        for h in range(4):
            p3, bp3 = bank(h)
            fw.op("dve", lambda e, p3=p3, h=h, G_=G_: e.scalar_tensor_tensor(out=Mm[h][0][:], in0=p3[:, 0:128], scalar=G_[:, 8 + h:9 + h], in1=dS[h][:], op0=ALU.mult, op1=ALU.mult),
                  [bp3, b_gb[tl], b_dS[h]], [b_Mm[h][0]])
        fw.op("dve", lambda e: e.tensor_tensor(out=aTa[:], in0=G4[:, :, 128:256], in1=dTa[:], op=ALU.mult), bpall + b_dT, b_aT)
        fw.op("pool", lambda e: e.tensor_tensor(out=Pma[0][:], in0=Mma[0][:], in1=idb[:].unsqueeze(1).to_broadcast([128, 4, 128]), op=ALU.add), allb(b_Mm, 0) + [b_idb], allb(b_Pm, 0))
        if GDN_STOP <= 3:
            return
        yield
        for h in range(4):
            p4, bp4 = bank(h)
            p4b = p4[:].bitcast(BF16)
            fw.op("pe", lambda e, p4b=p4b, h=h: e.transpose(out=p4b[:, 0:128], in_=Mm[h][0][:], identity=idb[:]), [b_Mm[h][0], b_idb], [bp4])
        fw.op("act", lambda e: e.copy(out=MmTa[0][:], in_=G4b[:, :, 0:128]), bpall, allb(b_MmT, 0))
        if GDN_STOP <= 4:
            return
        for lvl in range(NEU_LEVELS):
            a, b = (lvl % 2, (lvl + 1) % 2) if NEU_PINGPONG else (lvl, lvl + 1)
            yield
            for h in range(4):
                p5, bp5 = bank(h)
                if lvl < NEU_LEVELS - 1:
                    fw.op("pe", lambda e, p5=p5, h=h, a=a: e.matmul(p5[:, 0:128], lhsT=MmT[h][a][:], rhs=Mm[h][a][:], start=True, stop=True), [b_MmT[h][a], b_Mm[h][a]], [bp5])
                fw.op("pe", lambda e, p5=p5, h=h, a=a: e.matmul(p5[:, 128:256], lhsT=Mm[h][a][:], rhs=MmT[h][a][:], start=True, stop=True), [b_MmT[h][a], b_Mm[h][a]], [bp5])
            ev = "act"
            if lvl < NEU_LEVELS - 1:
                if ev == "act":
                    fw.op("act", lambda e, b=b: e.copy(out=Mma[b][:], in_=G4[:, :, 0:128]), bpall, allb(b_Mm, b))
                else:
                    fw.op("dve", lambda e, b=b: e.tensor_copy(out=Mma[b][:], in_=G4[:, :, 0:128]), bpall, allb(b_Mm, b))
            if ev == "act":
                fw.op("act", lambda e, b=b: e.copy(out=MmTa[b][:], in_=G4[:, :, 128:256]), bpall, allb(b_MmT, b))
            else:
                fw.op("dve", lambda e, b=b: e.tensor_copy(out=MmTa[b][:], in_=G4[:, :, 128:256]), bpall, allb(b_MmT, b))
            yield
            for h in range(4):
                p6, bp6 = bank(h)
                fw.op("pe", lambda e, p6=p6, h=h, a=a, b=b: e.matmul(p6[:, 0:128], lhsT=MmT[h][b][:], rhs=Pm[h][a][:], start=True, stop=True), [b_MmT[h][b], b_Pm[h][a]], [bp6])
            fw.op("dve", lambda e, a=a, b=b: e.tensor_tensor(out=Pma[b][:], in0=G4[:, :, 0:128], in1=Pma[a][:], op=ALU.add), bpall + allb(b_Pm, a), allb(b_Pm, b))
        PF = (NEU_LEVELS % 2) if NEU_PINGPONG else NEU_LEVELS
        if GDN_STOP <= 5:
            return
        yield
        for h in range(4):
            p7, bp7 = bank(h)
            fw.op("pe", lambda e, p7=p7, h=h: e.matmul(p7[:, 0:128], lhsT=Pm[h][PF][:], rhs=k6[h][:, 5, :], start=True, stop=True), [b_Pm[h][PF], b_k6[h]], [bp7])
            fw.op("pe", lambda e, p7=p7, h=h: e.matmul(p7[:, 128:256], lhsT=k6[h][:, 1, :], rhs=Pm[h][PF][:], start=True, stop=True), [b_Pm[h][PF], b_k6[h]], [bp7])
        fw.op("dve", lambda e, G_=G_: e.tensor_tensor(out=Uba[:], in0=G4[:, :, 0:128], in1=bc(G_[:, 4:8]), op=ALU.mult), bpall + [b_gb[tl]], b_Ub)
        fw.op("dve", lambda e: e.tensor_copy(out=WTa[:], in_=G4[:, :, 128:256]), bpall, b_WT)
        yield
        while t > 0 and (t - 1) not in s8_done:
            yield
        for h in range(4):
            p8, bp8 = bank(h)
            fw.op("pe", lambda e, p8=p8, h=h: e.matmul(p8[:, 0:128], lhsT=WT[h][:], rhs=Sb[:, h, :], start=True, stop=True), [b_WT[h], b_Sb[h]], [bp8])
            fw.op("dve", lambda e, p8=p8, h=h, G_=G_: e.scalar_tensor_tensor(out=vn[h][:], in0=p8[:, 0:128], scalar=G_[:, 8 + h:9 + h], in1=Ub[h][:], op0=ALU.mult, op1=ALU.add),
                  [bp8, b_gb[tl], b_Ub[h]], [b_vn[h]])
        yield
        B9 = [bank(h) for h in range(4)]
        for h in range(4):
            p9, bp9 = B9[h]
            fw.op("pe", lambda e, p9=p9, h=h: e.matmul(p9[:, 0:128], lhsT=kqT[h][:, 2, :], rhs=Sb[:, h, :], start=True, stop=False), [b_kqT[h], b_Sb[h]], [bp9])
            fw.op("pe", lambda e, p9=p9, h=h: e.matmul(p9[:, 0:128], lhsT=aT[h][:], rhs=vn[h][:], start=False, stop=True), [b_aT[h], b_vn[h]], [bp9])
            fw.op("pe", lambda e, p9=p9, h=h: e.matmul(p9[:, 128:256], lhsT=k6[h][:, 2, :], rhs=vn[h][:], start=True, stop=True), [b_k6[h], b_vn[h]], [bp9])
            fw.op("act", lambda e, p9=p9, h=h: e.activation(out=junk[:, 0:128], in_=p9[:, 0:128], func=AF.Square, accum_out=so[:, h:h + 1]), [bp9], [b_so])
            fw.op("dve", lambda e, p9=p9, h=h: e.scalar_tensor_tensor(out=Sf[:, h, :], in0=Sf[:, h, :], scalar=Gs[:, 16 + h:17 + h], in1=p9[:, 128:256], op0=ALU.mult, op1=ALU.add),
                  [bp9, b_Gs, b_Sf[h]], [b_Sf[h]])
        fw.op("pool", lambda e: e.tensor_copy(out=Sb[:], in_=Sf[:]), b_Sf, b_Sb)
        fw.op("act", lambda e: e.activation(out=so[:, 4:8], in_=so[:, 0:4], func=AF.Ln, scale=1.0 / 128, bias=EPS), [b_so], [b_so])
        fw.op("act", lambda e: e.activation(out=so[:, 8:12], in_=so[:, 4:8], func=AF.Exp, scale=-0.5), [b_so], [b_so])
        for h in range(4):
            p9, bp9 = B9[h]
            fw.op("dve", lambda e, p9=p9, h=h, t=t, tl=tl: e.scalar_tensor_tensor(out=MIX[:, t, 512 + h * 128:512 + (h + 1) * 128], in0=p9[:, 0:128], scalar=so[:, 8 + h:9 + h],
                                                                               in1=zsg[:, tl, h * 128:(h + 1) * 128], op0=ALU.mult, op1=ALU.mult),
                  [bp9, b_so, b_zsg[tl]], [b_mix[t]])
        s8_done.add(t)


    for sti in range(4):
        if sti * 4 > GDN_MAXTILE:
            break
        def m2_proj_tile(tl, sti=sti):
            t = sti * 4 + tl
            r = t % 2
            fw.dma(xt[r][:], x[s, t * 128:(t + 1) * 128, :], writes=[b_xt[r]])
            fw.op("act", lambda e, r=r: e.activation(out=xn[r][:], in_=xt[r][:], func=AF.Square, accum_out=st8[r][:, 0:1]), [b_xt[r]], [b_st8[r], b_xn[r]])
            yield
            rstd_ops(None, st8[r][:, 0:1], st8[r][:, 1:2], 1.0 / DM, [b_st8[r]], [b_st8[r]], st8[r][:, 2:3], b_st8[r])
            fw.op("dve", lambda e, r=r: e.tensor_scalar(out=xn[r][:], in0=xt[r][:], scalar1=st8[r][:, 1:2], scalar2=None, op0=ALU.mult),
                  [b_xt[r], b_st8[r]], [b_xn[r]])
            yield
            pt_, bpt_ = ps.rot()
            ptb = pt_[:].bitcast(BF16).rearrange("p (k c) -> p k c", k=8)
            for k in range(8):
                fw.op("pe", lambda e, k=k, r=r, ptb=ptb: e.transpose(out=ptb[:, k, :], in_=xn[r][:, k * 128:(k + 1) * 128], identity=idb[:]),
                      [b_xn[r], b_idb], [bpt_])
            fw.op("dve", lambda e, tl=tl, ptb=ptb: e.tensor_copy(out=xnT[:, :, tl * 128:(tl + 1) * 128], in_=ptb), [bpt_], [b_xnT[tl]])
            yield
            need_weights()
            pa, bpa = ps.rot()
            for k in range(8):
                fw.op("pe", lambda e, k=k, tl=tl, pa=pa: e.matmul(pa[:, 0:8], lhsT=xnT[:, k, tl * 128:(tl + 1) * 128], rhs=w2[:, k, O_AB:O_AB + 8],
                                                                 start=(k == 0), stop=(k == 7)), [b_xnT[tl], b_w2[k]], [bpa])
            fw.op("act", lambda e, tl=tl, pa=pa: e.copy(out=ab[:, tl, :], in_=pa[:, 0:8]), [bpa], [b_ab[tl]])
            yield
            pz, bpz = ps.rot()
            for k in range(8):
                fw.op("pe", lambda e, k=k, tl=tl, pz=pz: e.matmul(pz[:, 0:512], lhsT=xnT[:, k, tl * 128:(tl + 1) * 128], rhs=w2[:, k, O_Z:O_Z + 512],
                                                                 start=(k == 0), stop=(k == 7)), [b_xnT[tl], b_w2[k]], [bpz])
            fw.op("act", lambda e, tl=tl, pz=pz: e.activation(out=zsg[:, tl, :], in_=pz[:, 0:512], func=AF.Silu), [bpz], [b_zsg[tl]])
            z3 = zsg[:, tl, :].rearrange("p (h d) -> p h d", h=4)
            fw.op("pool", lambda e, z3=z3: e.tensor_tensor(out=z3, in0=z3, in1=bct[:, B_GN:B_GN + 128].unsqueeze(1).to_broadcast([128, 4, 128]), op=ALU.mult),
                  [b_zsg[tl], b_bc], [b_zsg[tl]])
            yield
            G_ = gb[:, tl, :]
            fw.op("dve", lambda e, tl=tl, G_=G_: e.tensor_tensor(out=G_[:, 12:16], in0=ab[:, tl, 0:4], in1=bct[:, B_DT:B_DT + 4], op=ALU.add), [b_ab[tl], b_bc], [b_gb[tl]])
            fw.op("act", lambda e, G_=G_: e.activation(out=G_[:, 12:16], in_=G_[:, 12:16], func=AF.Exp), [b_gb[tl]], [b_gb[tl]])
            fw.op("act", lambda e, G_=G_: e.activation(out=G_[:, 12:16], in_=G_[:, 12:16], func=AF.Ln, bias=1.0), [b_gb[tl]], [b_gb[tl]])
            fw.op("act", lambda e, G_=G_: e.activation(out=G_[:, 8:12], in_=bct[:, B_AL:B_AL + 4], func=AF.Exp), [b_bc], [b_gb[tl]])
            fw.op("dve", lambda e, G_=G_: e.scalar_tensor_tensor(out=G_[:, 0:4], in0=G_[:, 12:16], scalar=-1.0, in1=G_[:, 8:12], op0=ALU.mult, op1=ALU.mult),
                  [b_gb[tl]], [b_gb[tl]])
            fw.op("act", lambda e, tl=tl, G_=G_: e.activation(out=G_[:, 12:16], in_=ab[:, tl, 4:8], func=AF.Exp, scale=-1.0), [b_ab[tl]], [b_gb[tl]])
            fw.op("dve", lambda e, G_=G_: e.tensor_scalar(out=G_[:, 12:16], in0=G_[:, 12:16], scalar1=1.0, scalar2=None, op0=ALU.add), [b_gb[tl]], [b_gb[tl]])
            fw.op("dve", lambda e, G_=G_: e.reciprocal(out=G_[:, 4:8], in_=G_[:, 12:16]), [b_gb[tl]], [b_gb[tl]])
            fw.op("dve", lambda e, G_=G_: e.tensor_scalar(out=G_[:, 8:12], in0=G_[:, 4:8], scalar1=-1.0, scalar2=None, op0=ALU.mult), [b_gb[tl]], [b_gb[tl]])

            yield

        run_rolling([m2_proj_tile(tl) for tl in range(4)], 2, bg=wl[0])
        need_weights()
        for c in range(12):
            pf, bpf = ps.rot()
            for k in range(8):
                fw.op("pe", lambda e, k=k, c=c, pf=pf: e.matmul(pf[:, 0:512], lhsT=w2[:, k, O_GQ + c * 128:O_GQ + (c + 1) * 128], rhs=xnT[:, k, :],
                                                               start=(k == 0), stop=(k == 7)), [b_w2[k]] + b_xnT, [bpf])
            xi = c % 2
            fw.op("pool", lambda e, xi=xi, c=c: e.tensor_copy(out=Xc[xi][:, 0:3], in_=halo[:, c, 0:3]), [b_halo[c]], [b_Xc[xi]])
            fw.op("act", lambda e, xi=xi, pf=pf: e.copy(out=Xc[xi][:, 3:515], in_=pf[:, 0:512]), [bpf], [b_Xc[xi]])
            fw.op("pool", lambda e, xi=xi, c=c: e.tensor_copy(out=halo[:, c, 0:3], in_=Xc[xi][:, 512:515]), [b_Xc[xi]], [b_halo[c]])
            cw = lambda i, c=c: ppt[:, P_CW + c * 4 + i:P_CW + c * 4 + i + 1]
            fw.op("act", lambda e, xi=xi, cw=cw: e.activation(out=yacc[xi][:], in_=Xc[xi][:, 0:512], func=AF.Copy, scale=cw(0)), [b_Xc[xi], b_pp], [b_yacc[xi]])
            for i in (1, 2, 3):
                fw.op("dve", lambda e, xi=xi, cw=cw, i=i: e.scalar_tensor_tensor(out=yacc[xi][:], in0=Xc[xi][:, i:i + 512], scalar=cw(i), in1=yacc[xi][:],
                                                                                op0=ALU.mult, op1=ALU.add), [b_Xc[xi], b_pp, b_yacc[xi]], [b_yacc[xi]])
            if c > 0:
                fw.op("act", lambda e, xj=(c - 1) % 2, cj=c - 1: e.activation(out=Y[:, cj, :], in_=yacc[xj][:], func=AF.Silu), [b_yacc[(c - 1) % 2]], [b_Y[c - 1]])
        fw.op("act", lambda e: e.activation(out=Y[:, 11, :], in_=yacc[11 % 2][:], func=AF.Silu), [b_yacc[11 % 2]], [b_Y[11]])


        gens = [gdn_tile(sti, tl) for tl in range(4)]
        active = []
        nxt = 0
        while nxt < len(gens) or active:
            while len(active) < 2 and nxt < len(gens):
                active.append(gens[nxt]); nxt += 1
            for g in list(active):
                try:
                    next(g)
                except StopIteration:
                    active.remove(g)
                    break


def build_wout(nc, fw, ps, st, sb, s, x, w_out, idb, b_idb, MIX, b_mix, H, b_h):
    wo = sb(st, "wo", [128, 8, DM], BF16); b_wo = fw.bufs_n("wo", 8)
    stg = [(sb(st, f"wstg{i}", [128, 1024], F32), fw.buf(f"wstg{i}")) for i in range(2)]
    wl = [load_cast_gen(fw, st, sb, "wo", w_out, 8, DM, wo, b_wo, None, None, stg)]

    def need_weights():
        drain(wl[0])
        wl[0] = None
    mT = [sb(st, f"mT{i}", [128, 8, 128], BF16) for i in range(2)]; b_mT = fw.bufs_n("mT", 2)
    def wout_tile(t):
        r = t % 2
        fw.dma(H[:, t, :], x[s, t * 128:(t + 1) * 128, :], writes=[b_h[t]])
        pt_, bpt_ = ps.rot()
        ptb = pt_[:].bitcast(BF16).rearrange("p (k c) -> p k c", k=8)
        for k in range(8):
            fw.op("pe", lambda e, k=k, t=t, ptb=ptb: e.transpose(out=ptb[:, k, :], in_=MIX[:, t, k * 128:(k + 1) * 128], identity=idb[:]),
                  [b_mix[t], b_idb], [bpt_])
        fw.op("act", lambda e, r=r, ptb=ptb: e.copy(out=mT[r][:], in_=ptb), [bpt_], [b_mT[r]])
        yield
        need_weights()
        for half in range(2):
            po, bpo = ps.rot()
            for k in range(8):
                fw.op("pe", lambda e, k=k, r=r, po=po, half=half: e.matmul(po[:, 0:512], lhsT=mT[r][:, k, :], rhs=wo[:, k, half * 512:(half + 1) * 512],
                                                                          start=(k == 0), stop=(k == 7)), [b_mT[r], b_wo[k]], [bpo])
            fw.op("dve", lambda e, t=t, po=po, half=half: e.tensor_tensor(out=H[:, t, half * 512:(half + 1) * 512], in0=po[:, 0:512],
                                                                          in1=H[:, t, half * 512:(half + 1) * 512], op=ALU.add),
                  [bpo, b_h[t]], [b_h[t]])
            yield

    run_rolling([wout_tile(t) for t in range(NT)], 2, bg=wl[0])
    need_weights()


def build_mlp(nc, fw, ps, st, sb, s, w_up, w_down, ppt, b_pp, idb, b_idb, HNT, H, b_h, out, rstd_ops):
    NB = 4
    FB = DFF // NB
    junk = sb(st, "junk2", [128, DM], F32); b_junk = fw.buf("junk2")
    hn = [sb(st, f"hn{i}", [128, DM], BF16) for i in range(2)]; b_hn = fw.bufs_n("hn", 2)
    st4 = [sb(st, f"st4{i}", [128, 4], F32) for i in range(2)]; b_st4 = fw.bufs_n("st4", 2)
    b_hnT = fw.bufs_n("hnT", NT)
    wu = [sb(st, f"wu{i}", [128, 8, FB], BF16) for i in range(2)]; b_wu = [fw.bufs_n(f"wu{i}", 8) for i in range(2)]
    wd = [sb(st, f"wd{i}", [128, 8, DM], BF16) for i in range(2)]; b_wd = [fw.bufs_n(f"wd{i}", 8) for i in range(2)]
    stg = [(sb(st, f"mstg{i}", [128, 1024], F32), fw.buf(f"mstg{i}")) for i in range(3)]
    aT = sb(st, "aT", [128, 8, 512], BF16); b_aT = fw.bufs_n("aT", 8)
    rl = [sb(st, f"rl{i}", [128, 512], BF16) for i in range(2)]; b_rl = fw.bufs_n("rl", 2)

    def load_block(nb):
        r = nb % 2
        load_cast(fw, st, sb, "wu", w_up[:, nb * FB:(nb + 1) * FB], 8, FB, wu[r], b_wu[r], ppt[:, P_MN:P_MN + 8], b_pp, stg)
        load_cast(fw, st, sb, "wd", w_down[nb * FB:(nb + 1) * FB, :], 8, DM, wd[r], b_wd[r], None, None, stg)

    def load_block_gen(nb):
        r = nb % 2
        gain = ppt[:, P_MN:P_MN + 8]
        for k in range(8):
            stt, b_st = stg[k % len(stg)]
            fw.dma(stt[:, 0:FB], w_up[k * 128:(k + 1) * 128, nb * FB:(nb + 1) * FB], writes=[b_st])
            fw.op("pool", lambda e, k=k, stt=stt, r=r: e.tensor_scalar(out=wu[r][:, k, :], in0=stt[:, 0:FB], scalar1=gain[:, k:k + 1], scalar2=1.0,
                                                                      op0=ALU.mult, op1=ALU.mult), [b_st, b_pp], [b_wu[r][k]])
            yield
        for k in range(8):
            stt, b_st = stg[k % len(stg)]
            fw.dma(stt[:, 0:DM], w_down[nb * FB + k * 128:nb * FB + (k + 1) * 128, :], writes=[b_st])
            fw.op("pool", lambda e, k=k, stt=stt, r=r: e.tensor_copy(out=wd[r][:, k, :], in_=stt[:, 0:DM]), [b_st], [b_wd[r][k]])
            yield

    def step_loader(g):
        try:
            next(g)
            return g
        except StopIteration:
            return None

    def block0_loader():
        yield from load_cast_gen(fw, st, sb, "wu", w_up[:, 0:FB], 8, FB, wu[0], b_wu[0], ppt[:, P_MN:P_MN + 8], b_pp, stg)
        yield from load_cast_gen(fw, st, sb, "wd", w_down[0:FB, :], 8, DM, wd[0], b_wd[0], None, None, stg)
    wloader = block0_loader()
    def hn_tile(t):
        r = t % 2
        fw.op("act", lambda e, t=t, r=r: e.activation(out=hn[r][:], in_=H[:, t, :], func=AF.Square, accum_out=st4[r][:, 0:1]), [b_h[t]], [b_st4[r], b_hn[r]])
        yield
        rstd_ops(None, st4[r][:, 0:1], st4[r][:, 1:2], 1.0 / DM, [b_st4[r]], [b_st4[r]], st4[r][:, 2:3], b_st4[r])
        yield
        fw.op("dve", lambda e, t=t, r=r: e.tensor_scalar(out=hn[r][:], in0=H[:, t, :], scalar1=st4[r][:, 1:2], scalar2=None, op0=ALU.mult),
              [b_h[t], b_st4[r]], [b_hn[r]])
        yield
        pt_, bpt_ = ps.rot()
        ptb = pt_[:].bitcast(BF16).rearrange("p (k c) -> p k c", k=8)
        for k in range(8):
            fw.op("pe", lambda e, k=k, r=r, ptb=ptb: e.transpose(out=ptb[:, k, :], in_=hn[r][:, k * 128:(k + 1) * 128], identity=idb[:]), [b_hn[r], b_idb], [bpt_])
        fw.op("act", lambda e, t=t, ptb=ptb: e.copy(out=HNT[:, :, t * 128:(t + 1) * 128], in_=ptb), [bpt_], [b_hnT[t]])
        hn_done.add(t)
        yield

    def rolling_gen(gens, width):
        gens = list(gens)
        active = []
        nxt = 0
        while nxt < len(gens) or active:
            while len(active) < width and nxt < len(gens):
                active.append(gens[nxt]); nxt += 1
            for g_ in list(active):
                try:
                    next(g_)
                except StopIteration:
                    active.remove(g_)
                    break
            yield

    hn_done = set()
    wloader = run_rolling([hn_tile(t) for t in range(4)], 2, bg=wloader)
    drain(wloader)
    hn_bg = rolling_gen([hn_tile(t) for t in range(4, NT)], 2)

    def step_hn():
        nonlocal hn_bg
        if hn_bg is not None:
            try:
                next(hn_bg)
            except StopIteration:
                hn_bg = None
    rc = 0
    for nb in range(NB):
        r = nb % 2
        loader = load_block_gen(nb + 1) if nb + 1 < NB else None
        for g in range(4):
            while hn_bg is not None and not all(t_ in hn_done for t_ in range(g * 4, g * 4 + 4)):
                step_hn()
            for c in range(8):
                pu, bpu = ps.rot()
                for k in range(8):
                    fw.op("pe", lambda e, k=k, c=c, r=r, g=g, pu=pu: e.matmul(pu[:, 0:512], lhsT=wu[r][:, k, c * 128:(c + 1) * 128],
                                                                             rhs=HNT[:, k, g * 512:(g + 1) * 512], start=(k == 0), stop=(k == 7)),
                          [b_wu[r][k]] + b_hnT[g * 4:(g + 1) * 4], [bpu])
                ri = rc % 2
                rc += 1
                fw.op("act", lambda e, ri=ri, pu=pu: e.activation(out=rl[ri][:], in_=pu[:, 0:512], func=AF.Relu), [bpu], [b_rl[ri]])
                fw.op("act", lambda e, ri=ri, c=c: e.activation(out=aT[:, c, :], in_=rl[ri][:], func=AF.Square), [b_rl[ri]], [b_aT[c]])
                if loader is not None and c % 2 == 1:
                    loader = step_loader(loader)
                if nb == 0:
                    step_hn()
            for tl in range(4):
                t = g * 4 + tl
                for half in range(2):
                    po, bpo = ps.acc(tl % 2 * 2 + half)
                    for c in range(8):
                        fw.op("pe", lambda e, c=c, tl=tl, r=r, half=half, po=po: e.matmul(po[:, 0:512], lhsT=aT[:, c, tl * 128:(tl + 1) * 128],
                                                                                         rhs=wd[r][:, c, half * 512:(half + 1) * 512],
                                                                                         start=(c == 0), stop=(c == 7)), [b_aT[c], b_wd[r][c]], [bpo])
                    fw.op("dve", lambda e, t=t, po=po, half=half: e.tensor_tensor(out=H[:, t, half * 512:(half + 1) * 512], in0=po[:, 0:512],
                                                                                  in1=H[:, t, half * 512:(half + 1) * 512], op=ALU.add),
                          [bpo, b_h[t]], [b_h[t]])
                if nb == NB - 1:
                    fw.dma(out[s, t * 128:(t + 1) * 128, :], H[:, t, :], reads=[b_h[t]])
        while loader is not None:
            loader = step_loader(loader)


def host_prepare(inputs):
    f = lambda a: np.ascontiguousarray(np.asarray(a), dtype=np.float32)
    w_in = f(inputs["w_in"])[0]
    o = np.cumsum([0, 256, 256, 64, 512, 512, 512, 512, 4, 4])
    perm = np.concatenate([np.arange(o[0], o[3]), np.arange(o[7], o[9]), np.arange(o[6], o[7]), np.arange(o[3], o[6])])
    w_in_p = np.ascontiguousarray(w_in[:, perm])
    pp = np.zeros((128, NPP), np.float32)
    pp[:, P_AN:P_AN + 8] = f(inputs["attn_norm_w"])[0].reshape(8, 128).T
    pp[:, P_QLN:P_QLN + 2] = f(inputs["q_lat_norm_w"])[0].reshape(2, 128).T
    pp[:, P_KVLN:P_KVLN + 2] = f(inputs["kv_lat_norm_w"])[0].reshape(2, 128).T
    pp[:, P_MN:P_MN + 8] = f(inputs["mlp_norm_w"])[0].reshape(8, 128).T
    cw = f(inputs["conv_w"])[0]
    pp[:, P_CW:P_CW + 48] = cw.reshape(4, 12, 128).transpose(2, 1, 0).reshape(128, 48)
    bc = np.zeros((128, NBC), np.float32)
    bc[:, B_QN:B_QN + 192] = f(inputs["q_norm_w"])[0][None, :]
    bc[:, B_KN:B_KN + 192] = f(inputs["k_norm_w"])[0][None, :]
    bc[:, B_MO:B_MO + 512] = f(inputs["mla_out_norm_w"])[0].reshape(-1)[None, :]
    bc[:, B_GN:B_GN + 128] = f(inputs["gdn_norm_w"])[0][None, :]
    bc[:, B_AL:B_AL + 4] = f(inputs["a_log"])[0][None, :]
    bc[:, B_DT:B_DT + 4] = f(inputs["dt_bias"])[0][None, :]
    cst = np.zeros((128, NK), np.float32)
    i = np.arange(128)
    cst[:, K_ID:K_ID + 128] = np.eye(128)
    cst[:, K_U:K_U + 128] = (i[:, None] <= i[None, :])
    cst[:, K_MNEG:K_MNEG + 128] = np.where(i[None, :] >= i[:, None], 0.0, -30000.0)
    cst[:, K_STR:K_STR + 128] = (i[None, :] > i[:, None])
    cst[:, K_INC:K_INC + 128] = (i[None, :] >= i[:, None])
    cst[:, K_ONE:K_ONE + 128] = 1.0
    half = 32
    inv_freq = (10000.0 ** (-(np.arange(half, dtype=np.float32) / np.float32(half)))).astype(np.float32)
    cst[:, K_IF:K_IF + 32] = inv_freq[None, :]
    x = f(inputs["x"])
    pos = np.asarray(inputs["positions"]).astype(np.int32)
    shared = {
        "w_in": w_in_p, "w_uq": f(inputs["w_uq"])[0], "w_ukv": f(inputs["w_ukv"])[0], "w_out": f(inputs["w_out"])[0],
        "w_up": f(inputs["w_up"])[0], "w_down": f(inputs["w_down"])[0], "pp": pp, "bc": bc, "cst": cst,
    }
    in_maps = []
    for c in range(NCORES):
        m = dict(shared)
        m["x"] = np.ascontiguousarray(x[2 * c:2 * c + 2])
        m["pos"] = np.ascontiguousarray(pos[2 * c:2 * c + 2].reshape(2, NT, 128).transpose(0, 2, 1))
        in_maps.append(m)
    return in_maps


_NC_CACHE = {}


def kernel(**inputs):
    in_maps = host_prepare(inputs)
    if "nc" not in _NC_CACHE:
        _NC_CACHE["nc"] = build_program()
    nc = _NC_CACHE["nc"]
    res = run_bass_kernel_spmd(nc, in_maps, core_ids=list(range(NCORES)))
    outs = [res.results[c]["out"] for c in range(NCORES)]
    return np.concatenate(outs, axis=0).astype(np.float32)
```

```python
import numpy as np
from contextlib import ExitStack
import concourse.bass as bass
import concourse.mybir as mybir
from concourse.bass_utils import run_bass_kernel_spmd

F32 = mybir.dt.float32
BF16 = mybir.dt.bfloat16
I32 = mybir.dt.int32
AF = mybir.ActivationFunctionType
ALU = mybir.AluOpType
AX = mybir.AxisListType

NCORES = 8
GDN_STOP = 99
NEU_LEVELS = 6
GDN_MAXTILE = 99
M1_BG = False
NEU_PINGPONG = True
SEQ = 2048
DM = 1024
NT = 16
DFF = 4096
EPS = 1e-6
PI = float(np.pi)
C_QL, C_KVL, C_KPE, C_A, C_B, C_Z, C_G = 0, 256, 512, 576, 580, 584, 1096
NW1 = 576
NW2 = 2632 - 576
B_QN, B_KN, B_MO, B_GN, B_AL, B_DT, NBC = 0, 192, 384, 896, 1024, 1028, 1032
P_AN, P_QLN, P_KVLN, P_MN, P_CW, NPP = 0, 8, 10, 12, 20, 68
K_ID, K_U, K_MNEG, K_STR, K_INC, K_ONE, K_IF, NK = 0, 128, 256, 384, 512, 640, 768, 800


class Buf:
    __slots__ = ("name", "last_w", "readers", "psum")

    def __init__(self, name):
        self.name = name
        self.last_w = None
        self.readers = []
        self.psum = False


class Op:
    __slots__ = ("eng", "fn", "deps", "signal", "dma", "tok")


class Chan:
    __slots__ = ("sem", "count", "last")


class Fw:
    ENG = ("pe", "act", "dve", "pool", "sp")

    def __init__(self, nc, stack, nchan=24):
        self.nc = nc
        self.ops = {e: [] for e in self.ENG}
        self.esem = {e: stack.enter_context(nc.semaphore("s_" + e)) for e in self.ENG}
        self.ecnt = {e: 0 for e in self.ENG}
        self.known = {e: {} for e in self.ENG}
        self.chans = []
        for i in range(nchan):
            c = Chan()
            c.sem = stack.enter_context(nc.semaphore(f"dch{i}"))
            c.count = 0
            c.last = None
            self.chans.append(c)
        self.nextchan = 0
        self.autoflush = True
        self.bufs = []
        self.pass_dmas = []
        self.ninst = 0
        self.nwait = 0

    def buf(self, name="b"):
        b = Buf(name)
        self.bufs.append(b)
        return b

    def bufs_n(self, name, n):
        return [self.buf(f"{name}{i}") for i in range(n)]

    def op(self, eng, fn, reads=(), writes=(), dma=False, extra=()):
        if self.autoflush and len(self.ops[eng]) >= 900:
            self.flush()
        o = Op()
        o.eng = eng; o.fn = fn; o.signal = False; o.dma = dma; o.tok = None
        deps = list(extra)
        for b in reads:
            if b.last_w is not None:
                deps.append(b.last_w)
            if b.psum:
                deps.extend(r for r in b.readers if r.eng != eng)
        for b in writes:
            if b.last_w is not None:
                deps.append(b.last_w)
            deps.extend(b.readers)
        dd = []
        seen = set()
        for d in deps:
            if id(d) in seen or d is None:
                continue
            seen.add(id(d))
            if eng == "pe" and d.eng == "pe" and not d.dma and not dma:
                continue
            dd.append(d)
        o.deps = dd
        for b in reads:
            b.readers.append(o)
        for b in writes:
            b.last_w = o
            b.readers = []
        self.ops[eng].append(o)
        return o

    def maybe_flush(self, limit=900):
        if max(len(v) for v in self.ops.values()) >= limit:
            self.flush()

    def flush(self):
        for b in self.bufs:
            if b.last_w is not None and not b.last_w.dma:
                b.last_w.signal = True
            for r in b.readers:
                if not r.dma:
                    r.signal = True
        self._emit()

    def dma(self, out, in_, reads=(), writes=(), eng="sp"):
        ch = self.chans[self.nextchan]
        self.nextchan = (self.nextchan + 1) % len(self.chans)
        ch.count += 16
        o = self.op(eng, lambda e: e.dma_start(out=out, in_=in_), reads=reads, writes=writes, dma=True,
                    extra=[ch.last])
        o.tok = (ch.sem, ch.count)
        ch.last = o
        self.pass_dmas.append(o)
        return o

    def end_pass(self):
        self.autoflush = False
        lasts = {}
        for e in self.ENG:
            for o in reversed(self.ops[e]):
                if not o.dma:
                    lasts[e] = o
                    break
        dma_last = [c.last for c in self.chans if c.last is not None]
        for f in self.ENG:
            extra = [lasts[e] for e in lasts if e != f] + dma_last
            self.op(f, lambda e: e.nop(), extra=extra)
        self._emit()
        self.autoflush = True
        for b in self.bufs:
            b.last_w = None
            b.readers = []
        for c in self.chans:
            c.last = None
        self.pass_dmas = []

    def _emit(self):
        for e in self.ENG:
            for o in self.ops[e]:
                for d in o.deps:
                    if not d.dma:
                        d.signal = True
        for e in self.ENG:
            for o in self.ops[e]:
                if (not o.dma) and o.signal and o.tok is None:
                    self.ecnt[e] += 1
                    o.tok = (self.esem[e], self.ecnt[e])
        fw = self
        with self.nc.Block() as block:
            def run(ename):
                def body(eng):
                    known = fw.known[ename]
                    for o in fw.ops[ename]:
                        need = {}
                        for d in o.deps:
                            assert d.tok is not None, f"dep without token on {ename}"
                            sem, val = d.tok
                            k = id(sem)
                            if known.get(k, 0) >= val:
                                continue
                            if k not in need or need[k][1] < val:
                                need[k] = (sem, val)
                        for k, (sem, val) in need.items():
                            eng.wait_ge(sem, val)
                            known[k] = val
                            fw.nwait += 1
                        ins = o.fn(eng)
                        fw.ninst += 1
                        if o.dma:
                            ins.then_inc(o.tok[0], 16)
                        elif o.signal:
                            ins.then_inc(o.tok[0], 1)
                return body
            block.tensor(run("pe"))
            block.scalar(run("act"))
            block.vector(run("dve"))
            block.gpsimd(run("pool"))
            block.sync(run("sp"))
        for e in self.ENG:
            self.ops[e] = []


class PS:
    def __init__(self, nc, fw, stack):
        self.big = stack.enter_context(nc.psum_tensor("psbig", [128, 8, 512], F32))
        self.t = [self.big[:, i, :] for i in range(8)]
        self.b = [fw.buf(f"psb{i}") for i in range(8)]
        for b in self.b:
            b.psum = True
        self.rb = 0

    def rot(self):
        i = 4 + self.rb
        self.rb = (self.rb + 1) % 4
        return self.t[i], self.b[i]

    def acc(self, i):
        return self.t[i], self.b[i]


def build_program(stages=("m1", "m2", "wout", "mlp"), nseq=2, dbg=False):
    nc = bass.Bass("TRN2", target_bir_lowering=False)

    def din(name, shape, dt=F32):
        return nc.dram_tensor(name, list(shape), dt, kind="ExternalInput").ap()
    x = din("x", [2, SEQ, DM])
    pos = din("pos", [2, 128, NT], I32)
    w_in = din("w_in", [DM, 2632])
    w_uq = din("w_uq", [256, 768])
    w_ukv = din("w_ukv", [256, 1024])
    w_out = din("w_out", [DM, DM])
    w_up = din("w_up", [DM, DFF])
    w_down = din("w_down", [DFF, DM])
    pp_d = din("pp", [128, NPP])
    bc_d = din("bc", [128, NBC])
    cst_d = din("cst", [128, NK])
    out = nc.dram_tensor("out", [2, SEQ, DM], F32, kind="ExternalOutput").ap()
    dbg_mix = None
    if dbg:
        dbg_mix = nc.dram_tensor("dbg_mix", [2, 128, NT * DM], BF16, kind="ExternalOutput").ap()

    with ExitStack() as gs:
        fw = Fw(nc, gs)
        ps = PS(nc, fw, gs)

        cnt = [0]

        def sb(st, name, shape, dt):
            cnt[0] += 1
            return st.enter_context(nc.sbuf_tensor(f"sb{cnt[0]}_{name}", list(shape), dt))

        cst = sb(gs, "cst", [128, NK], F32); b_cst = fw.buf("cst")
        ppt = sb(gs, "ppt", [128, NPP], F32); b_pp = fw.buf("pp")
        bct = sb(gs, "bct", [128, NBC], F32); b_bc = fw.buf("bc")
        idb = sb(gs, "idb", [128, 128], BF16); b_idb = fw.buf("idb")
        incb = sb(gs, "incb", [128, 128], BF16); b_incb = fw.buf("incb")
        fw.dma(cst[:], cst_d, writes=[b_cst])
        fw.dma(ppt[:], pp_d, writes=[b_pp])
        fw.dma(bct[:], bc_d, writes=[b_bc])
        fw.op("dve", lambda e: e.tensor_copy(out=idb[:], in_=cst[:, K_ID:K_ID + 128]), [b_cst], [b_idb])
        fw.op("dve", lambda e: e.tensor_copy(out=incb[:], in_=cst[:, K_INC:K_INC + 128]), [b_cst], [b_incb])
        fw.op("dve", lambda e: e.tensor_scalar(out=bct[:, B_QN:B_QN + 192], in0=bct[:, B_QN:B_QN + 192],
                                               scalar1=float(192 ** -0.5), scalar2=None, op0=ALU.mult), [b_bc], [b_bc])
        idf = cst[:, K_ID:K_ID + 128]
        fw.end_pass()

        def rstd_ops(eng_unused, ssq_ap, out_ap, scale, reads, writes, tmp_ap, b_tmp):
            fw.op("act", lambda e: e.activation(out=tmp_ap, in_=ssq_ap, func=AF.Ln, scale=scale, bias=EPS), reads, [b_tmp])
            fw.op("act", lambda e: e.activation(out=out_ap, in_=tmp_ap, func=AF.Exp, scale=-0.5), [b_tmp], writes)

        for s in range(nseq):
            with ExitStack() as ss:
                MIXR = sb(ss, "mixr", [128, NT * DM], BF16)
                MIX = MIXR[:].rearrange("p (t f) -> p t f", t=NT)
                HNT = MIXR[:].rearrange("p (k t) -> p k t", k=8)
                b_mix = fw.bufs_n("mix", NT)
                cs = sb(ss, "cs", [128, NT, 64], F32); b_cs = fw.buf("cs")

                if "m2" not in stages:
                    fw.op("pool", lambda e: e.memset(MIX[:, :, 512:1024], 0.0), [], b_mix)
                if "m1" in stages:
                    with ExitStack() as p1:
                        build_m1(nc, fw, ps, p1, sb, s, x, pos, w_in, w_uq, w_ukv, cst, ppt, bct, idb, incb,
                                 b_cst, b_pp, b_bc, b_idb, b_incb, MIX, b_mix, cs, b_cs, rstd_ops)
                        fw.end_pass()
                else:
                    fw.op("pool", lambda e: e.memset(MIXR[:], 0.0), [], b_mix)
                    fw.end_pass()
                if "m2" in stages:
                    with ExitStack() as p2:
                        build_m2(nc, fw, ps, p2, sb, s, x, w_in, cst, ppt, bct, idb, b_cst, b_pp, b_bc, b_idb, MIX, b_mix, rstd_ops)
                        fw.end_pass()
                if dbg:
                    fw.dma(dbg_mix[s], MIXR[:], reads=b_mix)
                    fw.end_pass()
                with ExitStack() as hs:
                    H = sb(hs, "H", [128, NT, DM], F32)
                    b_h = fw.bufs_n("h", NT)
                    if "wout" in stages:
                        with ExitStack() as p3:
                            build_wout(nc, fw, ps, p3, sb, s, x, w_out, idb, b_idb, MIX, b_mix, H, b_h)
                            fw.end_pass()
                    if "mlp" in stages:
                        with ExitStack() as p4:
                            build_mlp(nc, fw, ps, p4, sb, s, w_up, w_down, ppt, b_pp, idb, b_idb, HNT, H, b_h, out, rstd_ops)
                            fw.end_pass()
                    else:
                        for t in range(NT):
                            fw.dma(out[s, t * 128:(t + 1) * 128, :], H[:, t, :], reads=[b_h[t]])
                        fw.end_pass()
        print("instructions", fw.ninst, "waits", fw.nwait)
    return nc


def load_cast_gen(fw, st, sb, name, w_ap, nk, ncols, dst, b_dst, gain=None, b_gain=None, stg=None, engs=("act", "dve")):
    sw = min(int(stg[0][0].shape[1]), ncols)
    n = 0
    for k in range(nk):
        for c0 in range(0, ncols, sw):
            c1 = min(ncols, c0 + sw)
            stt, b_st = stg[n % len(stg)]
            eng = engs[n % len(engs)]
            n += 1
            fw.dma(stt[:, 0:c1 - c0], w_ap[k * 128:(k + 1) * 128, c0:c1], writes=[b_st])
            rd = [b_st] + ([b_gain] if gain is not None else [])
            if eng == "act":
                if gain is not None:
                    fw.op("act", lambda e, k=k, stt=stt, c0=c0, c1=c1: e.activation(out=dst[:, k, c0:c1], in_=stt[:, 0:c1 - c0], func=AF.Copy, scale=gain[:, k:k + 1]),
                          rd, [b_dst[k]])
                else:
                    fw.op("act", lambda e, k=k, stt=stt, c0=c0, c1=c1: e.activation(out=dst[:, k, c0:c1], in_=stt[:, 0:c1 - c0], func=AF.Copy), rd, [b_dst[k]])
            elif eng == "dve":
                if gain is not None:
                    fw.op("dve", lambda e, k=k, stt=stt, c0=c0, c1=c1: e.tensor_scalar(out=dst[:, k, c0:c1], in0=stt[:, 0:c1 - c0], scalar1=gain[:, k:k + 1], scalar2=None, op0=ALU.mult),
                          rd, [b_dst[k]])
                else:
                    fw.op("dve", lambda e, k=k, stt=stt, c0=c0, c1=c1: e.tensor_copy(out=dst[:, k, c0:c1], in_=stt[:, 0:c1 - c0]), rd, [b_dst[k]])
            else:
                if gain is not None:
                    fw.op("pool", lambda e, k=k, stt=stt, c0=c0, c1=c1: e.tensor_scalar(out=dst[:, k, c0:c1], in0=stt[:, 0:c1 - c0], scalar1=gain[:, k:k + 1], scalar2=1.0,
                                                                                     op0=ALU.mult, op1=ALU.mult), rd, [b_dst[k]])
                else:
                    fw.op("pool", lambda e, k=k, stt=stt, c0=c0, c1=c1: e.tensor_copy(out=dst[:, k, c0:c1], in_=stt[:, 0:c1 - c0]), rd, [b_dst[k]])
            yield


def load_cast(*a, **kw):
    for _ in load_cast_gen(*a, **kw):
        pass


def drain(g):
    if g is not None:
        for _ in g:
            pass


def run_rolling(gens, width=2, bg=None):
    gens = list(gens)
    active = []
    nxt = 0
    while nxt < len(gens) or active:
        while len(active) < width and nxt < len(gens):
            active.append(gens[nxt]); nxt += 1
        for g in list(active):
            try:
                next(g)
            except StopIteration:
                active.remove(g)
                break
        if bg is not None:
            try:
                next(bg)
            except StopIteration:
                bg = None
    return bg


def build_m1(nc, fw, ps, st, sb, s, x, pos, w_in, w_uq, w_ukv, cst, ppt, bct, idb, incb,
             b_cst, b_pp, b_bc, b_idb, b_incb, MIX, b_mix, cs, b_cs, rstd_ops):
    w1 = sb(st, "w1", [128, 8, NW1], BF16); b_w1 = fw.bufs_n("w1", 8)
    wuq = sb(st, "wuq", [128, 2, 768], BF16); b_wuq = fw.bufs_n("wuq", 2)
    wukv = sb(st, "wukv", [128, 2, 1024], BF16); b_wukv = fw.bufs_n("wukv", 2)
    stg = [(sb(st, f"stg{i}", [128, 1024], F32), fw.buf(f"stg{i}")) for i in range(2)]
    def m1_loader():
        yield from load_cast_gen(fw, st, sb, "w1", w_in[:, 0:NW1], 8, NW1, w1, b_w1, ppt[:, P_AN:P_AN + 8], b_pp, stg)
        yield from load_cast_gen(fw, st, sb, "wuq", w_uq, 2, 768, wuq, b_wuq, ppt[:, P_QLN:P_QLN + 2], b_pp, stg)
        yield from load_cast_gen(fw, st, sb, "wukv", w_ukv, 2, 1024, wukv, b_wukv, ppt[:, P_KVLN:P_KVLN + 2], b_pp, stg)
    wl = [m1_loader()]

    def need_weights():
        drain(wl[0])
        wl[0] = None

    posi = sb(st, "posi", [128, NT], I32); b_posi = fw.buf("posi")
    ang = sb(st, "ang", [128, NT, 32], F32); b_ang = fw.buf("ang")
    kq = sb(st, "kq", [128, NT, 32], F32); b_kq = fw.buf("kq")
    kqi = sb(st, "kqi", [128, NT, 32], I32); b_kqi = fw.buf("kqi")
    posf = sb(st, "posf", [128, NT], F32); b_posf = fw.buf("posf")
    fw.dma(posi[:], pos[s], writes=[b_posi])
    fw.op("dve", lambda e: e.tensor_copy(out=posf[:], in_=posi[:]), [b_posi], [b_posf])
    invf = cst[:, K_IF:K_IF + 32]
    fw.op("dve", lambda e: e.tensor_tensor(out=ang[:], in0=posf[:].unsqueeze(2).to_broadcast([128, NT, 32]),
                                           in1=invf.unsqueeze(1).to_broadcast([128, NT, 32]), op=ALU.mult),
          [b_posf, b_cst], [b_ang])
    fw.op("dve", lambda e: e.tensor_scalar(out=kq[:], in0=ang[:], scalar1=float(1.0 / (2 * PI)), scalar2=None, op0=ALU.mult), [b_ang], [b_kq])
    fw.op("dve", lambda e: e.tensor_copy(out=kqi[:], in_=kq[:]), [b_kq], [b_kqi])
    fw.op("dve", lambda e: e.tensor_copy(out=kq[:], in_=kqi[:]), [b_kqi], [b_kq])
    C1 = 6.28125
    C2 = float(2 * np.pi - 6.28125)
    fw.op("dve", lambda e: e.scalar_tensor_tensor(out=ang[:], in0=kq[:], scalar=-C1, in1=ang[:], op0=ALU.mult, op1=ALU.add), [b_kq, b_ang], [b_ang])
    fw.op("dve", lambda e: e.scalar_tensor_tensor(out=ang[:], in0=kq[:], scalar=-C2, in1=ang[:], op0=ALU.mult, op1=ALU.add), [b_kq, b_ang], [b_ang])
    fw.op("dve", lambda e: e.tensor_scalar(out=kq[:], in0=ang[:], scalar1=PI, scalar2=None, op0=ALU.is_gt), [b_ang], [b_kq])
    fw.op("dve", lambda e: e.scalar_tensor_tensor(out=ang[:], in0=kq[:], scalar=-2 * PI, in1=ang[:], op0=ALU.mult, op1=ALU.add), [b_kq, b_ang], [b_ang])
    fw.op("dve", lambda e: e.tensor_scalar(out=kq[:], in0=ang[:], scalar1=-PI, scalar2=None, op0=ALU.is_lt), [b_ang], [b_kq])
    fw.op("dve", lambda e: e.scalar_tensor_tensor(out=ang[:], in0=kq[:], scalar=2 * PI, in1=ang[:], op0=ALU.mult, op1=ALU.add), [b_kq, b_ang], [b_ang])
    fw.op("dve", lambda e: e.tensor_scalar(out=ang[:], in0=ang[:], scalar1=PI, scalar2=-PI, op0=ALU.min, op1=ALU.max), [b_ang], [b_ang])
    fw.op("act", lambda e: e.activation(out=cs[:, :, 32:64], in_=ang[:], func=AF.Sin), [b_ang], [b_cs])
    fw.op("act", lambda e: e.activation(out=kq[:], in_=ang[:], func=AF.Abs), [b_ang], [b_kq])
    fw.op("dve", lambda e: e.tensor_scalar(out=kq[:], in0=kq[:], scalar1=-1.0, scalar2=PI / 2, op0=ALU.mult, op1=ALU.add), [b_kq], [b_kq])
    fw.op("act", lambda e: e.activation(out=cs[:, :, 0:32], in_=kq[:], func=AF.Sin), [b_kq], [b_cs])
    cs2 = sb(st, "cs2", [128, NT, 128], F32); b_cs2 = fw.buf("cs2")
    fw.op("pool", lambda e: e.tensor_copy(out=cs2[:, :, 0:32], in_=cs[:, :, 0:32]), [b_cs], [b_cs2])
    fw.op("pool", lambda e: e.tensor_copy(out=cs2[:, :, 32:64], in_=cs[:, :, 0:32]), [b_cs], [b_cs2])
    fw.op("dve", lambda e: e.tensor_scalar(out=cs2[:, :, 64:96], in0=cs[:, :, 32:64], scalar1=-1.0, scalar2=None, op0=ALU.mult), [b_cs], [b_cs2])
    fw.op("pool", lambda e: e.tensor_copy(out=cs2[:, :, 96:128], in_=cs[:, :, 32:64]), [b_cs], [b_cs2])

    KT = sb(st, "KT", [128, 4, SEQ], BF16); b_kt = fw.bufs_n("kt", NT)
    KR = sb(st, "KR", [128, SEQ], BF16); b_kr = fw.bufs_n("kr", NT)
    fw.op("pool", lambda e: e.memset(KR[:], 0.0), [], b_kr)
    V = sb(st, "V", [128, NT, 4, 132], BF16); b_v = fw.bufs_n("v", NT)
    fw.op("pool", lambda e: e.memset(V[:], 1.0), [], b_v)
    xt = [sb(st, f"xt{i}", [128, DM], F32) for i in range(2)]; b_xt = fw.bufs_n("xt", 2)
    junk = sb(st, "junk", [128, DM], F32); b_junk = fw.buf("junk")
    xn = [sb(st, f"xn{i}", [128, DM], BF16) for i in range(2)]; b_xn = fw.bufs_n("xn", 2)
    xnT = [sb(st, f"xnT{i}", [128, 8, 128], BF16) for i in range(2)]; b_xnT = fw.bufs_n("xnT", 2)
    st8 = [sb(st, f"st8{i}", [128, 16], F32) for i in range(2)]; b_st8 = fw.bufs_n("st8", 2)
    tm8 = [sb(st, f"tm8{i}", [128, 16], F32) for i in range(2)]; b_tm8 = fw.bufs_n("tm8", 2)
    latn = [sb(st, f"latn{i}", [128, 512], BF16) for i in range(2)]; b_latn = fw.bufs_n("latn", 2)
    latT = [sb(st, f"latT{i}", [128, 4, 128], BF16) for i in range(2)]; b_latT = fw.bufs_n("latT", 2)
    kpe = [sb(st, f"kpe{i}", [128, 64], F32) for i in range(2)]; b_kpe = fw.bufs_n("kpe", 2)
    rtmp = [sb(st, f"rtmp{i}", [128, 4, 4, 32], F32) for i in range(2)]; b_rtmp = fw.bufs_n("rtmp", 2)
    krb = [sb(st, f"krb{i}", [128, 64], BF16) for i in range(2)]; b_krb = fw.bufs_n("krb", 2)
    qf = [sb(st, f"qf{i}", [128, 768], F32) for i in range(2)]; b_qf = fw.bufs_n("qf", 2)
    sq = [sb(st, f"sq{i}", [128, 1024], F32) for i in range(2)]; b_sq = fw.bufs_n("sq", 2)
    qb = [sb(st, f"qb{i}", [128, 4, 192], BF16) for i in range(2)]; b_qb = fw.bufs_n("qb", 2)
    kvf = [sb(st, f"kvf{i}", [128, 1024], F32) for i in range(2)]; b_kvf = fw.bufs_n("kvf", 2)
    kb = [sb(st, f"kb{i}", [128, 4, 128], BF16) for i in range(2)]; b_kb = fw.bufs_n("kb", 2)
    QT2 = [sb(st, f"QT{i}", [128, 4, 512], BF16) for i in range(2)]; b_qt2 = [fw.bufs_n(f"qt{i}", 4) for i in range(2)]
    QR2 = [sb(st, f"QR{i}", [128, 4, 512], BF16) for i in range(2)]; b_qr2 = [fw.bufs_n(f"qr{i}", 4) for i in range(2)]
    for i_ in range(2):
        fw.op("pool", lambda e, i_=i_: e.memset(QR2[i_][:], 0.0), [], b_qr2[i_])
    PT = [sb(st, f"PT{i}", [128, 512], BF16) for i in range(3)]; b_pt = fw.bufs_n("pt", 3)
    of = [sb(st, f"of{i}", [128, 128], F32) for i in range(2)]; b_of = fw.bufs_n("of", 2)
    ost = [sb(st, f"ost{i}", [128, 4], F32) for i in range(2)]; b_ost = fw.bufs_n("ost", 2)
    ptc = 0
    ofc = 0

    def proj_tile(t, tl, sti):
        nonlocal ptc, ofc
        t = sti * 4 + tl
        r = t % 2
        fw.dma(xt[r][:], x[s, t * 128:(t + 1) * 128, :], writes=[b_xt[r]])
        fw.op("act", lambda e, r=r: e.activation(out=junk[:], in_=xt[r][:], func=AF.Square, accum_out=st8[r][:, 0:1]),
              [b_xt[r]], [b_st8[r]])
        rstd_ops(None, st8[r][:, 0:1], st8[r][:, 1:2], 1.0 / DM, [b_st8[r]], [b_st8[r]], tm8[r][:, 0:1], b_tm8[r])
        fw.op("dve", lambda e, r=r: e.tensor_scalar(out=xn[r][:], in0=xt[r][:], scalar1=st8[r][:, 1:2], scalar2=None, op0=ALU.mult),
              [b_xt[r], b_st8[r]], [b_xn[r]])
        pt_, bpt_ = ps.rot()
        ptb = pt_[:].bitcast(BF16).rearrange("p (k c) -> p k c", k=8)
        for k in range(8):
            fw.op("pe", lambda e, k=k, r=r, ptb=ptb: e.transpose(out=ptb[:, k, :], in_=xn[r][:, k * 128:(k + 1) * 128], identity=idb[:]),
                  [b_xn[r], b_idb], [bpt_])
        fw.op("dve", lambda e, r=r, ptb=ptb: e.tensor_copy(out=xnT[r][:], in_=ptb), [bpt_], [b_xnT[r]])
        yield
        need_weights()
        pl, bpl = ps.rot()
        for k in range(8):
            fw.op("pe", lambda e, k=k, r=r, pl=pl: e.matmul(pl[:, 0:512], lhsT=xnT[r][:, k, :], rhs=w1[:, k, 0:512], start=(k == 0), stop=(k == 7)),
                  [b_xnT[r], b_w1[k]], [bpl])
        pk, bpk = ps.rot()
        for k in range(8):
            fw.op("pe", lambda e, k=k, r=r, pk=pk: e.matmul(pk[:, 0:64], lhsT=xnT[r][:, k, :], rhs=w1[:, k, 512:576], start=(k == 0), stop=(k == 7)),
                  [b_xnT[r], b_w1[k]], [bpk])
        fw.op("act", lambda e, r=r, pl=pl: e.activation(out=junk[:, 0:256], in_=pl[:, 0:256], func=AF.Square, accum_out=st8[r][:, 2:3]),
              [bpl], [b_st8[r]])
        fw.op("act", lambda e, r=r, pl=pl: e.activation(out=junk[:, 256:512], in_=pl[:, 256:512], func=AF.Square, accum_out=st8[r][:, 3:4]),
              [bpl], [b_st8[r]])
        fw.op("act", lambda e, r=r, pk=pk: e.activation(out=junk[:, 512:576], in_=pk[:, 0:64], func=AF.Square, accum_out=st8[r][:, 4:5]),
              [bpk], [b_st8[r]])
        rstd_ops(None, st8[r][:, 2:4], st8[r][:, 5:7], 1.0 / 256, [b_st8[r]], [b_st8[r]], tm8[r][:, 2:4], b_tm8[r])
        rstd_ops(None, st8[r][:, 4:5], st8[r][:, 7:8], 1.0 / 64, [b_st8[r]], [b_st8[r]], tm8[r][:, 4:5], b_tm8[r])
        fw.op("dve", lambda e, r=r, pl=pl: e.tensor_scalar(out=latn[r][:, 0:256], in0=pl[:, 0:256], scalar1=st8[r][:, 5:6], scalar2=None, op0=ALU.mult),
              [bpl, b_st8[r]], [b_latn[r]])
        fw.op("dve", lambda e, r=r, pl=pl: e.tensor_scalar(out=latn[r][:, 256:512], in0=pl[:, 256:512], scalar1=st8[r][:, 6:7], scalar2=None, op0=ALU.mult),
              [bpl, b_st8[r]], [b_latn[r]])
        fw.op("dve", lambda e, r=r, pk=pk: e.scalar_tensor_tensor(out=kpe[r][:], in0=pk[:, 0:64], scalar=st8[r][:, 7:8], in1=bct[:, B_KN + 128:B_KN + 192],
                                                                   op0=ALU.mult, op1=ALU.mult),
              [bpk, b_st8[r], b_bc], [b_kpe[r]])
        Rf = rtmp[r][:].rearrange("p a b c -> p (a b c)")
        CCt, SNt, SPt = cs2[:, t, 0:64], cs2[:, t, 64:96], cs2[:, t, 96:128]
        fw.op("pool", lambda e, r=r, Rf=Rf, CCt=CCt: e.tensor_tensor(out=Rf[:, 0:64], in0=kpe[r][:, 0:64], in1=CCt, op=ALU.mult), [b_kpe[r], b_cs2], [b_rtmp[r]])
        fw.op("dve", lambda e, r=r, Rf=Rf, SNt=SNt: e.tensor_tensor(out=Rf[:, 64:96], in0=kpe[r][:, 32:64], in1=SNt, op=ALU.mult), [b_kpe[r], b_cs2], [b_rtmp[r]])
        fw.op("dve", lambda e, r=r, Rf=Rf, SPt=SPt: e.tensor_tensor(out=Rf[:, 96:128], in0=kpe[r][:, 0:32], in1=SPt, op=ALU.mult), [b_kpe[r], b_cs2], [b_rtmp[r]])
        fw.op("pool", lambda e, r=r, Rf=Rf: e.tensor_tensor(out=krb[r][:, 0:64], in0=Rf[:, 0:64], in1=Rf[:, 64:128], op=ALU.add), [b_rtmp[r]], [b_krb[r]])
        yield
        pt2, bpt2 = ps.rot()
        pt2b = pt2[:].bitcast(BF16).rearrange("p (k c) -> p k c", k=8)
        for c in range(4):
            fw.op("pe", lambda e, c=c, r=r, pt2b=pt2b: e.transpose(out=pt2b[:, c, :], in_=latn[r][:, c * 128:(c + 1) * 128], identity=idb[:]),
                  [b_latn[r], b_idb], [bpt2])
        fw.op("pe", lambda e, r=r, pt2b=pt2b: e.transpose(out=pt2b[0:64, 4, :], in_=krb[r][:, 0:64], identity=idb[:]),
              [b_krb[r], b_idb], [bpt2])
        fw.op("act", lambda e, r=r, pt2b=pt2b: e.copy(out=latT[r][:], in_=pt2b[:, 0:4, :]), [bpt2], [b_latT[r]])
        fw.op("act", lambda e, t=t, pt2b=pt2b: e.copy(out=KR[0:64, t * 128:(t + 1) * 128], in_=pt2b[0:64, 4, :]), [bpt2], [b_kr[t]])
        yield
        pq0, bpq0 = ps.rot()
        pq1, bpq1 = ps.rot()
        for c in range(2):
            fw.op("pe", lambda e, c=c, r=r, pq0=pq0: e.matmul(pq0[:, 0:512], lhsT=latT[r][:, c, :], rhs=wuq[:, c, 0:512], start=(c == 0), stop=(c == 1)),
                  [b_latT[r], b_wuq[c]], [bpq0])
        for c in range(2):
            fw.op("pe", lambda e, c=c, r=r, pq1=pq1: e.matmul(pq1[:, 0:256], lhsT=latT[r][:, c, :], rhs=wuq[:, c, 512:768], start=(c == 0), stop=(c == 1)),
                  [b_latT[r], b_wuq[c]], [bpq1])
        fw.op("act", lambda e, r=r, pq0=pq0: e.copy(out=qf[r][:, 0:512], in_=pq0[:, 0:512]), [bpq0], [b_qf[r]])
        fw.op("act", lambda e, r=r, pq1=pq1: e.copy(out=qf[r][:, 512:768], in_=pq1[:, 0:256]), [bpq1], [b_qf[r]])
        pv0, bpv0 = ps.rot()
        pv1, bpv1 = ps.rot()
        for hh, (pv, bpv) in enumerate(((pv0, bpv0), (pv1, bpv1))):
            for c in range(2):
                fw.op("pe", lambda e, c=c, r=r, pv=pv, hh=hh: e.matmul(pv[:, 0:512], lhsT=latT[r][:, 2 + c, :], rhs=wukv[:, c, hh * 512:(hh + 1) * 512],
                                                                      start=(c == 0), stop=(c == 1)),
                      [b_latT[r], b_wukv[c]], [bpv])
            fw.op("dve", lambda e, r=r, pv=pv, hh=hh: e.tensor_copy(out=kvf[r][:, hh * 512:(hh + 1) * 512], in_=pv[:, 0:512]), [bpv], [b_kvf[r]])
        yield
        q3 = qf[r][:].rearrange("p (h d) -> p h d", h=4)
        s3 = sq[r][:, 0:768].rearrange("p (h d) -> p h d", h=4)
        fw.op("dve", lambda e, r=r: e.tensor_tensor(out=sq[r][:, 0:768], in0=qf[r][:], in1=qf[r][:], op=ALU.mult), [b_qf[r]], [b_sq[r]])
        fw.op("dve", lambda e, r=r, s3=s3: e.tensor_reduce(out=st8[r][:, 8:12], in_=s3[:, :, 0:128], axis=AX.X, op=ALU.add), [b_sq[r]], [b_st8[r]])
        fw.op("dve", lambda e, r=r, s3=s3: e.tensor_reduce(out=st8[r][:, 12:16], in_=s3[:, :, 128:192], axis=AX.X, op=ALU.add), [b_sq[r]], [b_st8[r]])
        rstd_ops(None, st8[r][:, 8:12], tm8[r][:, 8:12], 1.0 / 128, [b_st8[r]], [b_tm8[r]], st8[r][:, 8:12], b_st8[r])
        rstd_ops(None, st8[r][:, 12:16], tm8[r][:, 12:16], 1.0 / 64, [b_st8[r]], [b_tm8[r]], st8[r][:, 12:16], b_st8[r])
        yield
        s4 = sq[r][:, 0:768].rearrange("p (h d) -> p h d", h=4)
        fw.op("dve", lambda e, r=r, q3=q3, s4=s4: e.tensor_tensor(out=s4[:, :, 0:128], in0=q3[:, :, 0:128],
                                                                  in1=tm8[r][:, 8:12].unsqueeze(2).to_broadcast([128, 4, 128]), op=ALU.mult),
              [b_qf[r], b_tm8[r]], [b_sq[r]])
        fw.op("dve", lambda e, r=r, s4=s4: e.tensor_tensor(out=qb[r][:, :, 0:128], in0=s4[:, :, 0:128],
                                                            in1=bct[:, B_QN:B_QN + 128].unsqueeze(1).to_broadcast([128, 4, 128]), op=ALU.mult),
              [b_sq[r], b_bc], [b_qb[r]])
        fw.op("dve", lambda e, r=r, q3=q3, s4=s4: e.tensor_tensor(out=s4[:, :, 128:192], in0=q3[:, :, 128:192],
                                                                  in1=tm8[r][:, 12:16].unsqueeze(2).to_broadcast([128, 4, 64]), op=ALU.mult),
              [b_qf[r], b_tm8[r]], [b_sq[r]])
        fw.op("dve", lambda e, r=r, s4=s4: e.tensor_tensor(out=s4[:, :, 128:192], in0=s4[:, :, 128:192],
                                                            in1=bct[:, B_QN + 128:B_QN + 192].unsqueeze(1).to_broadcast([128, 4, 64]), op=ALU.mult),
              [b_sq[r], b_bc], [b_sq[r]])
        A4 = Rf[:, 0:256].rearrange("p (h d) -> p h d", h=4)
        B4 = Rf[:, 256:512].rearrange("p (h d) -> p h d", h=4)
        fw.op("pool", lambda e, s4=s4, A4=A4, CCt=CCt: e.tensor_tensor(out=A4, in0=s4[:, :, 128:192], in1=CCt.unsqueeze(1).to_broadcast([128, 4, 64]), op=ALU.mult),
              [b_sq[r], b_cs2], [b_rtmp[r]])
        fw.op("dve", lambda e, s4=s4, B4=B4, SNt=SNt: e.tensor_tensor(out=B4[:, :, 0:32], in0=s4[:, :, 160:192], in1=SNt.unsqueeze(1).to_broadcast([128, 4, 32]), op=ALU.mult),
              [b_sq[r], b_cs2], [b_rtmp[r]])
        fw.op("dve", lambda e, s4=s4, B4=B4, SPt=SPt: e.tensor_tensor(out=B4[:, :, 32:64], in0=s4[:, :, 128:160], in1=SPt.unsqueeze(1).to_broadcast([128, 4, 32]), op=ALU.mult),
              [b_sq[r], b_cs2], [b_rtmp[r]])
        fw.op("pool", lambda e, r=r, A4=A4, B4=B4: e.tensor_tensor(out=qb[r][:, :, 128:192], in0=A4, in1=B4, op=ALU.add), [b_rtmp[r]], [b_qb[r]])
        yield
        pt3, bpt3 = ps.rot()
        pt3b = pt3[:].bitcast(BF16).rearrange("p (k c) -> p k c", k=8)
        for h in range(4):
            fw.op("pe", lambda e, h=h, r=r, pt3b=pt3b: e.transpose(out=pt3b[:, h, :], in_=qb[r][:, h, 0:128], identity=idb[:]), [b_qb[r], b_idb], [bpt3])
        for h in range(4):
            fw.op("pe", lambda e, h=h, r=r, pt3b=pt3b: e.transpose(out=pt3b[0:64, 4 + h, :], in_=qb[r][:, h, 128:192], identity=idb[:]), [b_qb[r], b_idb], [bpt3])
        fw.op("act", lambda e, tl=tl, pt3b=pt3b: e.copy(out=QT2[sti % 2][:, :, tl * 128:(tl + 1) * 128], in_=pt3b[:, 0:4, :]), [bpt3], [b_qt2[sti % 2][tl]])
        fw.op("act", lambda e, tl=tl, pt3b=pt3b: e.copy(out=QR2[sti % 2][0:64, :, tl * 128:(tl + 1) * 128], in_=pt3b[0:64, 4:8, :]), [bpt3], [b_qr2[sti % 2][tl]])
        yield
        k3 = kvf[r][:].rearrange("p (h d) -> p h d", h=4)
        sk = sq[r][:, 0:512].rearrange("p (h d) -> p h d", h=4)
        fw.op("dve", lambda e, k3=k3, sk=sk: e.tensor_tensor(out=sk, in0=k3[:, :, 0:128], in1=k3[:, :, 0:128], op=ALU.mult), [b_kvf[r]], [b_sq[r]])
        fw.op("dve", lambda e, r=r, sk=sk: e.tensor_reduce(out=st8[r][:, 8:12], in_=sk, axis=AX.X, op=ALU.add), [b_sq[r]], [b_st8[r]])
        rstd_ops(None, st8[r][:, 8:12], tm8[r][:, 8:12], 1.0 / 128, [b_st8[r]], [b_tm8[r]], st8[r][:, 8:12], b_st8[r])
        fw.op("dve", lambda e, r=r, k3=k3, sk=sk: e.tensor_tensor(out=sk, in0=k3[:, :, 0:128],
                                                                  in1=tm8[r][:, 8:12].unsqueeze(2).to_broadcast([128, 4, 128]), op=ALU.mult),
              [b_kvf[r], b_tm8[r]], [b_sq[r]])
        fw.op("dve", lambda e, r=r, sk=sk: e.tensor_tensor(out=kb[r][:], in0=sk,
                                                            in1=bct[:, B_KN:B_KN + 128].unsqueeze(1).to_broadcast([128, 4, 128]), op=ALU.mult),
              [b_sq[r], b_bc], [b_kb[r]])
        fw.op("dve", lambda e, t=t, k3=k3: e.tensor_copy(out=V[:, t, :, 0:128], in_=k3[:, :, 128:256]), [b_kvf[r]], [b_v[t]])
        pt4, bpt4 = ps.rot()
        pt4b = pt4[:].bitcast(BF16).rearrange("p (k c) -> p k c", k=8)
        for h in range(4):
            fw.op("pe", lambda e, h=h, r=r, pt4b=pt4b: e.transpose(out=pt4b[:, h, :], in_=kb[r][:, h, :], identity=idb[:]), [b_kb[r], b_idb], [bpt4])
        fw.op("act", lambda e, t=t, pt4b=pt4b: e.copy(out=KT[:, :, t * 128:(t + 1) * 128], in_=pt4b[:, 0:4, :]), [bpt4], [b_kt[t]])

        yield

    def attn(sti):
        nonlocal ptc, ofc
        qp = sti % 2
        nkt = sti * 4 + 4
        for h in range(4):
            oacc = [ps.acc(i) for i in range(4)]
            def issue_qk(j, h=h):
                c0 = max(j, sti * 4) - sti * 4
                ncol = (4 - c0) * 128
                sp_, bsp_ = ps.rot()
                rds = [b_kt[j], b_kr[j]] + [b_qt2[qp][i] for i in range(c0, 4)] + [b_qr2[qp][i] for i in range(c0, 4)]
                fw.op("pe", lambda e, h=h, j=j, c0=c0, ncol=ncol, sp_=sp_, qp=qp: e.matmul(sp_[:, 0:ncol], lhsT=KT[:, h, j * 128:(j + 1) * 128],
                                                                                   rhs=QT2[qp][:, h, c0 * 128:512], start=True, stop=False), rds, [bsp_])
                fw.op("pe", lambda e, h=h, j=j, c0=c0, ncol=ncol, sp_=sp_, qp=qp: e.matmul(sp_[:, 0:ncol], lhsT=KR[:, j * 128:(j + 1) * 128],
                                                                                   rhs=QR2[qp][:, h, c0 * 128:512], start=False, stop=True), rds, [bsp_])
                return sp_, bsp_, c0, ncol

            nxt = issue_qk(0)
            for j in range(nkt):
                sp_, bsp_, c0, ncol = nxt
                if j + 1 < nkt:
                    nxt = issue_qk(j + 1)
                pi = ptc % 3
                ptc += 1
                fw.op("act", lambda e, pi=pi, ncol=ncol, sp_=sp_: e.activation(out=PT[pi][:, 0:ncol], in_=sp_[:, 0:ncol], func=AF.Exp), [bsp_], [b_pt[pi]])
                if j >= sti * 4:
                    fw.op("pool", lambda e, pi=pi: e.tensor_tensor(out=PT[pi][:, 0:128], in0=PT[pi][:, 0:128], in1=incb[:], op=ALU.mult),
                          [b_pt[pi], b_incb], [b_pt[pi]])
                for qi in range(c0, 4):
                    po, bpo = oacc[qi]
                    fw.op("pe", lambda e, pi=pi, qi=qi, c0=c0, j=j, h=h, po=po, sti=sti: e.matmul(po[:, 0:129], lhsT=PT[pi][:, (qi - c0) * 128:(qi - c0 + 1) * 128],
                                                                                        rhs=V[:, j, h, 0:129], start=(j == 0), stop=(j == sti * 4 + qi)),
                          [b_pt[pi], b_v[j]], [bpo])
                if M1_BG:
                    yield
            for qi in range(4):
                t = sti * 4 + qi
                po, bpo = oacc[qi]
                oi = ofc % 2
                ofc += 1
                fw.op("dve", lambda e, oi=oi, po=po: e.reciprocal(out=ost[oi][:, 0:1], in_=po[:, 128:129]), [bpo], [b_ost[oi]])
                fw.op("dve", lambda e, oi=oi, po=po: e.tensor_scalar(out=of[oi][:], in0=po[:, 0:128], scalar1=ost[oi][:, 0:1], scalar2=None, op0=ALU.mult),
                      [bpo, b_ost[oi]], [b_of[oi]])
                fw.op("act", lambda e, oi=oi: e.activation(out=junk[:, 0:128], in_=of[oi][:], func=AF.Square, accum_out=ost[oi][:, 1:2]),
                      [b_of[oi]], [b_ost[oi]])
                rstd_ops(None, ost[oi][:, 1:2], ost[oi][:, 2:3], 1.0 / 128, [b_ost[oi]], [b_ost[oi]], ost[oi][:, 3:4], b_ost[oi])
                fw.op("dve", lambda e, oi=oi, t=t, h=h: e.scalar_tensor_tensor(out=MIX[:, t, h * 128:(h + 1) * 128], in0=of[oi][:], scalar=ost[oi][:, 2:3],
                                                                              in1=bct[:, B_MO + h * 128:B_MO + (h + 1) * 128], op0=ALU.mult, op1=ALU.mult),
                      [b_of[oi], b_ost[oi], b_bc], [b_mix[t]])


                yield

    def run_strands(strands, bg=None, bg_steps=1):
        strands = list(strands)
        while strands:
            for g in list(strands):
                try:
                    next(g)
                except StopIteration:
                    strands.remove(g)
            if bg is not None:
                for _ in range(bg_steps):
                    try:
                        next(bg)
                    except StopIteration:
                        bg = None
                        break
        return bg

    for sti in range(4):
        run_rolling([proj_tile(sti * 4 + tl, tl, sti) for tl in range(4)], 2, bg=wl[0])
        need_weights()
        run_strands([attn(sti)])


def build_m2(nc, fw, ps, st, sb, s, x, w_in, cst, ppt, bct, idb, b_cst, b_pp, b_bc, b_idb, MIX, b_mix, rstd_ops):
    idf = cst[:, K_ID:K_ID + 128]
    Uf = cst[:, K_U:K_U + 128]
    onesf = cst[:, K_ONE:K_ONE + 128]
    mneg = cst[:, K_MNEG:K_MNEG + 128]
    strf = cst[:, K_STR:K_STR + 128]
    w2 = sb(st, "w2", [128, 8, NW2], BF16); b_w2 = fw.bufs_n("w2", 8)
    stg = [(sb(st, f"stg2{i}", [128, NW2 // 2], F32), fw.buf(f"stg2{i}")) for i in range(2)]
    wl = [load_cast_gen(fw, st, sb, "w2", w_in[:, NW1:NW1 + NW2], 8, NW2, w2, b_w2, ppt[:, P_AN:P_AN + 8], b_pp, stg)]

    def need_weights():
        drain(wl[0])
        wl[0] = None
    O_AB, O_Z, O_GQ = 0, 8, 520
    xt = [sb(st, f"xt{i}", [128, DM], F32) for i in range(2)]; b_xt = fw.bufs_n("xt", 2)
    junk = sb(st, "junk", [128, DM], F32)
    xn = [sb(st, f"xn{i}", [128, DM], BF16) for i in range(2)]; b_xn = fw.bufs_n("xn", 2)
    xnT = sb(st, "xnT", [128, 8, 512], BF16); b_xnT = fw.bufs_n("xnT", 4)
    st8 = [sb(st, f"st8{i}", [128, 4], F32) for i in range(2)]; b_st8 = fw.bufs_n("st8", 2)
    ab = sb(st, "ab", [128, 4, 8], F32); b_ab = fw.bufs_n("ab", 4)
    gb = sb(st, "gb", [128, 4, 16], F32); b_gb = fw.bufs_n("gb", 4)
    zsg = sb(st, "zsg", [128, 4, 512], F32); b_zsg = fw.bufs_n("zsg", 4)
    Xc = [sb(st, f"Xc{i}", [128, 515], F32) for i in range(2)]; b_Xc = fw.bufs_n("Xc", 2)
    yacc = [sb(st, f"yacc{i}", [128, 512], F32) for i in range(2)]; b_yacc = fw.bufs_n("yacc", 2)
    halo = sb(st, "halo", [128, 12, 4], F32); b_halo = fw.bufs_n("halo", 12)
    Y = sb(st, "Y", [128, 12, 512], BF16); b_Y = fw.bufs_n("Y", 12)
    Sf = sb(st, "Sf", [128, 4, 128], F32); b_Sf = fw.bufs_n("Sf", 4)
    Sb = sb(st, "Sb", [128, 4, 128], BF16); b_Sb = fw.bufs_n("Sb", 4)
    fw.op("pool", lambda e: e.memset(halo[:], 0.0), [], b_halo)
    fw.op("pool", lambda e: e.memset(Sf[:], 0.0), [], b_Sf)
    fw.op("pool", lambda e: e.memset(Sb[:], 0.0), [], b_Sb)
    def two(f):
        return [f(0), f(1)]
    Gs2 = two(lambda q: sb(st, f"Gs{q}", [128, 24], F32)); b_Gs2 = fw.bufs_n("Gs", 2)
    sc2 = two(lambda q: sb(st, f"sc{q}", [128, 40], F32)); b_sc2 = fw.bufs_n("sc", 2)
    so2 = two(lambda q: sb(st, f"so{q}", [128, 12], F32)); b_so2 = fw.bufs_n("so", 2)
    g3f2 = two(lambda q: sb(st, f"g3f{q}", [128, 3, 4], F32)); b_g3f2 = fw.bufs_n("g3f", 2)
    g3b2 = two(lambda q: sb(st, f"g3b{q}", [128, 3, 4], BF16)); b_g3b2 = fw.bufs_n("g3b", 2)
    gr2 = two(lambda q: sb(st, f"gr{q}", [128, 8], F32)); b_gr2 = fw.bufs_n("gr", 2)
    NS = 2 if NEU_PINGPONG else 7

    def per_head(name, shape, dt):
        arrs = two(lambda q: sb(st, f"{name}{q}", [128, 4] + list(shape[1:]), dt))
        views = [[arrs[q][:, h] for h in range(4)] for q in range(2)]
        return views, two(lambda q: fw.bufs_n(f"{name}{q}", 4)), arrs

    def per_head_ns(name, shape, dt):
        arrs = two(lambda q: [sb(st, f"{name}{q}_{i}", [128, 4] + list(shape[1:]), dt) for i in range(NS)])
        views = [[[arrs[q][i][:, h] for i in range(NS)] for h in range(4)] for q in range(2)]
        return views, two(lambda q: [fw.bufs_n(f"{name}{q}{h}", NS) for h in range(4)]), arrs
    k6s, b_k6s, k6A = per_head("k6", [128, 6, 128], BF16)
    kqTs, b_kqTs, kqTA = per_head("kqT", [128, 3, 128], BF16)
    Ug3s, b_Ugs, Ug3A = per_head("Ug3", [128, 3, 128], BF16)
    tDs, b_tDs, tDA = per_head("tD", [128, 128], F32)
    dTs, b_dTs, dTA = per_head("dT", [128, 128], F32)
    dSs, b_dSs, dSA = per_head("dS", [128, 128], F32)
    Mms, b_Mms, MmA = per_head_ns("Mm", [128, 128], BF16)
    MmTs, b_MmTs, MmTA = per_head_ns("MmT", [128, 128], BF16)
    Pms, b_Pms, PmA = per_head_ns("Pm", [128, 128], BF16)
    aTs, b_aTs, aTA = per_head("attT", [128, 128], BF16)
    Ubs, b_Ubs, UbA = per_head("Ub", [128, 128], F32)
    WTs, b_WTs, WTA = per_head("WT", [128, 128], BF16)
    vns, b_vns, vnA = per_head("vn", [128, 128], BF16)
    ubf = sb(st, "ubf", [128, 128], BF16); b_ubf = fw.buf("ubf")
    onb = sb(st, "onb", [128, 128], BF16); b_onb = fw.buf("onb")
    fw.op("dve", lambda e: e.tensor_copy(out=ubf[:], in_=Uf), [b_cst], [b_ubf])
    fw.op("dve", lambda e: e.tensor_copy(out=onb[:], in_=onesf), [b_cst], [b_onb])
    stage = [0]

    def bank(h):
        i = (stage[0] % 2) * 4 + h
        return ps.t[i], ps.b[i]

    def next_stage():
        stage[0] += 1

    s8_done = set()

    def gdn_tile(sti, tl):
        sel = tl % 2
        Gs, sc, so, g3f, g3b, gr = Gs2[sel], sc2[sel], so2[sel], g3f2[sel], g3b2[sel], gr2[sel]
        b_Gs, b_sc, b_so, b_g3f, b_g3b, b_gr = b_Gs2[sel], b_sc2[sel], b_so2[sel], b_g3f2[sel], b_g3b2[sel], b_gr2[sel]
        k6, kqT, Ug3, tD, dT, dS, Mm, MmT, Pm, aT, Ub, WT, vn = [x_[sel] for x_ in (k6s, kqTs, Ug3s, tDs, dTs, dSs, Mms, MmTs, Pms, aTs, Ubs, WTs, vns)]
        b_k6, b_kqT, b_Ug, b_tD, b_dT, b_dS, b_Mm, b_MmT, b_Pm, b_aT, b_Ub, b_WT, b_vn = [x_[sel] for x_ in (b_k6s, b_kqTs, b_Ugs, b_tDs, b_dTs, b_dSs, b_Mms, b_MmTs, b_Pms, b_aTs, b_Ubs, b_WTs, b_vns)]

        def bank(h):
            return ps.t[sel * 4 + h], ps.b[sel * 4 + h]
        bpall = [ps.b[sel * 4 + h] for h in range(4)]
        G4 = ps.big[:, sel * 4:(sel + 1) * 4, :]
        G4b = G4.bitcast(BF16)
        k6a, kqTa, Ug3a, tDa, dTa, dSa, aTa, Uba, WTa = k6A[sel], kqTA[sel], Ug3A[sel], tDA[sel], dTA[sel], dSA[sel], aTA[sel], UbA[sel], WTA[sel]
        Mma, MmTa, Pma = MmA[sel], MmTA[sel], PmA[sel]

        def bc(ap4):
            return ap4.unsqueeze(2).to_broadcast([128, 4, 128])

        def allb(bl, i=None):
            return [bl[h] if i is None else bl[h][i] for h in range(4)]
        t = sti * 4 + tl
        cols = slice(tl * 128, (tl + 1) * 128)
        G_ = gb[:, tl, :]
        if t > GDN_MAXTILE:
            return
        fw.op("dve", lambda e, G_=G_: e.tensor_copy(out=g3b[:, 0, :], in_=G_[:, 0:4]), [b_gb[tl]], [b_g3b])
        fw.op("dve", lambda e: e.tensor_copy(out=g3f[:, 0, :], in_=g3b[:, 0, :]), [b_g3b], [b_g3f])
        fw.op("dve", lambda e, G_=G_: e.tensor_tensor(out=gr[:, 0:4], in0=G_[:, 0:4], in1=g3f[:, 0, :], op=ALU.subtract), [b_gb[tl], b_g3f], [b_gr])
        fw.op("dve", lambda e: e.tensor_copy(out=g3b[:, 1, :], in_=gr[:, 0:4]), [b_gr], [b_g3b])
        fw.op("dve", lambda e: e.tensor_copy(out=g3f[:, 1, :], in_=g3b[:, 1, :]), [b_g3b], [b_g3f])
        fw.op("dve", lambda e: e.tensor_tensor(out=gr[:, 4:8], in0=gr[:, 0:4], in1=g3f[:, 1, :], op=ALU.subtract), [b_gr, b_g3f], [b_gr])
        fw.op("dve", lambda e: e.tensor_copy(out=g3b[:, 2, :], in_=gr[:, 4:8]), [b_gr], [b_g3b])
        fw.op("dve", lambda e: e.tensor_copy(out=g3f[:, 2, :], in_=g3b[:, 2, :]), [b_g3b], [b_g3f])
        pg, bpg = ps.rot()
        for i in range(3):
            fw.op("pe", lambda e, pg=pg, i=i: e.matmul(pg[:, 0:4], lhsT=ubf[:], rhs=g3b[:, i, :], start=(i == 0), stop=(i == 2)), [b_ubf, b_g3b], [bpg])
        for i in range(3):
            fw.op("pe", lambda e, pg=pg, i=i: e.matmul(pg[:, 4:8], lhsT=onb[:], rhs=g3b[:, i, :], start=(i == 0), stop=(i == 2), skip_group_check=True), [b_onb, b_g3b], [bpg])
        fw.op("dve", lambda e, pg=pg: e.tensor_copy(out=Gs[:, 0:8], in_=pg[:, 0:8]), [bpg], [b_Gs])
        fw.op("act", lambda e: e.activation(out=Gs[:, 8:12], in_=Gs[:, 0:4], func=AF.Exp), [b_Gs], [b_Gs])
        fw.op("dve", lambda e: e.tensor_tensor(out=Gs[:, 20:24], in0=Gs[:, 4:8], in1=Gs[:, 0:4], op=ALU.subtract), [b_Gs], [b_Gs])
        fw.op("act", lambda e: e.activation(out=Gs[:, 12:16], in_=Gs[:, 20:24], func=AF.Exp), [b_Gs], [b_Gs])
        fw.op("act", lambda e: e.activation(out=Gs[:, 16:20], in_=Gs[:, 4:8], func=AF.Exp), [b_Gs], [b_Gs])
        yield
        B1 = [bank(h) for h in range(4)]
        for h in range(4):
            p1, bp1 = B1[h]
            p1b = p1[:].bitcast(BF16).rearrange("p (k c) -> p k c", k=8)
            for i, c in enumerate((h, 4 + h, 8 + h)):
                fw.op("pe", lambda e, p1b=p1b, i=i, c=c, cols=cols: e.transpose(out=p1b[:, i, :], in_=Y[:, c, cols], identity=idb[:]), [b_Y[c], b_idb], [bp1])
            fw.op("act", lambda e, p1b=p1b, h=h: e.activation(out=junk[:, 0:128], in_=p1b[:, 0, :], func=AF.Square, accum_out=sc[:, h:h + 1]), [bp1], [b_sc])
            fw.op("act", lambda e, p1b=p1b, h=h: e.activation(out=junk[:, 128:256], in_=p1b[:, 1, :], func=AF.Square, accum_out=sc[:, 4 + h:5 + h]), [bp1], [b_sc])
        fw.op("act", lambda e: e.activation(out=sc[:, 8:16], in_=sc[:, 0:8], func=AF.Ln, bias=EPS), [b_sc], [b_sc])
        fw.op("act", lambda e: e.activation(out=sc[:, 16:24], in_=sc[:, 8:16], func=AF.Exp, scale=-0.5), [b_sc], [b_sc])
        fw.op("dve", lambda e: e.tensor_tensor(out=sc[:, 24:28], in0=sc[:, 20:24], in1=Gs[:, 8:12], op=ALU.mult), [b_sc, b_Gs], [b_sc])
        fw.op("dve", lambda e: e.tensor_tensor(out=sc[:, 28:32], in0=sc[:, 20:24], in1=Gs[:, 12:16], op=ALU.mult), [b_sc, b_Gs], [b_sc])
        fw.op("dve", lambda e: e.tensor_scalar(out=sc[:, 32:36], in0=sc[:, 16:20], scalar1=float(128 ** -0.5), scalar2=None, op0=ALU.mult), [b_sc], [b_sc])
        fw.op("dve", lambda e: e.tensor_tensor(out=sc[:, 36:40], in0=sc[:, 32:36], in1=Gs[:, 8:12], op=ALU.mult), [b_sc, b_Gs], [b_sc])
        kps4, qps4, vps4 = G4b[:, :, 128:256], G4b[:, :, 0:128], G4b[:, :, 256:384]
        fw.op("act", lambda e: e.copy(out=k6a[:, :, 5, :], in_=vps4), bpall, b_k6)
        for i_, (src4, c0_) in enumerate(((kps4, 20), (kps4, 24), (kps4, 28), (qps4, 32), (qps4, 36))):
            fw.op("dve", lambda e, i_=i_, src4=src4, c0_=c0_: e.tensor_tensor(out=k6a[:, :, i_, :], in0=src4, in1=bc(sc[:, c0_:c0_ + 4]), op=ALU.mult), bpall + [b_sc], b_k6)
        for i in range(3):
            fw.op("pool", lambda e, i=i: e.tensor_tensor(out=Ug3a[:, :, i, :], in0=Uf.unsqueeze(1).to_broadcast([128, 4, 128]), in1=bc(g3f[:, i, :]), op=ALU.mult),
                  [b_cst, b_g3f], b_Ug)
        if GDN_STOP <= 1:
            return
        yield
        for h in range(4):
            p2, bp2 = bank(h)
            p2b = p2[:].bitcast(BF16).rearrange("p (k c) -> p k c", k=8)
            for i, src in enumerate((0, 3, 4)):
                fw.op("pe", lambda e, p2b=p2b, i=i, src=src, h=h: e.transpose(out=p2b[:, i, :], in_=k6[h][:, src, :], identity=idb[:]), [b_k6[h], b_idb], [bp2])
        fw.op("act", lambda e: e.copy(out=kqTa[:].rearrange("p h k c -> p h (k c)"), in_=G4b[:, :, 0:384]), bpall, b_kqT)
        if GDN_STOP <= 2:
            return
        yield
        for h in range(4):
            p3, bp3 = bank(h)
            fw.op("pe", lambda e, p3=p3, h=h: e.matmul(p3[:, 0:256], lhsT=kqT[h][:, 0, :], rhs=kqT[h][:, 0:2, :], start=True, stop=True), [b_kqT[h]], [bp3])
            for i in range(3):
                fw.op("pe", lambda e, p3=p3, h=h, i=i: e.matmul(p3[:, 256:384], lhsT=onb[:], rhs=Ug3[h][:, i, :], start=(i == 0), stop=(i == 2), skip_group_check=True), [b_onb, b_Ug[h]], [bp3])
            fw.op("dve", lambda e, p3=p3, h=h: e.scalar_tensor_tensor(out=tD[h][:], in0=p3[:, 256:384], scalar=Gs[:, h:h + 1], in1=mneg, op0=ALU.subtract, op1=ALU.add),
                  [bp3, b_Gs, b_cst], [b_tD[h]])
        fw.op("act", lambda e: e.activation(out=dTa[:], in_=tDa[:], func=AF.Exp), b_tD, b_dT)
        fw.op("dve", lambda e: e.tensor_tensor(out=dSa[:], in0=dTa[:], in1=strf.unsqueeze(1).to_broadcast([128, 4, 128]), op=ALU.mult), b_dT + [b_cst], b_dS)
        for h in range(4):
            p3, bp3 = bank(h)
            fw.op("dve", lambda e, p3=p3, h=h, G_=G_: e.scalar_tensor_tensor(out=Mm[h][0][:], in0=p3[:, 0:128], scalar=G_[:, 8 + h:9 + h], in1=dS[h][:], op0=ALU.mult, op1=ALU.mult),
                  [bp3, b_gb[tl], b_dS[h]], [b_Mm[h][0]])
        fw.op("dve", lambda e: e.tensor_tensor(out=aTa[:], in0=G4[:, :, 128:256], in1=dTa[:], op=ALU.mult), bpall + b_dT, b_aT)
        fw.op("dve", lambda e: e.tensor_tensor(out=Pma[0][:], in0=Mma[0][:], in1=idb[:].unsqueeze(1).to_broadcast([128, 4, 128]), op=ALU.add), allb(b_Mm, 0) + [b_idb], allb(b_Pm, 0))
        if GDN_STOP <= 3:
            return
        yield
        for h in range(4):
            p4, bp4 = bank(h)
            p4b = p4[:].bitcast(BF16)
            fw.op("pe", lambda e, p4b=p4b, h=h: e.transpose(out=p4b[:, 0:128], in_=Mm[h][0][:], identity=idb[:]), [b_Mm[h][0], b_idb], [bp4])
        fw.op("act", lambda e: e.copy(out=MmTa[0][:], in_=G4b[:, :, 0:128]), bpall, allb(b_MmT, 0))
        if GDN_STOP <= 4:
            return
        for lvl in range(NEU_LEVELS):
            a, b = (lvl % 2, (lvl + 1) % 2) if NEU_PINGPONG else (lvl, lvl + 1)
            yield
            for h in range(4):
                p5, bp5 = bank(h)
                if lvl < NEU_LEVELS - 1:
                    fw.op("pe", lambda e, p5=p5, h=h, a=a: e.matmul(p5[:, 0:128], lhsT=MmT[h][a][:], rhs=Mm[h][a][:], start=True, stop=True), [b_MmT[h][a], b_Mm[h][a]], [bp5])
                fw.op("pe", lambda e, p5=p5, h=h, a=a: e.matmul(p5[:, 128:256], lhsT=Mm[h][a][:], rhs=MmT[h][a][:], start=True, stop=True), [b_MmT[h][a], b_Mm[h][a]], [bp5])
            ev = "act"
            if lvl < NEU_LEVELS - 1:
                if ev == "act":
                    fw.op("act", lambda e, b=b: e.copy(out=Mma[b][:], in_=G4[:, :, 0:128]), bpall, allb(b_Mm, b))
                else:
                    fw.op("dve", lambda e, b=b: e.tensor_copy(out=Mma[b][:], in_=G4[:, :, 0:128]), bpall, allb(b_Mm, b))
            if ev == "act":
                fw.op("act", lambda e, b=b: e.copy(out=MmTa[b][:], in_=G4[:, :, 128:256]), bpall, allb(b_MmT, b))
            else:
                fw.op("dve", lambda e, b=b: e.tensor_copy(out=MmTa[b][:], in_=G4[:, :, 128:256]), bpall, allb(b_MmT, b))
            yield
            for h in range(4):
                p6, bp6 = bank(h)
                fw.op("pe", lambda e, p6=p6, h=h, a=a, b=b: e.matmul(p6[:, 0:128], lhsT=MmT[h][b][:], rhs=Pm[h][a][:], start=True, stop=True), [b_MmT[h][b], b_Pm[h][a]], [bp6])
            fw.op("dve", lambda e, a=a, b=b: e.tensor_tensor(out=Pma[b][:], in0=G4[:, :, 0:128], in1=Pma[a][:], op=ALU.add), bpall + allb(b_Pm, a), allb(b_Pm, b))
        PF = (NEU_LEVELS % 2) if NEU_PINGPONG else NEU_LEVELS
        if GDN_STOP <= 5:
            return
        yield
        for h in range(4):
            p7, bp7 = bank(h)
            fw.op("pe", lambda e, p7=p7, h=h: e.matmul(p7[:, 0:128], lhsT=Pm[h][PF][:], rhs=k6[h][:, 5, :], start=True, stop=True), [b_Pm[h][PF], b_k6[h]], [bp7])
            fw.op("pe", lambda e, p7=p7, h=h: e.matmul(p7[:, 128:256], lhsT=k6[h][:, 1, :], rhs=Pm[h][PF][:], start=True, stop=True), [b_Pm[h][PF], b_k6[h]], [bp7])
        fw.op("dve", lambda e, G_=G_: e.tensor_tensor(out=Uba[:], in0=G4[:, :, 0:128], in1=bc(G_[:, 4:8]), op=ALU.mult), bpall + [b_gb[tl]], b_Ub)
        fw.op("dve", lambda e: e.tensor_copy(out=WTa[:], in_=G4[:, :, 128:256]), bpall, b_WT)
        yield
        while t > 0 and (t - 1) not in s8_done:
            yield
        for h in range(4):
            p8, bp8 = bank(h)
            fw.op("pe", lambda e, p8=p8, h=h: e.matmul(p8[:, 0:128], lhsT=WT[h][:], rhs=Sb[:, h, :], start=True, stop=True), [b_WT[h], b_Sb[h]], [bp8])
            fw.op("dve", lambda e, p8=p8, h=h, G_=G_: e.scalar_tensor_tensor(out=vn[h][:], in0=p8[:, 0:128], scalar=G_[:, 8 + h:9 + h], in1=Ub[h][:], op0=ALU.mult, op1=ALU.add),
                  [bp8, b_gb[tl], b_Ub[h]], [b_vn[h]])
        yield
        B9 = [bank(h) for h in range(4)]
        for h in range(4):
            p9, bp9 = B9[h]
            fw.op("pe", lambda e, p9=p9, h=h: e.matmul(p9[:, 0:128], lhsT=kqT[h][:, 2, :], rhs=Sb[:, h, :], start=True, stop=False), [b_kqT[h], b_Sb[h]], [bp9])
            fw.op("pe", lambda e, p9=p9, h=h: e.matmul(p9[:, 0:128], lhsT=aT[h][:], rhs=vn[h][:], start=False, stop=True), [b_aT[h], b_vn[h]], [bp9])
            fw.op("pe", lambda e, p9=p9, h=h: e.matmul(p9[:, 128:256], lhsT=k6[h][:, 2, :], rhs=vn[h][:], start=True, stop=True), [b_k6[h], b_vn[h]], [bp9])
            fw.op("act", lambda e, p9=p9, h=h: e.activation(out=junk[:, 0:128], in_=p9[:, 0:128], func=AF.Square, accum_out=so[:, h:h + 1]), [bp9], [b_so])
            fw.op("dve", lambda e, p9=p9, h=h: e.scalar_tensor_tensor(out=Sf[:, h, :], in0=Sf[:, h, :], scalar=Gs[:, 16 + h:17 + h], in1=p9[:, 128:256], op0=ALU.mult, op1=ALU.add),
                  [bp9, b_Gs, b_Sf[h]], [b_Sf[h]])
        fw.op("pool", lambda e: e.tensor_copy(out=Sb[:], in_=Sf[:]), b_Sf, b_Sb)
        fw.op("act", lambda e: e.activation(out=so[:, 4:8], in_=so[:, 0:4], func=AF.Ln, scale=1.0 / 128, bias=EPS), [b_so], [b_so])
        fw.op("act", lambda e: e.activation(out=so[:, 8:12], in_=so[:, 4:8], func=AF.Exp, scale=-0.5), [b_so], [b_so])
        for h in range(4):
            p9, bp9 = B9[h]
            fw.op("dve", lambda e, p9=p9, h=h, t=t, tl=tl: e.scalar_tensor_tensor(out=MIX[:, t, 512 + h * 128:512 + (h + 1) * 128], in0=p9[:, 0:128], scalar=so[:, 8 + h:9 + h],
                                                                               in1=zsg[:, tl, h * 128:(h + 1) * 128], op0=ALU.mult, op1=ALU.mult),
                  [bp9, b_so, b_zsg[tl]], [b_mix[t]])
        s8_done.add(t)


    for sti in range(4):
        if sti * 4 > GDN_MAXTILE:
            break
        def m2_proj_tile(tl, sti=sti):
            t = sti * 4 + tl
            r = t % 2
            fw.dma(xt[r][:], x[s, t * 128:(t + 1) * 128, :], writes=[b_xt[r]])
            fw.op("act", lambda e, r=r: e.activation(out=xn[r][:], in_=xt[r][:], func=AF.Square, accum_out=st8[r][:, 0:1]), [b_xt[r]], [b_st8[r], b_xn[r]])
            yield
            rstd_ops(None, st8[r][:, 0:1], st8[r][:, 1:2], 1.0 / DM, [b_st8[r]], [b_st8[r]], st8[r][:, 2:3], b_st8[r])
            fw.op("dve", lambda e, r=r: e.tensor_scalar(out=xn[r][:], in0=xt[r][:], scalar1=st8[r][:, 1:2], scalar2=None, op0=ALU.mult),
                  [b_xt[r], b_st8[r]], [b_xn[r]])
            yield
            pt_, bpt_ = ps.rot()
            ptb = pt_[:].bitcast(BF16).rearrange("p (k c) -> p k c", k=8)
            for k in range(8):
                fw.op("pe", lambda e, k=k, r=r, ptb=ptb: e.transpose(out=ptb[:, k, :], in_=xn[r][:, k * 128:(k + 1) * 128], identity=idb[:]),
                      [b_xn[r], b_idb], [bpt_])
            fw.op("dve", lambda e, tl=tl, ptb=ptb: e.tensor_copy(out=xnT[:, :, tl * 128:(tl + 1) * 128], in_=ptb), [bpt_], [b_xnT[tl]])
            yield
            need_weights()
            pa, bpa = ps.rot()
            for k in range(8):
                fw.op("pe", lambda e, k=k, tl=tl, pa=pa: e.matmul(pa[:, 0:8], lhsT=xnT[:, k, tl * 128:(tl + 1) * 128], rhs=w2[:, k, O_AB:O_AB + 8],
                                                                 start=(k == 0), stop=(k == 7)), [b_xnT[tl], b_w2[k]], [bpa])
            fw.op("act", lambda e, tl=tl, pa=pa: e.copy(out=ab[:, tl, :], in_=pa[:, 0:8]), [bpa], [b_ab[tl]])
            yield
            pz, bpz = ps.rot()
            for k in range(8):
                fw.op("pe", lambda e, k=k, tl=tl, pz=pz: e.matmul(pz[:, 0:512], lhsT=xnT[:, k, tl * 128:(tl + 1) * 128], rhs=w2[:, k, O_Z:O_Z + 512],
                                                                 start=(k == 0), stop=(k == 7)), [b_xnT[tl], b_w2[k]], [bpz])
            fw.op("act", lambda e, tl=tl, pz=pz: e.activation(out=zsg[:, tl, :], in_=pz[:, 0:512], func=AF.Silu), [bpz], [b_zsg[tl]])
            z3 = zsg[:, tl, :].rearrange("p (h d) -> p h d", h=4)
            fw.op("pool", lambda e, z3=z3: e.tensor_tensor(out=z3, in0=z3, in1=bct[:, B_GN:B_GN + 128].unsqueeze(1).to_broadcast([128, 4, 128]), op=ALU.mult),
                  [b_zsg[tl], b_bc], [b_zsg[tl]])
            yield
            G_ = gb[:, tl, :]
            fw.op("dve", lambda e, tl=tl, G_=G_: e.tensor_tensor(out=G_[:, 12:16], in0=ab[:, tl, 0:4], in1=bct[:, B_DT:B_DT + 4], op=ALU.add), [b_ab[tl], b_bc], [b_gb[tl]])
            fw.op("act", lambda e, G_=G_: e.activation(out=G_[:, 12:16], in_=G_[:, 12:16], func=AF.Exp), [b_gb[tl]], [b_gb[tl]])
            fw.op("act", lambda e, G_=G_: e.activation(out=G_[:, 12:16], in_=G_[:, 12:16], func=AF.Ln, bias=1.0), [b_gb[tl]], [b_gb[tl]])
            fw.op("act", lambda e, G_=G_: e.activation(out=G_[:, 8:12], in_=bct[:, B_AL:B_AL + 4], func=AF.Exp), [b_bc], [b_gb[tl]])
            fw.op("dve", lambda e, G_=G_: e.scalar_tensor_tensor(out=G_[:, 0:4], in0=G_[:, 12:16], scalar=-1.0, in1=G_[:, 8:12], op0=ALU.mult, op1=ALU.mult),
                  [b_gb[tl]], [b_gb[tl]])
            fw.op("act", lambda e, tl=tl, G_=G_: e.activation(out=G_[:, 12:16], in_=ab[:, tl, 4:8], func=AF.Exp, scale=-1.0), [b_ab[tl]], [b_gb[tl]])
            fw.op("dve", lambda e, G_=G_: e.tensor_scalar(out=G_[:, 12:16], in0=G_[:, 12:16], scalar1=1.0, scalar2=None, op0=ALU.add), [b_gb[tl]], [b_gb[tl]])
            fw.op("dve", lambda e, G_=G_: e.reciprocal(out=G_[:, 4:8], in_=G_[:, 12:16]), [b_gb[tl]], [b_gb[tl]])
            fw.op("dve", lambda e, G_=G_: e.tensor_scalar(out=G_[:, 8:12], in0=G_[:, 4:8], scalar1=-1.0, scalar2=None, op0=ALU.mult), [b_gb[tl]], [b_gb[tl]])

            yield

        run_rolling([m2_proj_tile(tl) for tl in range(4)], 2, bg=wl[0])
        need_weights()
        for c in range(12):
            pf, bpf = ps.rot()
            for k in range(8):
                fw.op("pe", lambda e, k=k, c=c, pf=pf: e.matmul(pf[:, 0:512], lhsT=w2[:, k, O_GQ + c * 128:O_GQ + (c + 1) * 128], rhs=xnT[:, k, :],
                                                               start=(k == 0), stop=(k == 7)), [b_w2[k]] + b_xnT, [bpf])
            xi = c % 2
            fw.op("pool", lambda e, xi=xi, c=c: e.tensor_copy(out=Xc[xi][:, 0:3], in_=halo[:, c, 0:3]), [b_halo[c]], [b_Xc[xi]])
            fw.op("act", lambda e, xi=xi, pf=pf: e.copy(out=Xc[xi][:, 3:515], in_=pf[:, 0:512]), [bpf], [b_Xc[xi]])
            fw.op("pool", lambda e, xi=xi, c=c: e.tensor_copy(out=halo[:, c, 0:3], in_=Xc[xi][:, 512:515]), [b_Xc[xi]], [b_halo[c]])
            cw = lambda i, c=c: ppt[:, P_CW + c * 4 + i:P_CW + c * 4 + i + 1]
            fw.op("act", lambda e, xi=xi, cw=cw: e.activation(out=yacc[xi][:], in_=Xc[xi][:, 0:512], func=AF.Copy, scale=cw(0)), [b_Xc[xi], b_pp], [b_yacc[xi]])
            for i in (1, 2, 3):
                fw.op("dve", lambda e, xi=xi, cw=cw, i=i: e.scalar_tensor_tensor(out=yacc[xi][:], in0=Xc[xi][:, i:i + 512], scalar=cw(i), in1=yacc[xi][:],
                                                                                op0=ALU.mult, op1=ALU.add), [b_Xc[xi], b_pp, b_yacc[xi]], [b_yacc[xi]])
            if c > 0:
                fw.op("act", lambda e, xj=(c - 1) % 2, cj=c - 1: e.activation(out=Y[:, cj, :], in_=yacc[xj][:], func=AF.Silu), [b_yacc[(c - 1) % 2]], [b_Y[c - 1]])
        fw.op("act", lambda e: e.activation(out=Y[:, 11, :], in_=yacc[11 % 2][:], func=AF.Silu), [b_yacc[11 % 2]], [b_Y[11]])


        gens = [gdn_tile(sti, tl) for tl in range(4)]
        active = []
        nxt = 0
        while nxt < len(gens) or active:
            while len(active) < 2 and nxt < len(gens):
                active.append(gens[nxt]); nxt += 1
            for g in list(active):
                try:
                    next(g)
                except StopIteration:
                    active.remove(g)
                    break


def build_wout(nc, fw, ps, st, sb, s, x, w_out, idb, b_idb, MIX, b_mix, H, b_h):
    wo = sb(st, "wo", [128, 8, DM], BF16); b_wo = fw.bufs_n("wo", 8)
    stg = [(sb(st, f"wstg{i}", [128, 1024], F32), fw.buf(f"wstg{i}")) for i in range(2)]
    wl = [load_cast_gen(fw, st, sb, "wo", w_out, 8, DM, wo, b_wo, None, None, stg)]

    def need_weights():
        drain(wl[0])
        wl[0] = None
    mT = [sb(st, f"mT{i}", [128, 8, 128], BF16) for i in range(2)]; b_mT = fw.bufs_n("mT", 2)
    def wout_tile(t):
        r = t % 2
        fw.dma(H[:, t, :], x[s, t * 128:(t + 1) * 128, :], writes=[b_h[t]])
        pt_, bpt_ = ps.rot()
        ptb = pt_[:].bitcast(BF16).rearrange("p (k c) -> p k c", k=8)
        for k in range(8):
            fw.op("pe", lambda e, k=k, t=t, ptb=ptb: e.transpose(out=ptb[:, k, :], in_=MIX[:, t, k * 128:(k + 1) * 128], identity=idb[:]),
                  [b_mix[t], b_idb], [bpt_])
        fw.op("act", lambda e, r=r, ptb=ptb: e.copy(out=mT[r][:], in_=ptb), [bpt_], [b_mT[r]])
        yield
        need_weights()
        for half in range(2):
            po, bpo = ps.rot()
            for k in range(8):
                fw.op("pe", lambda e, k=k, r=r, po=po, half=half: e.matmul(po[:, 0:512], lhsT=mT[r][:, k, :], rhs=wo[:, k, half * 512:(half + 1) * 512],
                                                                          start=(k == 0), stop=(k == 7)), [b_mT[r], b_wo[k]], [bpo])
            fw.op("dve", lambda e, t=t, po=po, half=half: e.tensor_tensor(out=H[:, t, half * 512:(half + 1) * 512], in0=po[:, 0:512],
                                                                          in1=H[:, t, half * 512:(half + 1) * 512], op=ALU.add),
                  [bpo, b_h[t]], [b_h[t]])
            yield

    run_rolling([wout_tile(t) for t in range(NT)], 2, bg=wl[0])
    need_weights()


def build_mlp(nc, fw, ps, st, sb, s, w_up, w_down, ppt, b_pp, idb, b_idb, HNT, H, b_h, out, rstd_ops):
    NB = 4
    FB = DFF // NB
    junk = sb(st, "junk2", [128, DM], F32); b_junk = fw.buf("junk2")
    hn = [sb(st, f"hn{i}", [128, DM], BF16) for i in range(2)]; b_hn = fw.bufs_n("hn", 2)
    st4 = [sb(st, f"st4{i}", [128, 4], F32) for i in range(2)]; b_st4 = fw.bufs_n("st4", 2)
    b_hnT = fw.bufs_n("hnT", NT)
    wu = [sb(st, f"wu{i}", [128, 8, FB], BF16) for i in range(2)]; b_wu = [fw.bufs_n(f"wu{i}", 8) for i in range(2)]
    wd = [sb(st, f"wd{i}", [128, 8, DM], BF16) for i in range(2)]; b_wd = [fw.bufs_n(f"wd{i}", 8) for i in range(2)]
    stg = [(sb(st, f"mstg{i}", [128, 1024], F32), fw.buf(f"mstg{i}")) for i in range(3)]
    aT = sb(st, "aT", [128, 8, 512], BF16); b_aT = fw.bufs_n("aT", 8)
    rl = [sb(st, f"rl{i}", [128, 512], BF16) for i in range(2)]; b_rl = fw.bufs_n("rl", 2)

    def load_block(nb):
        r = nb % 2
        load_cast(fw, st, sb, "wu", w_up[:, nb * FB:(nb + 1) * FB], 8, FB, wu[r], b_wu[r], ppt[:, P_MN:P_MN + 8], b_pp, stg)
        load_cast(fw, st, sb, "wd", w_down[nb * FB:(nb + 1) * FB, :], 8, DM, wd[r], b_wd[r], None, None, stg)

    def load_block_gen(nb):
        r = nb % 2
        gain = ppt[:, P_MN:P_MN + 8]
        for k in range(8):
            stt, b_st = stg[k % len(stg)]
            fw.dma(stt[:, 0:FB], w_up[k * 128:(k + 1) * 128, nb * FB:(nb + 1) * FB], writes=[b_st])
            fw.op("pool", lambda e, k=k, stt=stt, r=r: e.tensor_scalar(out=wu[r][:, k, :], in0=stt[:, 0:FB], scalar1=gain[:, k:k + 1], scalar2=1.0,
                                                                      op0=ALU.mult, op1=ALU.mult), [b_st, b_pp], [b_wu[r][k]])
            yield
        for k in range(8):
            stt, b_st = stg[k % len(stg)]
            fw.dma(stt[:, 0:DM], w_down[nb * FB + k * 128:nb * FB + (k + 1) * 128, :], writes=[b_st])
            fw.op("pool", lambda e, k=k, stt=stt, r=r: e.tensor_copy(out=wd[r][:, k, :], in_=stt[:, 0:DM]), [b_st], [b_wd[r][k]])
            yield

    def step_loader(g):
        try:
            next(g)
            return g
        except StopIteration:
            return None

    def block0_loader():
        yield from load_cast_gen(fw, st, sb, "wu", w_up[:, 0:FB], 8, FB, wu[0], b_wu[0], ppt[:, P_MN:P_MN + 8], b_pp, stg)
        yield from load_cast_gen(fw, st, sb, "wd", w_down[0:FB, :], 8, DM, wd[0], b_wd[0], None, None, stg)
    wloader = block0_loader()
    def hn_tile(t):
        r = t % 2
        fw.op("act", lambda e, t=t, r=r: e.activation(out=hn[r][:], in_=H[:, t, :], func=AF.Square, accum_out=st4[r][:, 0:1]), [b_h[t]], [b_st4[r], b_hn[r]])
        yield
        rstd_ops(None, st4[r][:, 0:1], st4[r][:, 1:2], 1.0 / DM, [b_st4[r]], [b_st4[r]], st4[r][:, 2:3], b_st4[r])
        yield
        fw.op("dve", lambda e, t=t, r=r: e.tensor_scalar(out=hn[r][:], in0=H[:, t, :], scalar1=st4[r][:, 1:2], scalar2=None, op0=ALU.mult),
              [b_h[t], b_st4[r]], [b_hn[r]])
        yield
        pt_, bpt_ = ps.rot()
        ptb = pt_[:].bitcast(BF16).rearrange("p (k c) -> p k c", k=8)
        for k in range(8):
            fw.op("pe", lambda e, k=k, r=r, ptb=ptb: e.transpose(out=ptb[:, k, :], in_=hn[r][:, k * 128:(k + 1) * 128], identity=idb[:]), [b_hn[r], b_idb], [bpt_])
        fw.op("act", lambda e, t=t, ptb=ptb: e.copy(out=HNT[:, :, t * 128:(t + 1) * 128], in_=ptb), [bpt_], [b_hnT[t]])
        hn_done.add(t)
        yield

    def rolling_gen(gens, width):
        gens = list(gens)
        active = []
        nxt = 0
        while nxt < len(gens) or active:
            while len(active) < width and nxt < len(gens):
                active.append(gens[nxt]); nxt += 1
            for g_ in list(active):
                try:
                    next(g_)
                except StopIteration:
                    active.remove(g_)
                    break
            yield

    hn_done = set()
    wloader = run_rolling([hn_tile(t) for t in range(4)], 2, bg=wloader)
    drain(wloader)
    hn_bg = rolling_gen([hn_tile(t) for t in range(4, NT)], 2)

    def step_hn():
        nonlocal hn_bg
        if hn_bg is not None:
            try:
                next(hn_bg)
            except StopIteration:
                hn_bg = None
    rc = 0
    for nb in range(NB):
        r = nb % 2
        loader = load_block_gen(nb + 1) if nb + 1 < NB else None
        for g in range(4):
            while hn_bg is not None and not all(t_ in hn_done for t_ in range(g * 4, g * 4 + 4)):
                step_hn()
            for c in range(8):
                pu, bpu = ps.rot()
                for k in range(8):
                    fw.op("pe", lambda e, k=k, c=c, r=r, g=g, pu=pu: e.matmul(pu[:, 0:512], lhsT=wu[r][:, k, c * 128:(c + 1) * 128],
                                                                             rhs=HNT[:, k, g * 512:(g + 1) * 512], start=(k == 0), stop=(k == 7)),
                          [b_wu[r][k]] + b_hnT[g * 4:(g + 1) * 4], [bpu])
                ri = rc % 2
                rc += 1
                fw.op("act", lambda e, ri=ri, pu=pu: e.activation(out=rl[ri][:], in_=pu[:, 0:512], func=AF.Relu), [bpu], [b_rl[ri]])
                fw.op("act", lambda e, ri=ri, c=c: e.activation(out=aT[:, c, :], in_=rl[ri][:], func=AF.Square), [b_rl[ri]], [b_aT[c]])
                if loader is not None and c % 2 == 1:
                    loader = step_loader(loader)
                if nb == 0:
                    step_hn()
            for tl in range(4):
                t = g * 4 + tl
                for half in range(2):
                    po, bpo = ps.acc(tl % 2 * 2 + half)
                    for c in range(8):
                        fw.op("pe", lambda e, c=c, tl=tl, r=r, half=half, po=po: e.matmul(po[:, 0:512], lhsT=aT[:, c, tl * 128:(tl + 1) * 128],
                                                                                         rhs=wd[r][:, c, half * 512:(half + 1) * 512],
                                                                                         start=(c == 0), stop=(c == 7)), [b_aT[c], b_wd[r][c]], [bpo])
                    fw.op("dve", lambda e, t=t, po=po, half=half: e.tensor_tensor(out=H[:, t, half * 512:(half + 1) * 512], in0=po[:, 0:512],
                                                                                  in1=H[:, t, half * 512:(half + 1) * 512], op=ALU.add),
                          [bpo, b_h[t]], [b_h[t]])
                if nb == NB - 1:
                    fw.dma(out[s, t * 128:(t + 1) * 128, :], H[:, t, :], reads=[b_h[t]])
        while loader is not None:
            loader = step_loader(loader)


def host_prepare(inputs):
    f = lambda a: np.ascontiguousarray(np.asarray(a), dtype=np.float32)
    w_in = f(inputs["w_in"])[0]
    o = np.cumsum([0, 256, 256, 64, 512, 512, 512, 512, 4, 4])
    perm = np.concatenate([np.arange(o[0], o[3]), np.arange(o[7], o[9]), np.arange(o[6], o[7]), np.arange(o[3], o[6])])
    w_in_p = np.ascontiguousarray(w_in[:, perm])
    pp = np.zeros((128, NPP), np.float32)
    pp[:, P_AN:P_AN + 8] = f(inputs["attn_norm_w"])[0].reshape(8, 128).T
    pp[:, P_QLN:P_QLN + 2] = f(inputs["q_lat_norm_w"])[0].reshape(2, 128).T
    pp[:, P_KVLN:P_KVLN + 2] = f(inputs["kv_lat_norm_w"])[0].reshape(2, 128).T
    pp[:, P_MN:P_MN + 8] = f(inputs["mlp_norm_w"])[0].reshape(8, 128).T
    cw = f(inputs["conv_w"])[0]
    pp[:, P_CW:P_CW + 48] = cw.reshape(4, 12, 128).transpose(2, 1, 0).reshape(128, 48)
    bc = np.zeros((128, NBC), np.float32)
    bc[:, B_QN:B_QN + 192] = f(inputs["q_norm_w"])[0][None, :]
    bc[:, B_KN:B_KN + 192] = f(inputs["k_norm_w"])[0][None, :]
    bc[:, B_MO:B_MO + 512] = f(inputs["mla_out_norm_w"])[0].reshape(-1)[None, :]
    bc[:, B_GN:B_GN + 128] = f(inputs["gdn_norm_w"])[0][None, :]
    bc[:, B_AL:B_AL + 4] = f(inputs["a_log"])[0][None, :]
    bc[:, B_DT:B_DT + 4] = f(inputs["dt_bias"])[0][None, :]
    cst = np.zeros((128, NK), np.float32)
    i = np.arange(128)
    cst[:, K_ID:K_ID + 128] = np.eye(128)
    cst[:, K_U:K_U + 128] = (i[:, None] <= i[None, :])
    cst[:, K_MNEG:K_MNEG + 128] = np.where(i[None, :] >= i[:, None], 0.0, -30000.0)
    cst[:, K_STR:K_STR + 128] = (i[None, :] > i[:, None])
    cst[:, K_INC:K_INC + 128] = (i[None, :] >= i[:, None])
    cst[:, K_ONE:K_ONE + 128] = 1.0
    half = 32
    inv_freq = (10000.0 ** (-(np.arange(half, dtype=np.float32) / np.float32(half)))).astype(np.float32)
    cst[:, K_IF:K_IF + 32] = inv_freq[None, :]
    x = f(inputs["x"])
    pos = np.asarray(inputs["positions"]).astype(np.int32)
    shared = {
        "w_in": w_in_p, "w_uq": f(inputs["w_uq"])[0], "w_ukv": f(inputs["w_ukv"])[0], "w_out": f(inputs["w_out"])[0],
        "w_up": f(inputs["w_up"])[0], "w_down": f(inputs["w_down"])[0], "pp": pp, "bc": bc, "cst": cst,
    }
    in_maps = []
    for c in range(NCORES):
        m = dict(shared)
        m["x"] = np.ascontiguousarray(x[2 * c:2 * c + 2])
        m["pos"] = np.ascontiguousarray(pos[2 * c:2 * c + 2].reshape(2, NT, 128).transpose(0, 2, 1))
        in_maps.append(m)
    return in_maps


_NC_CACHE = {}


def kernel(**inputs):
    in_maps = host_prepare(inputs)
    if "nc" not in _NC_CACHE:
        _NC_CACHE["nc"] = build_program()
    nc = _NC_CACHE["nc"]
    res = run_bass_kernel_spmd(nc, in_maps, core_ids=list(range(NCORES)))
    outs = [res.results[c]["out"] for c in range(NCORES)]
    return np.concatenate(outs, axis=0).astype(np.float32)
```
